# Optimizing a Trainium2 kernel written in Bass

```python
import math
import jax, jax.numpy as jnp
from jax import lax
import numpy as np

D_MODEL = 1024
BATCH = 32
SEQ = 2048
DEPTH = 1
DEC_BATCH = 16
DEC_SEQ = 32
PAST_LEN = 2048

CHUNK = 64
A_HEADS = 4
A_HEAD_DIM = 256
A_WIDTH = A_HEADS * A_HEAD_DIM
B_HEADS = 8
B_HEAD_DIM = 128
B_WIDTH = B_HEADS * B_HEAD_DIM
IDX_HEADS = 8
IDX_DIM = 64
TOPK_MAX = 256
Q_BLOCK = 16
N_BUCKETS = 32
MAX_DISTANCE = 128
LN_EPS = 1e-5
HEAD_NORM_EPS = 1e-6
DN_ALPHA = (2 * DEPTH) ** 0.25
DN_BETA = (8 * DEPTH) ** -0.25

COL_SIZES = (A_WIDTH, A_WIDTH, A_WIDTH, A_WIDTH, A_WIDTH, A_HEADS, A_HEADS,
             B_WIDTH, B_WIDTH, B_WIDTH, B_WIDTH, IDX_HEADS * IDX_DIM, IDX_DIM, IDX_HEADS,
             D_MODEL, D_MODEL)
IN_COLS = sum(COL_SIZES)

kernel_name = 'hybrid_mlstm_dsa_stream_step'


def layer_norm(x, g, b):
    xf = x.astype(jnp.float32)
    mu = xf.mean(-1, keepdims=True)
    var = jnp.square(xf - mu).mean(-1, keepdims=True)
    return ((xf - mu) * lax.rsqrt(var + LN_EPS) * g.astype(jnp.float32) + b.astype(jnp.float32)).astype(x.dtype)


def t5_bucket(rel):
    half = N_BUCKETS // 2
    max_exact = half // 2
    ret = jnp.where(rel > 0, half, 0)
    n = jnp.abs(rel)
    nf = jnp.maximum(n, 1).astype(jnp.float32)
    large = max_exact + (jnp.log(nf / max_exact) / math.log(MAX_DISTANCE / max_exact)
                         * (half - max_exact)).astype(jnp.int32)
    large = jnp.minimum(large, half - 1)
    return ret + jnp.where(n < max_exact, n, large)


def project(x, w_in, b_gates):
    B, T, _ = x.shape
    p = x @ w_in
    cuts = [int(c) for c in np.cumsum(COL_SIZES)[:-1]]
    (qa, ka, va, oa, za, ia, fa, qb, kb, vb, zb, qi, ki, wi, ga, gb) = jnp.split(p, cuts, axis=-1)
    ig = ia.astype(jnp.float32) + b_gates[:A_HEADS].astype(jnp.float32)
    lf = jax.nn.log_sigmoid(fa.astype(jnp.float32) + b_gates[A_HEADS:].astype(jnp.float32))
    mlstm_in = (qa.reshape(B, T, A_HEADS, A_HEAD_DIM),
                ka.reshape(B, T, A_HEADS, A_HEAD_DIM) * (A_HEAD_DIM ** -0.5),
                va.reshape(B, T, A_HEADS, A_HEAD_DIM), ig, lf)
    attn_in = (qb.reshape(B, T, B_HEADS, B_HEAD_DIM), kb.reshape(B, T, B_HEADS, B_HEAD_DIM),
               vb.reshape(B, T, B_HEADS, B_HEAD_DIM), qi.reshape(B, T, IDX_HEADS, IDX_DIM), ki, wi)
    gates = (oa, za, zb, ga, gb)
    return mlstm_in, attn_in, gates


def mlstm_chunk(carry, inp):
    C, n, m = carry
    q, k, v, ig, lf = inp
    q = q.astype(jnp.float32)
    k = k.astype(jnp.float32)
    v = v.astype(jnp.float32)
    L = q.shape[1]
    b = jnp.swapaxes(jnp.cumsum(lf, axis=1), 1, 2)
    igt = jnp.swapaxes(ig, 1, 2)
    causal = jnp.tril(jnp.ones((L, L), dtype=bool))
    dmat = jnp.where(causal, b[..., :, None] - b[..., None, :] + igt[..., None, :], -jnp.inf)
    a_inter = b + m[..., None]
    m_j = jnp.maximum(a_inter, dmat.max(-1))
    wmat = jnp.exp(dmat - m_j[..., None])
    inter = jnp.exp(a_inter - m_j)
    s = jnp.einsum('blhd,bshd->bhls', q, k) * wmat
    num = (jnp.einsum('bhls,bshd->blhd', s, v)
           + jnp.einsum('blhk,bhkv->blhv', q, C) * jnp.swapaxes(inter, 1, 2)[..., None])
    den = s.sum(-1) + inter * jnp.einsum('blhk,bhk->bhl', q, n)
    h = num / jnp.swapaxes(jnp.maximum(jnp.abs(den), jnp.exp(-m_j)), 1, 2)[..., None]
    b_last = b[..., -1]
    g = b_last[..., None] - b + igt
    m_new = jnp.maximum(b_last + m, g.max(-1))
    decay = jnp.exp(b_last + m - m_new)
    w_s = jnp.exp(g - m_new[..., None])
    C_new = decay[..., None, None] * C + jnp.einsum('bhs,bshk,bshv->bhkv', w_s, k, v)
    n_new = decay[..., None] * n + jnp.einsum('bhs,bshk->bhk', w_s, k)
    return (C_new, n_new, m_new), h


def mlstm_prompt(q, k, v, ig, lf):
    B, T, H, D = q.shape
    nc = T // CHUNK
    to_chunks = lambda a: jnp.swapaxes(a.reshape((B, nc, CHUNK) + a.shape[2:]), 0, 1)
    init = (jnp.zeros((B, H, D, D), jnp.float32), jnp.zeros((B, H, D), jnp.float32),
            jnp.zeros((B, H), jnp.float32))
    state, h = lax.scan(mlstm_chunk, init, tuple(to_chunks(a) for a in (q, k, v, ig, lf)))
    return jnp.swapaxes(h, 0, 1).reshape(B, T, H, D), state


def dsa_block(q, qi, wi, qpos, k, v, ki, rel_bias, k_sel):
    S = k.shape[1]
    sc = jnp.einsum('bqhd,bsd->bqhs', qi, ki).astype(jnp.float32) * (IDX_DIM ** -0.5)
    score = jnp.einsum('bqh,bqhs->bqs', wi.astype(jnp.float32) * (IDX_HEADS ** -0.5), jax.nn.relu(sc))
    limit = (qpos // CHUNK + 1) * CHUNK
    adm = jnp.arange(S, dtype=jnp.int32)[None, :] < limit[:, None]
    score = jnp.where(adm[None], score, -jnp.inf)
    _, idx = lax.top_k(score, k_sel)
    gather = jax.vmap(lambda rows, ii: rows[ii])
    kg = gather(k, idx)
    vg = gather(v, idx)
    logits = jnp.einsum('bqhd,bqkhd->bhqk', q, kg).astype(jnp.float32) * (B_HEAD_DIM ** -0.5)
    bias = rel_bias[t5_bucket(idx - qpos[None, :, None])]
    logits = logits + jnp.moveaxis(bias, -1, 1).astype(jnp.float32)
    ok = idx < limit[None, :, None]
    logits = jnp.where(ok[:, None], logits, -jnp.inf)
    p = jax.nn.softmax(logits, axis=-1)
    return jnp.einsum('bhqk,bqkhd->bqhd', p.astype(v.dtype), vg)


def dsa_prompt(q, k, v, qi, ki, wi, rel_bias):
    B, T = q.shape[:2]
    k_sel = min(TOPK_MAX, T // 4)
    nb = T // Q_BLOCK
    to_blocks = lambda a: jnp.swapaxes(a.reshape((B, nb, Q_BLOCK) + a.shape[2:]), 0, 1)
    pos_blocks = jnp.arange(T, dtype=jnp.int32).reshape(nb, Q_BLOCK)
    out = lax.map(lambda a: dsa_block(a[0], a[1], a[2], a[3], k, v, ki, rel_bias, k_sel),
                  (to_blocks(q), to_blocks(qi), to_blocks(wi), pos_blocks))
    return jnp.swapaxes(out, 0, 1).reshape(B, T, B_HEADS, B_HEAD_DIM)


def dsa_sample(q, k_new, v_new, qi, ki_new, wi, cache_k, cache_v, cache_idx_k, rel_bias):
    past = cache_k.shape[1]
    T = q.shape[1]
    k = jnp.concatenate([cache_k, k_new], axis=1)
    v = jnp.concatenate([cache_v, v_new], axis=1)
    ki = jnp.concatenate([cache_idx_k, ki_new], axis=1)
    k_sel = min(TOPK_MAX, (past + T) // 4)
    qpos = past + jnp.arange(T, dtype=jnp.int32)
    return dsa_block(q, qi, wi, qpos, k, v, ki, rel_bias, k_sel)


def merge(x, h_a, o_b, gates, a_norm_g, w_out_a, w_out_b, w_o, ln_g, ln_b):
    oa, za, zb, ga, gb = gates
    B, T, _ = x.shape
    mu = h_a.mean(-1, keepdims=True)
    var = jnp.square(h_a - mu).mean(-1, keepdims=True)
    hn = ((h_a - mu) * lax.rsqrt(var + HEAD_NORM_EPS)).reshape(B, T, A_WIDTH) * a_norm_g.astype(jnp.float32)
    branch_a = (hn * jax.nn.sigmoid(oa.astype(jnp.float32))).astype(x.dtype) * jax.nn.silu(za)
    branch_b = o_b.reshape(B, T, B_WIDTH) * jax.nn.silu(zb)
    mixed = jax.nn.sigmoid(ga) * (branch_a @ w_out_a) + jax.nn.sigmoid(gb) * (branch_b @ w_out_b)
    return layer_norm(DN_ALPHA * x + mixed @ w_o, ln_g, ln_b)


def setup_inputs(seed: int = 0) -> dict:
    key = jax.random.key(seed)
    ks = jax.random.split(key, 20)
    nrm = lambda i, shape: jax.random.normal(ks[i], shape, jnp.float32)
    return {
        'x_prompt': nrm(0, (BATCH, SEQ, D_MODEL)),
        'x_sample': nrm(1, (DEC_BATCH, DEC_SEQ, D_MODEL)),
        'cache_k': nrm(2, (DEC_BATCH, PAST_LEN, B_HEADS, B_HEAD_DIM)),
        'cache_v': nrm(3, (DEC_BATCH, PAST_LEN, B_HEADS, B_HEAD_DIM)),
        'cache_idx_k': nrm(4, (DEC_BATCH, PAST_LEN, IDX_DIM)),
        'state_C': nrm(5, (DEC_BATCH, A_HEADS, A_HEAD_DIM, A_HEAD_DIM)) * 0.1,
        'state_n': nrm(6, (DEC_BATCH, A_HEADS, A_HEAD_DIM)) * 0.5,
        'state_m': nrm(7, (DEC_BATCH, A_HEADS)),
        'w_in': nrm(8, (D_MODEL, IN_COLS)) * (D_MODEL ** -0.5),
        'b_gates': jnp.concatenate([0.1 * nrm(9, (A_HEADS,)),
                                    jnp.linspace(3.0, 6.0, A_HEADS, dtype=jnp.float32) + 0.01 * nrm(10, (A_HEADS,))]),
        'a_norm_g': 1.0 + 0.01 * nrm(11, (A_WIDTH,)),
        'w_out_a': nrm(12, (A_WIDTH, D_MODEL)) * (A_WIDTH ** -0.5) * DN_BETA,
        'w_out_b': nrm(13, (B_WIDTH, D_MODEL)) * (B_WIDTH ** -0.5) * DN_BETA,
        'w_o': nrm(14, (D_MODEL, D_MODEL)) * (D_MODEL ** -0.5) * DN_BETA,
        'rel_bias': 0.2 * nrm(15, (N_BUCKETS, B_HEADS)),
        'ln_g': 1.0 + 0.01 * nrm(16, (D_MODEL,)),
        'ln_b': 0.01 * nrm(17, (D_MODEL,)),
    }


def reference(x_prompt, x_sample, cache_k, cache_v, cache_idx_k, state_C, state_n, state_m,
              w_in, b_gates, a_norm_g, w_out_a, w_out_b, w_o, rel_bias, ln_g, ln_b):
    for _ in range(DEPTH):
        (qa, ka, va, ig, lf), (qb, kb, vb, qi, ki, wi), gates_p = project(x_prompt, w_in, b_gates)
        h_a, (c_p, n_p, m_p) = mlstm_prompt(qa, ka, va, ig, lf)
        o_b = dsa_prompt(qb, kb, vb, qi, ki, wi, rel_bias)
        y_prompt = merge(x_prompt, h_a, o_b, gates_p, a_norm_g, w_out_a, w_out_b, w_o, ln_g, ln_b)
        (qa_s, ka_s, va_s, ig_s, lf_s), (qb_s, kb_s, vb_s, qi_s, ki_s, wi_s), gates_s = project(x_sample, w_in, b_gates)
        carry = (state_C.astype(jnp.float32), state_n.astype(jnp.float32), state_m.astype(jnp.float32))
        (c_s, n_s, m_s), h_a_s = mlstm_chunk(carry, (qa_s, ka_s, va_s, ig_s, lf_s))
        o_b_s = dsa_sample(qb_s, kb_s, vb_s, qi_s, ki_s, wi_s, cache_k, cache_v, cache_idx_k, rel_bias)
        y_sample = merge(x_sample, h_a_s, o_b_s, gates_s, a_norm_g, w_out_a, w_out_b, w_o, ln_g, ln_b)
    return (y_prompt, y_sample, kb, vb, ki, c_p, n_p, m_p, kb_s, vb_s, ki_s, c_s, n_s, m_s)
```

```python
import math
import numpy as np
import concourse.bass as bass
import concourse.mybir as mybir
from concourse.bass_utils import run_bass_kernel_spmd

F32 = mybir.dt.float32
BF16 = mybir.dt.bfloat16
AF = mybir.ActivationFunctionType
ALU = mybir.AluOpType
AX = mybir.AxisListType

D = 1024
NCORES = 8
DS = 32
NEG = -2.0e30
REPL = -1.0e30
LN_EPS = 1e-5
HN_EPS = 1e-6
ALPHA = 2.0 ** 0.25
NW = 2
NWH = 4
NBT = 2

_off = {}
_c = 0
for _n, _s in (("qa", 1024), ("ka", 1024), ("va", 1024), ("oa", 1024), ("za", 1024), ("ia", 4), ("fa", 4),
               ("qb", 1024), ("kb", 1024), ("vb", 1024), ("zb", 1024), ("qi", 512), ("ki", 64), ("wi", 8),
               ("ga", 1024), ("gb", 1024)):
    _off[_n] = _c
    _c += _s
IN_COLS = _c
GROUPS = []
for _k in ("kb", "vb"):
    GROUPS += [(_k, 0), (_k, 1)]
GROUPS += [("small", 0), ("qi", 0)]
for _k in ("qa", "ka", "va", "oa", "za", "woa", "ga", "qb", "zb", "wob", "gb", "wo"):
    GROUPS += [(_k, 0), (_k, 1)]
NG = len(GROUPS)


def _host_weight_groups(w_in, w_out_a, w_out_b, w_o):
    out = np.zeros((NG, 128, 8, 512), np.float32)
    for g, (kind, hf) in enumerate(GROUPS):
        if kind == "small":
            cols = np.zeros((1024, 512), np.float32)
            cols[:, 0:64] = w_in[:, _off["ki"]:_off["ki"] + 64]
            cols[:, 64:68] = w_in[:, _off["ia"]:_off["ia"] + 4]
            cols[:, 68:72] = w_in[:, _off["fa"]:_off["fa"] + 4]
            cols[:, 72:80] = w_in[:, _off["wi"]:_off["wi"] + 8]
        elif kind == "woa":
            cols = w_out_a[:, hf * 512:(hf + 1) * 512]
        elif kind == "wob":
            cols = w_out_b[:, hf * 512:(hf + 1) * 512]
        elif kind == "wo":
            cols = w_o[:, hf * 512:(hf + 1) * 512]
        elif kind == "qi":
            cols = w_in[:, _off["qi"]:_off["qi"] + 512]
        else:
            cols = w_in[:, _off[kind] + hf * 512:_off[kind] + (hf + 1) * 512]
        out[g] = cols.reshape(8, 128, 512).transpose(1, 0, 2)
    return out


def _bucket_onehot():
    import jax
    import jax.numpy as jnp
    n_buckets, max_distance = 32, 128
    cpu = jax.devices("cpu")[0]
    with jax.default_device(cpu):
        rel = jnp.arange(384, dtype=jnp.int32) - 256
        half = n_buckets // 2
        max_exact = half // 2
        ret = jnp.where(rel > 0, half, 0)
        n = jnp.abs(rel)
        nf = jnp.maximum(n, 1).astype(jnp.float32)
        large = max_exact + (jnp.log(nf / max_exact) / math.log(max_distance / max_exact)
                             * (half - max_exact)).astype(jnp.int32)
        large = jnp.minimum(large, half - 1)
        bk = np.asarray(ret + jnp.where(n < max_exact, n, large))
    oh = np.zeros((32, 384), np.float32)
    oh[bk, np.arange(384)] = 1.0
    return oh


class Res:
    __slots__ = ("name", "w", "r")

    def __init__(self, name):
        self.name = name
        self.w = None
        self.r = {}


EPOCH = 30000
NDMASEM = 8


class Sched:
    ENGS = ("pe", "act", "dve", "pool", "sp")

    def __init__(self, nc):
        self.nc = nc
        self.ops = {e: [] for e in self.ENGS}
        self.cnt = {e: 0 for e in self.ENGS}
        self.epoch = {e: 0 for e in self.ENGS}
        self.sems = {}
        self.seen = {e: {} for e in self.ENGS}
        self.dma_i = {e: 0 for e in self.ENGS}
        self.ninstr = 0
        self.final = {}

    def _sem(self, key):
        if key not in self.sems:
            self.sems[key] = self.nc.alloc_semaphore(name="s_%s_%s" % key)
        return self.sems[key]

    def _deps(self, eng, reads, writes):
        toks = []
        for r in reads:
            if r.w is not None:
                toks.append(r.w)
        for w in writes:
            if w.w is not None:
                toks.append(w.w)
            toks.extend(w.r.items())
        waits = {}
        for (key, val) in toks:
            if eng == "pe" and key[0] == "pe":
                continue
            if self.seen[eng].get(key, 0) >= val:
                continue
            if waits.get(key, 0) < val:
                waits[key] = val
        for key, val in waits.items():
            self.seen[eng][key] = val
        return [(self._sem(k), v) for k, v in waits.items()]

    def _commit(self, tok, reads, writes):
        for w in writes:
            w.w = tok
            w.r = {}
        for r in reads:
            if r not in writes:
                if r.r.get(tok[0], 0) < tok[1]:
                    r.r[tok[0]] = tok[1]

    def op(self, eng, fn, reads=(), writes=()):
        pbs = [r for r in reads if r.name.startswith("pb") and r not in writes]
        if pbs:
            writes = list(writes) + pbs
        waits = self._deps(eng, reads, writes)
        if self.cnt[eng] >= EPOCH:
            self.epoch[eng] += 1
            self.cnt[eng] = 0
        self.cnt[eng] += 1
        key = (eng, self.epoch[eng])
        tok = (key, self.cnt[eng])
        self.ops[eng].append((waits, fn, self._sem(key), 1))
        self.final[key] = tok[1]
        self._commit(tok, reads, writes)
        self.ninstr += 1 + len(waits)
        return tok

    def dma(self, eng, fn, reads=(), writes=()):
        i = self.dma_i[eng]
        self.dma_i[eng] += 1
        nsem = 3 if eng == "pool" else NDMASEM
        slot = i % nsem
        gen = i // nsem
        key = ("dma_" + eng, slot)
        waits = self._deps(eng, reads, writes)
        sem = self._sem(key)
        if gen > 0 and self.seen[eng].get(key, 0) < 16 * gen:
            waits.append((sem, 16 * gen))
            self.seen[eng][key] = 16 * gen
        tok = (key, 16 * (gen + 1))
        self.ops[eng].append((waits, fn, sem, 16))
        self.final[key] = tok[1]
        self._commit(tok, reads, writes)
        self.ninstr += 1 + len(waits)
        return tok

    def wait_all(self, eng, resources=()):
        waits = []
        for key, val in self.final.items():
            if self.seen[eng].get(key, 0) < val:
                waits.append((self._sem(key), val))
                self.seen[eng][key] = val
        self.ops[eng].append((waits, None, None, 0))

    def emit(self):
        nc = self.nc
        ops = self.ops
        with nc.Block() as block:
            def run(e, lst):
                for waits, fn, sem, inc in lst:
                    for s, v in waits:
                        e.wait_ge(s, v)
                    if fn is not None:
                        fn(e).then_inc(sem, inc)

            @block.tensor
            def _(e):
                run(e, ops["pe"])

            @block.scalar
            def _(e):
                run(e, ops["act"])

            @block.vector
            def _(e):
                run(e, ops["dve"])

            @block.gpsimd
            def _(e):
                run(e, ops["pool"])

            @block.sync
            def _(e):
                run(e, ops["sp"])


class _Stop(Exception):
    pass


def build(NPS, T, NSS, PAST, debug=False, stop=None):
    assert T % 256 == 0 and PAST % 128 == 0
    nc = bass.Bass("TRN2", target_bir_lowering=False)
    S = Sched(nc)
    KSEL_P = min(256, T // 4)
    KSEL_S = min(256, (PAST + DS) // 4)
    assert KSEL_P % 8 == 0 and KSEL_S % 8 == 0
    SK = max(T, PAST + 128)
    NKB = SK // 128

    def din(name, shape, dt=F32):
        return nc.dram_tensor(name, list(shape), dt, kind="ExternalInput")

    def dout(name, shape, dt=F32):
        return nc.dram_tensor(name, list(shape), dt, kind="ExternalOutput")

    xp = din("xp", [NPS, T, D]).ap()
    xs = din("xs", [NSS, DS, D]).ap()
    ck = din("ck", [NSS, PAST, D]).ap()
    cv = din("cv", [NSS, PAST, D]).ap()
    cik = din("cik", [NSS, PAST, 64]).ap()
    sC_t = din("sC", [NSS, 4, 256, 256])
    sn_t = din("sn", [NSS, 4, 256])
    sm_t = din("sm", [NSS, 4])
    wg = din("wg", [NG, 1024, 512]).ap()
    bg_t = din("bg", [8])
    ang_t = din("ang", [D])
    relb_t = din("relb", [32, 8])
    lng_t = din("lng", [D])
    lnb_t = din("lnb", [D])
    oh = din("oh", [32, 384]).ap()

    yp = dout("yp", [NPS, T, D]).ap()
    ysd = dout("ys", [NSS, DS, D]).ap()
    kp = dout("kp", [NPS, T, D]).ap()
    vp = dout("vp", [NPS, T, D]).ap()
    ikp = dout("ikp", [NPS, T, 64]).ap()
    Cp_t = dout("Cp", [NPS, 4, 256, 256])
    np_t = dout("np", [NPS, 4, 256])
    mp_t = dout("mp", [NPS, 4])
    ksd = dout("ks", [NSS, DS, D]).ap()
    vsd = dout("vs", [NSS, DS, D]).ap()
    iksd = dout("iks", [NSS, DS, 64]).ap()
    Cs_t = dout("Cs", [NSS, 4, 256, 256])
    ns_t = dout("ns", [NSS, 4, 256])
    ms_t = dout("ms", [NSS, 4])

    wbf = nc.dram_tensor("wbf", [NG, 1024, 512], BF16, kind="Internal").ap()
    tblD_t = nc.dram_tensor("tblD", [8, 384], F32, kind="Internal")

    OUT = Res("outputs")
    R_wbf = [Res("wbf%d" % g) for g in range(NG)]
    R_tbl = Res("tblD")

    _rs = {}

    def sb(name, shape, dt):
        t = nc.alloc_sbuf_tensor(name, list(shape), dt).ap()
        _rs[name] = Res(name)
        return t

    def R(name):
        return _rs[name]

    kTs = sb("kTs", [128, 8, SK], BF16)
    vst = sb("vst", [128, NKB, 8, 129], BF16)
    kiTs = sb("kiTs", [64, SK], BF16)
    wring = [sb("wr%d" % i, [128, 4, 512], BF16) for i in range(NWH)]
    big = [[sb("big%d_%d" % (ti, k), [128, D], F32) for k in range(2)] for ti in range(NBT)]
    tkb = [[sb("tkb%d_%d" % (ti, k), [128, D], BF16) for k in range(2)] for ti in range(NBT)]
    vaug = [sb("vaug%d" % ti, [128, 4, 257], BF16) for ti in range(NBT)]
    trb = [[sb("trb%d_%d" % (ti, k), [128, 8, 128], BF16) for k in range(4)] for ti in range(NBT)]
    qtmp = [sb("qtmp%d" % i, [128, 512], BF16) for i in range(2)]
    smf = [sb("smf%d" % ti, [128, 80], F32) for ti in range(NBT)]
    kib = [sb("kib%d" % ti, [128, 64], BF16) for ti in range(NBT)]
    sc = sb("sc", [128, SK], F32)
    sel = sb("sel", [128, SK], BF16)
    selTs = [sb("selT%d" % ti, [128, NKB, 128], BF16) for ti in range(NBT)]
    qiTb = [sb("qiTb%d" % ti, [64, 8, 128], BF16) for ti in range(NBT)]
    PT = [sb("PT%d" % i, [128, 4, 128], BF16) for i in range(3)]
    rb = [sb("rb%d" % i, [128, 512], F32) for i in range(3)]
    tmpA = rb
    for i in range(3):
        _rs["tmpA%d" % i] = _rs["rb%d" % i]
    Cst = sb("Cst", [128, 2, 4, 257], F32)
    Csb = sb("Csb", [128, 2, 4, 257], BF16)
    G = sb("G", [128, 8, 2, 128], BF16)
    lngb = sb("lngb", [128, D], F32)
    lnbb = sb("lnbb", [128, D], F32)
    idf = sb("idf", [128, 128], F32)
    idb = sb("idb", [128, 128], BF16)
    tri = sb("tri", [128, 128], F32)
    ones = sb("ones", [128, 128], F32)
    bgb = sb("bgb", [128, 8], F32)
    angT = sb("angT", [128, 8], F32)
    cfar = sb("cfar", [128, 8], F32)
    ncfar = sb("ncfar", [128, 8], F32)
    relb = sb("relb_s", [32, 8], F32)
    ohs = rb[0][0:32, 0:384]
    _rs["ohs"] = _rs["rb0"]
    tb8 = sc[0:8, 0:384]
    _rs["tb8"] = _rs["sc"]
    mrun = sb("mrun", [128, 4], F32)
    gt = sb("gt", [128, 8], F32)
    lfn = sb("lfn", [128, 4], F32)
    dd = sb("dd", [128, 4], F32)
    dmx = sb("dmx", [4, 1], F32)
    D4 = sb("D4", [4, 4], F32)
    Mb = sb("Mb", [128, 4], F32)
    ex12s = [sb("ex12_%d" % ti, [128, 12], F32) for ti in range(NBT)]
    sTm = [sb("sTm%d" % i, [128, 128], BF16) for i in range(2)]
    vu = [sb("vu%d" % i, [128, 257], BF16) for i in range(2)]
    dns = [sb("dn%d" % ti, [128, 4], F32) for ti in range(NBT)]
    for h_ in range(4):
        _rs["Cst%d" % h_] = Res("Cst%d" % h_)
        _rs["Csb%d" % h_] = Res("Csb%d" % h_)
    Rc_all = [_rs["Cst%d" % h_] for h_ in range(4)]
    st4 = sb("st4", [128, 4, 6], F32)
    mv4 = sb("mv4", [128, 4, 2], F32)
    rs4 = sb("rs4", [128, 4], F32)
    aw = sb("aw", [128, 8], F32)
    sg = sb("sg", [128, 8], F32)
    m8 = sb("m8", [128, 8], F32)
    NIT = 9
    bhi = sb("bhi", [128, 1], F32)
    blo = sb("blo", [128, 1], F32)
    bd0 = sb("bd0", [128, 1], F32)
    dks = sb("dks", [128, NIT], F32)
    ndks = sb("ndks", [128, NIT], F32)
    cvec = sb("cvec", [128, NIT], F32)
    ivec3 = sb("ivec3", [128, 3], F32)
    nm3 = sb("nm3", [128, 3], F32)
    ssum = sb("ssum", [128, 3], F32)
    p3 = sb("p3", [128, 3], F32)
    pc = sb("pc", [128, 1], F32)
    thr3 = sb("thr3", [128, 1], F32)
    c3 = sb("c3", [128, 1], F32)
    c23 = sb("c23", [128, 2], F32)
    thr2 = sb("thr2", [128, 1], F32)
    _rs["junkA"] = Res("junkA")
    _rs["junkD"] = Res("junkD")
    xbn = [tkb[ti][0] for ti in range(NBT)]
    for ti in range(NBT):
        _rs["xbn%d" % ti] = _rs["tkb%d_0" % ti]
    sc2 = sb("sc2", [128, T], F32)
    rsum = sb("rsum", [128, 8], F32)
    lst = sb("lst", [128, 2, 6], F32)
    lmv = sb("lmv", [128, 2], F32)
    lrs = sb("lrs", [128, 1], F32)

    banks = []
    for i in range(8):
        t = nc.alloc_psum_tensor("pb%d" % i, [128, 512], F32).ap()
        _rs["pb%d" % i] = Res("pb%d" % i)
        banks.append(t)
    PJ = [(banks[0], R("pb0")), (banks[1], R("pb1"))]
    TPf = [(banks[2], R("pb2")), (banks[3], R("pb3"))]
    TP = [(banks[2].bitcast(BF16).rearrange("p (c t) -> p c t", t=128), R("pb2")),
          (banks[3].bitcast(BF16).rearrange("p (c t) -> p c t", t=128), R("pb3"))]
    LG = [(banks[4], R("pb4")), (banks[5], R("pb5"))]
    OV = [(banks[6], R("pb6")), (banks[7], R("pb7"))]
    cnt = {"pj": 0, "tp": 0, "lg": 0, "qt": 0, "ta": 0, "pt": 0, "rb": 0, "blk": -1, "stm": 0}

    def nxt(k, n):
        v = cnt[k] % n
        cnt[k] += 1
        return v

    def mm(out, lhsT, rhs, start, stop, reads, writes):
        S.op("pe", lambda e: e.matmul(out=out, lhsT=lhsT, rhs=rhs, start=start, stop=stop), reads, writes)

    def tr(out, in_, ident, reads, writes):
        S.op("pe", lambda e: e.transpose(out=out, in_=in_, identity=ident), reads, writes)

    def act(out, in_, func, reads, writes, bias=None, scale=None, accum_out=None):
        kw = {}
        if bias is not None:
            kw["bias"] = bias
        if scale is not None:
            kw["scale"] = scale
        if accum_out is not None:
            kw["accum_out"] = accum_out
        S.op("act", lambda e: e.activation(out=out, in_=in_, func=func, **kw), reads, writes)

    def acopy(out, in_, reads, writes):
        S.op("act", lambda e: e.copy(out=out, in_=in_), reads, writes)

    def vcopy(out, in_, reads, writes, eng="dve"):
        S.op(eng, lambda e: e.tensor_copy(out=out, in_=in_), reads, writes)

    def tt(out, in0, in1, op, reads, writes, eng="dve"):
        S.op(eng, lambda e: e.tensor_tensor(out=out, in0=in0, in1=in1, op=op), reads, writes)

    def ts(out, in0, s1, op0, reads, writes, s2=None, op1=None, eng="dve"):
        if op1 is None:
            S.op(eng, lambda e: e.tensor_scalar(out=out, in0=in0, scalar1=s1, scalar2=None, op0=op0), reads, writes)
        else:
            S.op(eng, lambda e: e.tensor_scalar(out=out, in0=in0, scalar1=s1, scalar2=s2, op0=op0, op1=op1),
                 reads, writes)

    def stt(out, in0, scalar, in1, op0, op1, reads, writes):
        S.op("dve", lambda e: e.scalar_tensor_tensor(out=out, in0=in0, scalar=scalar, in1=in1, op0=op0, op1=op1),
             reads, writes)

    def memset(ap, val, writes, eng="pool"):
        S.op(eng, lambda e: e.memset(ap, val), [], writes)

    def dma(eng, out, in_, reads, writes, nonc=False):
        if nonc:
            S.dma(eng, lambda e: e.dma_start(out=out, in_=in_, allow_slow_non_contiguous=True), reads, writes)
        else:
            S.dma(eng, lambda e: e.dma_start(out=out, in_=in_), reads, writes)

    X = Res("ext_in")

    import os
    SET = os.environ.get("KSET", "abcdef")
    for g in range(NG):
        if "a" in SET:
            dma("pool", wbf[g], wg[g], [X], [R_wbf[g]])
    memset(idf, 0.0, [R("idf")])
    S.op("pool", lambda e: e.affine_select(out=idf, in_=idf, pattern=[[-1, 128]], compare_op=ALU.not_equal,
                                           fill=1.0, base=0, channel_multiplier=1), [R("idf")], [R("idf")])
    vcopy(idb, idf, [R("idf")], [R("idb")])
    memset(tri, 1.0, [R("tri")])
    S.op("pool", lambda e: e.affine_select(out=tri, in_=tri, pattern=[[1, 128]], compare_op=ALU.is_ge,
                                           fill=0.0, base=0, channel_multiplier=-1), [R("tri")], [R("tri")])
    memset(ones, 1.0, [R("ones")])
    for k_ in range(NIT):
        memset(cvec[:, k_:k_ + 1], 0.25 ** (k_ + 1), [R("cvec")])
    for i_ in range(3):
        memset(ivec3[:, i_:i_ + 1], float(i_ + 1), [R("ivec3")])
    memset(vst, 1.0, [R("vst")])
    for ti in range(NBT):
        memset(vaug[ti], 1.0, [R("vaug%d" % ti)])
    if "b" in SET:
        dma("sp", bgb, bass.AP(bg_t, 0, [[0, 128], [1, 8]]), [X], [R("bgb")])
        dma("sp", lngb, bass.AP(lng_t, 0, [[0, 128], [1, D]]), [X], [R("lngb")])
        dma("sp", lnbb, bass.AP(lnb_t, 0, [[0, 128], [1, D]]), [X], [R("lnbb")])
    if "c" in SET:
        dma("sp", angT, bass.AP(ang_t, 0, [[1, 128], [128, 8]]), [X], [R("angT")], nonc=True)
    dma("sp", cfar, bass.AP(relb_t, 15 * 8, [[0, 128], [1, 8]]), [X], [R("cfar")])
    dma("sp", relb, bass.AP(relb_t, 0, [[8, 32], [1, 8]]), [X], [R("relb_s")])
    dma("sp", ohs, oh, [X], [R("ohs")])
    ts(ncfar, cfar, -1.0, ALU.mult, [R("cfar")], [R("ncfar")])
    pjt, pjr = PJ[0]
    mm(pjt[0:8, 0:384], relb, ohs, True, True, [R("relb_s"), R("ohs")], [pjr])
    vcopy(tb8, pjt[0:8, 0:384], [pjr], [R("tb8")])
    dma("sp", bass.AP(tblD_t, 0, [[384, 8], [1, 384]]), tb8, [R("tb8")], [R_tbl])
    antiI = rb[1][:, 0:128]
    _rs["antiI"] = _rs["rb1"]
    memset(antiI, 0.0, [R("antiI")])
    S.op("pool", lambda e: e.affine_select(out=antiI, in_=antiI, pattern=[[1, 128]], compare_op=ALU.not_equal,
                                           fill=1.0, base=-127, channel_multiplier=1), [R("antiI")], [R("antiI")])
    for h in range(8):
        for blk in range(2):
            k = (h * 2 + blk) % 2
            hk, hk_r = rb[2 * k], R("rb%d" % (2 * k))
            src = bass.AP(tblD_t, h * 384 + 129 - 128 * blk, [[1, 128], [1, 128]])
            dma("sp", hk[:, 0:128], src, [R_tbl], [hk_r])
            lg, lgr = LG[k]
            mm(lg[:, 0:128], hk[:, 0:128], antiI, True, True, [hk_r, R("antiI")], [lgr])
            act(G[:, h, blk, :], lg[:, 0:128], AF.Exp, [lgr, R("ncfar")], [R("G")], bias=ncfar[:, h:h + 1], scale=1.0)

    def finish():
        allr = list(_rs.values()) + [OUT, R_tbl] + R_wbf
        for e_ in ("sp",):
            S.wait_all(e_, allr)
        S.emit()
        return nc, S

    if stop == "setup":
        return finish()

    wstate = {"issued": 0, "used": 0}
    nblocks_total = NPS * (T // (128 * NBT)) + NSS
    total_uses = nblocks_total * NG

    def w_issue_upto(u):
        while wstate["issued"] <= u and wstate["issued"] < 2 * total_uses:
            i = wstate["issued"]
            g = (i // 2) % NG
            half = i % 2
            slot = i % NWH
            src = wbf[g].rearrange("(p a) c -> p (a c)", a=8)[:, half * 2048:(half + 1) * 2048]
            dma("sp", wring[slot].rearrange("p a c -> p (a c)"), src, [R_wbf[g]], [R("wr%d" % slot)])
            wstate["issued"] += 1

    def w_get():
        u = wstate["used"]
        w_issue_upto(2 * u + 1 + NWH - 2)
        wstate["used"] += 1
        out = []
        for half in range(2):
            slot = (2 * u + half) % NWH
            out.append((wring[slot], R("wr%d" % slot)))
        return out

    def transposes_to(dst, dst_r, src, src_r, nt, nchunk=8, width=128, evac=None):
        k = nxt("tp", 2)
        tp, tpr = TP[k]
        for c in range(nchunk):
            tr(tp[0:width, c, 0:nt], src[0:nt, c * width:(c + 1) * width], idb[0:nt, 0:nt],
               [src_r, R("idb")], [tpr])
        if evac is None:
            acopy(dst[0:width, 0:nchunk, 0:nt], tp[0:width, 0:nchunk, 0:nt], [tpr], [dst_r])
        else:
            evac(tp, tpr)

    def mlstm_gates(ti, nt):
        hn, hn_r = big[ti][0], R("big%d_0" % ti)
        kab, kab_r = tkb[ti][1], R("tkb%d_1" % ti)
        qaT, qaT_r = trb[ti][1], R("trb%d_1" % ti)
        kaT, kaT_r = trb[ti][2], R("trb%d_2" % ti)
        va, va_r = vaug[ti], R("vaug%d" % ti)
        sm, sm_r = smf[ti], R("smf%d" % ti)
        ex12, ex12_r = ex12s[ti], R("ex12_%d" % ti)
        tpf, tpr = TPf[nxt("tp", 2)]
        tt(gt[0:nt, 0:8], sm[0:nt, 64:72], bgb[0:nt, 0:8], ALU.add, [sm_r, R("bgb")], [R("gt")])
        act(lfn[0:nt, :], gt[0:nt, 4:8], AF.Exp, [R("gt")], [R("lfn")], scale=-1.0)
        act(lfn[0:nt, :], lfn[0:nt, :], AF.Ln, [R("lfn")], [R("lfn")], bias=1.0, scale=1.0)
        mm(tpf[0:nt, 0:4], tri[0:nt, 0:nt], lfn[0:nt, 0:4], True, True, [R("tri"), R("lfn")], [tpr])
        mm(tpf[0:128, 4:8], ones[0:nt, 0:128], lfn[0:nt, 0:4], True, True, [R("ones"), R("lfn")], [tpr])
        tt(dd[0:nt, :], gt[0:nt, 0:4], tpf[0:nt, 0:4], ALU.add, [R("gt"), tpr], [R("dd")])
        mm(tpf[0:4, 16:16 + nt], dd[0:nt, 0:4], idf[0:nt, 0:nt], True, True, [R("dd"), R("idf")], [tpr])
        S.op("dve", lambda e: e.reduce_max(out=dmx[0:4, 0:1], in_=tpf[0:4, 16:16 + nt], axis=AX.X), [tpr], [R("dmx")])
        ts(D4[0:4, 0:4], idf[0:4, 0:4], dmx[0:4, 0:1], ALU.mult, [R("idf"), R("dmx")], [R("D4")])
        mm(tpf[0:128, 160:164], ones[0:4, 0:128], D4[0:4, 0:4], True, True, [R("ones"), R("D4")], [tpr])
        tt(Mb[:, :], mrun[:, :], tpf[0:128, 160:164], ALU.max, [R("mrun"), tpr], [R("Mb")])
        tt(ex12[0:nt, 0:4], dd[0:nt, :], Mb[0:nt, :], ALU.subtract, [R("dd"), R("Mb")], [ex12_r])
        tt(ex12[:, 4:8], mrun[:, :], Mb[:, :], ALU.subtract, [R("mrun"), R("Mb")], [ex12_r])
        tt(ex12[0:nt, 8:12], tpf[0:nt, 0:4], Mb[0:nt, :], ALU.subtract, [tpr, R("Mb")], [ex12_r])
        act(ex12[:, :], ex12[:, :], AF.Exp, [ex12_r], [ex12_r])
        tt(mrun[:, :], Mb[:, :], tpf[0:128, 4:8], ALU.subtract, [R("Mb"), tpr], [R("mrun")])

    def mlstm_head(ti, nt, h, hook=None):
        def hk():
            if hook is not None:
                hook()

        hn, hn_r = big[ti][0], R("big%d_0" % ti)
        kab, kab_r = tkb[ti][1], R("tkb%d_1" % ti)
        qaT, qaT_r = trb[ti][1], R("trb%d_1" % ti)
        kaT, kaT_r = trb[ti][2], R("trb%d_2" % ti)
        va, va_r = vaug[ti], R("vaug%d" % ti)
        ex12, ex12_r = ex12s[ti], R("ex12_%d" % ti)
        dn, dn_r = dns[ti], R("dn%d" % ti)
        cst_r, csb_r = R("Cst%d" % h), R("Csb%d" % h)
        lg, lgr = PJ[nxt("pj", 2)]
        ts(Cst[:, :, h, :], Cst[:, :, h, :], ex12[:, 4 + h:5 + h], ALU.mult, [cst_r, ex12_r], [cst_r])
        hk()
        acopy(Csb[:, :, h, :], Cst[:, :, h, :], [cst_r], [csb_r])
        for c in range(2):
            mm(lg[0:nt, 0:nt], kaT[:, 2 * h + c, 0:nt], qaT[:, 2 * h + c, 0:nt], c == 0, c == 1,
               [kaT_r, qaT_r], [lgr])
        k = nxt("stm", 2)
        hk()
        stt(sTm[k][0:nt, 0:nt], lg[0:nt, 0:nt], ex12[0:nt, h:h + 1], tri[0:nt, 0:nt], ALU.mult, ALU.mult,
            [lgr, ex12_r, R("tri")], [R("sTm%d" % k)])
        nd = lg[0:nt, 128:385]
        mm(nd, sTm[k][0:nt, 0:nt], va[0:nt, h, :], True, False, [R("sTm%d" % k), va_r], [lgr])
        for c in range(2):
            mm(nd, qaT[:, 2 * h + c, 0:nt], Csb[:, c, h, :], False, c == 1, [qaT_r, csb_r], [lgr])
        hk()
        act(dn[0:nt, h:h + 1], lg[0:nt, 384:385], AF.Abs, [lgr], [dn_r])
        tt(dn[0:nt, h:h + 1], dn[0:nt, h:h + 1], ex12[0:nt, 8 + h:9 + h], ALU.max, [dn_r, ex12_r], [dn_r])
        S.op("dve", lambda e: e.reciprocal(out=dn[0:nt, h:h + 1], in_=dn[0:nt, h:h + 1]), [dn_r], [dn_r])
        hk()
        act(hn[0:nt, h * 256:(h + 1) * 256], lg[0:nt, 128:384], AF.Copy, [lgr, dn_r], [hn_r],
            scale=dn[0:nt, h:h + 1])
        act(vu[k][0:nt, :], va[0:nt, h, :], AF.Copy, [va_r, ex12_r], [R("vu%d" % k)], scale=ex12[0:nt, h:h + 1])
        hk()
        for c in range(2):
            ov, ovr = OV[c]
            mm(ov[:, 0:257], kab[0:nt, h * 256 + c * 128:h * 256 + (c + 1) * 128], vu[k][0:nt, :], True, True,
               [kab_r, R("vu%d" % k)], [ovr])
            tt(Cst[:, c, h, :], Cst[:, c, h, :], ov[:, 0:257], ALU.add, [cst_r, ovr], [cst_r])
        hk()

    def mlstm_finish(ti, nt):
        hn, hn_r = big[ti][0], R("big%d_0" % ti)
        for h in range(4):
            S.op("dve", lambda e, h=h: e.bn_stats(out=st4[0:nt, h, :], in_=hn[0:nt, h * 256:(h + 1) * 256]),
                 [hn_r], [R("st4")])
        for h in range(4):
            S.op("dve", lambda e, h=h: e.bn_aggr(out=mv4[0:nt, h, :], in_=st4[0:nt, h, :]), [R("st4")], [R("mv4")])
        act(rs4[0:nt, :], mv4[0:nt, :, 1], AF.Ln, [R("mv4")], [R("rs4")], bias=HN_EPS, scale=1.0)
        act(rs4[0:nt, :], rs4[0:nt, :], AF.Exp, [R("rs4")], [R("rs4")], scale=-0.5)
        for h in range(4):
            ts(hn[0:nt, h * 256:(h + 1) * 256], hn[0:nt, h * 256:(h + 1) * 256], mv4[0:nt, h, 0:1], ALU.subtract,
               [hn_r, R("mv4"), R("rs4")], [hn_r], s2=rs4[0:nt, h:h + 1], op1=ALU.mult)

    def mlstm_all(tiles, hook=None):
        n = len(tiles)
        seq = []
        for step in range(4 + n - 1):
            for ti in range(n):
                h = step - ti
                if 0 <= h < 4:
                    seq.append((ti, h))
        for (ti, h) in seq:
            mlstm_head(ti, tiles[ti][1], h, hook=hook)
            if h == 3:
                mlstm_finish(ti, tiles[ti][1])

    def dsa_pre_thunks(ti, j, nt, ksel):
        qiT, qiT_r = qiTb[ti], R("qiTb%d" % ti)
        sm, sm_r = smf[ti], R("smf%d" % ti)
        selT, selT_r = selTs[ti], R("selT%d" % ti)
        scb, scb_r = (sc, R("sc")) if ti == 0 else (sc2, R("sc2"))
        pos0 = j * 128
        Sk = pos0 + nt
        nkb = j + 1
        th = []

        def t_w():
            act(aw[0:nt, :], sm[0:nt, 72:80], AF.Abs, [sm_r], [R("aw")])
            ts(sg[0:nt, :], sm[0:nt, 72:80], 0.0, ALU.is_gt, [sm_r], [R("sg")], s2=2.0, op1=ALU.mult)
            ts(sg[0:nt, :], sg[0:nt, :], -1.0, ALU.add, [R("sg")], [R("sg")])
        th.append(("pre", t_w))
        for c0 in range(0, Sk, 512):
            w = min(512, Sk - c0)
            for h in range(8):
                def t_s(c0=c0, w=w, h=h):
                    lg, lgr = TPf[nxt("tp", 2)]
                    mm(lg[0:nt, 0:w], qiT[0:64, h, 0:nt], kiTs[0:64, c0:c0 + w], True, True, [qiT_r, R("kiTs")], [lgr])
                    k = nxt("rb", 3)
                    act(rb[k][0:nt, 0:w], lg[0:nt, 0:w], AF.Relu, [lgr, R("aw")], [R("rb%d" % k)],
                        scale=aw[0:nt, h:h + 1])
                    if h == 0:
                        ts(scb[0:nt, c0:c0 + w], rb[k][0:nt, 0:w], sg[0:nt, 0:1], ALU.mult, [R("rb%d" % k), R("sg")],
                           [scb_r])
                    else:
                        stt(scb[0:nt, c0:c0 + w], rb[k][0:nt, 0:w], sg[0:nt, h:h + 1], scb[0:nt, c0:c0 + w], ALU.mult,
                            ALU.add, [R("rb%d" % k), R("sg"), scb_r], [scb_r])
                th.append(("pre", t_s))
        if nt == 128:
            th.append(("pre", lambda: memset(scb[0:64, pos0 + 64:pos0 + 128], NEG, [scb_r], eng="dve")))
        if Sk > ksel:
            Sadm = pos0 + 64 if nt == 128 else Sk
            cthr = 2.0 * ksel - Sk - 0.5

            def t_init():
                S.op("dve", lambda e: e.tensor_reduce(out=bhi[0:nt, 0:1], in_=scb[0:nt, 0:Sk], axis=AX.X, op=ALU.max),
                     [scb_r], [R("bhi")])
                S.op("dve", lambda e: e.tensor_reduce(out=blo[0:nt, 0:1], in_=scb[0:nt, 0:Sadm], axis=AX.X, op=ALU.min),
                     [scb_r], [R("blo")])
                tt(bd0[0:nt, :], bhi[0:nt, :], blo[0:nt, :], ALU.subtract, [R("bhi"), R("blo")], [R("bd0")])
                ts(dks[0:nt, :], cvec[0:nt, :], bd0[0:nt, 0:1], ALU.mult, [R("cvec"), R("bd0")], [R("dks")])
                ts(ndks[0:nt, :], dks[0:nt, :], -1.0, ALU.mult, [R("dks")], [R("ndks")])
                ts(nm3[0:nt, :], ivec3[0:nt, :], ndks[0:nt, 0:1], ALU.mult, [R("ivec3"), R("ndks"), R("blo")], [R("nm3")],
                   s2=blo[0:nt, 0:1], op1=ALU.subtract)
            th.append(("fast", t_init))
            def t_thr3(k_):
                stt(thr3[0:nt, :], dks[0:nt, k_:k_ + 1], 3.0, blo[0:nt, :], ALU.mult, ALU.add,
                    [R("dks"), R("blo")], [R("thr3")])
            th.append(("fast", lambda: t_thr3(0)))
            def dve_count(thr_ap, thr_r, col):
                S.op("dve", lambda e: e.tensor_scalar(out=sel[0:nt, 0:Sk], in0=scb[0:nt, 0:Sk],
                                                      scalar1=thr_ap[0:nt, 0:1], scalar2=0.0, op0=ALU.is_ge, op1=ALU.add,
                                                      accum_out=c23[0:nt, col:col + 1]),
                     [scb_r, thr_r], [R("junkD"), R("c23")])

            def t_thr2(k_):
                stt(thr2[0:nt, :], dks[0:nt, k_:k_ + 1], 2.0, blo[0:nt, :], ALU.mult, ALU.add,
                    [R("dks"), R("blo")], [R("thr2")])

            for k_ in range(NIT):
                odd = (k_ % 2 == 1)

                def t_a():
                    act(sel[0:nt, 0:Sk], scb[0:nt, 0:Sk], AF.Sign, [scb_r, R("nm3")], [R("junkA"), R("ssum")],
                        bias=nm3[0:nt, 0:1], scale=1.0, accum_out=ssum[0:nt, 0:1])

                def t_b(odd=odd):
                    if odd:
                        dve_count(thr2, R("thr2"), 0)
                    else:
                        act(sel[0:nt, 0:Sk], scb[0:nt, 0:Sk], AF.Sign, [scb_r, R("nm3")], [R("junkA"), R("ssum")],
                            bias=nm3[0:nt, 1:2], scale=1.0, accum_out=ssum[0:nt, 1:2])
                    dve_count(thr3, R("thr3"), 1)

                def t_c(k_=k_, odd=odd):
                    if odd:
                        ts(ssum[0:nt, 1:3], c23[0:nt, 0:2], 2.0, ALU.mult, [R("c23")], [R("ssum")], s2=-float(Sk),
                           op1=ALU.add)
                    else:
                        ts(ssum[0:nt, 2:3], c23[0:nt, 1:2], 2.0, ALU.mult, [R("c23")], [R("ssum")], s2=-float(Sk),
                           op1=ALU.add)
                    S.op("dve", lambda e: e.tensor_scalar(out=p3[0:nt, :], in0=ssum[0:nt, 0:3], scalar1=cthr, scalar2=0.0,
                                                          op0=ALU.is_ge, op1=ALU.add, accum_out=pc[0:nt, 0:1]),
                         [R("ssum")], [R("p3"), R("pc")])
                    stt(blo[0:nt, :], pc[0:nt, :], dks[0:nt, k_:k_ + 1], blo[0:nt, :], ALU.mult, ALU.add,
                        [R("pc"), R("dks"), R("blo")], [R("blo")])
                    if k_ + 1 < NIT:
                        ts(nm3[0:nt, :], ivec3[0:nt, :], ndks[0:nt, k_ + 1:k_ + 2], ALU.mult,
                           [R("ivec3"), R("ndks"), R("blo")], [R("nm3")], s2=blo[0:nt, 0:1], op1=ALU.subtract)
                        t_thr3(k_ + 1)
                        if (k_ + 1) % 2 == 1:
                            t_thr2(k_ + 1)
                th.append(("chain", t_a))
                th.append(("chain", t_b))
                th.append(("chainend", t_c))
            th.append(("fast", lambda: ts(sel[0:nt, 0:Sk], scb[0:nt, 0:Sk], blo[0:nt, 0:1], ALU.is_ge,
                                          [scb_r, R("blo")], [R("sel"), R("junkA"), R("junkD")])))
        else:
            th.append(("fast", lambda: ts(sel[0:nt, 0:Sk], scb[0:nt, 0:Sk], -1.0e29, ALU.is_gt, [scb_r],
                                          [R("sel"), R("junkA"), R("junkD")])))
        for k0 in range(0, nkb, 8):
            def t_T(k0=k0):
                kn = min(8, nkb - k0)
                tp, tpr = TP[nxt("tp", 2)]
                full = 0
                for q in range(kn):
                    kb = k0 + q
                    wk = 128 if kb < nkb - 1 else nt
                    tr(tp[0:wk, q, 0:nt], sel[0:nt, kb * 128:kb * 128 + wk], idb[0:nt, 0:nt],
                       [R("sel"), R("junkA"), R("junkD"), R("idb")], [tpr])
                    if wk == 128:
                        full += 1
                if full:
                    acopy(selT[:, k0:k0 + full, 0:nt], tp[:, 0:full, 0:nt], [tpr], [selT_r])
                if full < kn:
                    acopy(selT[0:nt, k0 + full, 0:nt], tp[0:nt, full, 0:nt], [tpr], [selT_r])
            th.append(("fast", t_T))
        return th

    def dsa_attn(ti, j, nt, mask_eng="pool", between=None):
        ob, ob_r = big[ti][0], R("big%d_0" % ti)
        qbT, qbT_r = trb[ti][1], R("trb%d_1" % ti)
        selT, selT_r = selTs[ti], R("selT%d" % ti)
        nkb = j + 1
        groups = []
        nfull = nkb if nt == 128 else nkb - 1
        for k0 in range(0, nfull, 4):
            groups.append((k0, min(4, nfull - k0), 128))
        if nfull < nkb:
            groups.append((nfull, 1, nt))
        scale = 128.0 ** -0.5
        units = [(h, gi) for h in range(8) for gi in range(len(groups))]
        lgs = {}

        def qk(u):
            h, gi = units[u]
            k0, nq, wk = groups[gi]
            lg, lgr = LG[nxt("lg", 2)]
            lg3 = lg.rearrange("p (q t) -> p q t", t=128)
            for q in range(nq):
                kb = k0 + q
                mm(lg3[0:wk, q, 0:nt], kTs[:, h, kb * 128:kb * 128 + wk], qbT[:, h, 0:nt], True, True,
                   [R("kTs"), qbT_r], [lgr])
            lgs[u] = (lg3, lgr)

        qk(0)
        for u, (h, gi) in enumerate(units):
            k0, nq, wk = groups[gi]
            ov, ovr = OV[h % 2]
            if u + 1 < len(units):
                qk(u + 1)
            lg3, lgr = lgs.pop(u)
            pi = nxt("pt", 3)
            pt, ptr = PT[pi], R("PT%d" % pi)
            act(pt[0:wk, 0:nq, 0:nt], lg3[0:wk, 0:nq, 0:nt], AF.Exp, [lgr, R("cfar")], [ptr],
                bias=cfar[0:wk, h:h + 1], scale=scale)
            meng = mask_eng if mask_eng != "mix" else ("pool" if u % 2 == 0 else "dve")
            tt(pt[0:wk, 0:nq, 0:nt], pt[0:wk, 0:nq, 0:nt], selT[0:wk, k0:k0 + nq, 0:nt], ALU.mult,
               [ptr, selT_r], [ptr], eng=meng)
            for q in range(nq):
                kb = k0 + q
                blk = j - kb
                if blk in (0, 1):
                    tt(pt[0:wk, q, 0:nt], pt[0:wk, q, 0:nt], G[0:wk, h, blk, 0:nt], ALU.mult, [ptr, R("G")], [ptr],
                       eng=meng)
            for q in range(nq):
                kb = k0 + q
                mm(ov[0:nt, 0:129], pt[0:wk, q, 0:nt], vst[0:wk, kb, h, :], kb == 0, kb == nkb - 1,
                   [ptr, R("vst")], [ovr])
            if gi == len(groups) - 1:
                S.op("dve", lambda e, h=h, ov=ov: e.reciprocal(out=rsum[0:nt, h:h + 1], in_=ov[0:nt, 128:129]),
                     [ovr], [R("rsum")])
                act(ob[0:nt, h * 128:(h + 1) * 128], ov[0:nt, 0:128], AF.Copy, [ovr, R("rsum")], [ob_r],
                    scale=rsum[0:nt, h:h + 1])
            if between is not None:
                between()

    def store_kv_tile(kind, b, ti, j, nt):
        pass

    def xrows(kind, b, j, nt):
        if kind == "p":
            return xp[b, j * 128:j * 128 + nt, :]
        return xs[b, 0:nt, :]

    def x_load(kind, b, tiles):
        for ti, (j, nt) in enumerate(tiles):
            dma("pool", xbn[ti][0:nt, :], xrows(kind, b, j, nt), [X], [R("xbn%d" % ti)])

    def x_transpose(ti, nt):
        transposes_to(trb[ti][0], R("trb%d_0" % ti), xbn[ti], R("xbn%d" % ti), nt)

    def block(kind, b, tiles, have_x=False, nxt_blk=None):
        cnt["blk"] += 1
        if kind == "p":
            xsrc, ydst, kdst, vdst, ikdst, ksel = xp, yp, kp, vp, ikp, KSEL_P
        else:
            xsrc, ydst, kdst, vdst, ikdst, ksel = xs, ysd, ksd, vsd, iksd, KSEL_S

        def rows(ap, j, nt):
            if kind == "p":
                return ap[b, j * 128:j * 128 + nt, :]
            return ap[b, 0:nt, :]

        if not have_x:
            x_load(kind, b, tiles)
            for ti, (j, nt) in enumerate(tiles):
                x_transpose(ti, nt)
        bgs = [[] for _ in tiles]
        g_attn = GROUPS.index(("qb", 1))

        def advance(g):
            lists = [L for L in bgs if L]
            if not lists:
                return
            P = lists[0]
            n = 0
            chain = False
            while P and n < 9:
                kind_, f_ = P.pop(0)
                f_()
                n += 1
                if kind_ == "chainend":
                    chain = True
                    break
            if chain and len(lists) > 1:
                Q = lists[1]
                m = 0
                while Q and m < 4 and Q[0][0] == "pre":
                    Q.pop(0)[1]()
                    m += 1

        def drain_tile(ti):
            for t_ in range(ti + 1):
                while bgs[t_]:
                    bgs[t_].pop(0)[1]()

        fine = {"t": 0}

        def pop_fine():
            lists = [L for L in bgs if L]
            if not lists:
                return
            fine["t"] ^= 1
            if fine["t"] and len(lists) > 1 and lists[1][0][0] == "pre" and lists[0][0][0] in ("chain", "chainend"):
                lists[1].pop(0)[1]()
            else:
                lists[0].pop(0)[1]()

        def pop_one():
            for L in bgs:
                if L:
                    L.pop(0)[1]()
                    return

        for g, (gk, hf) in enumerate(GROUPS):
            if stop is not None and stop.startswith("g") and int(stop[1:]) == g:
                raise _Stop()
            if stop is not None and stop.startswith("b"):
                bi_, gi_ = stop[1:].split("g")
                if int(bi_) == cnt["blk"] and int(gi_) == g:
                    raise _Stop()
            wts = w_get()
            ncol = 80 if gk == "small" else 512
            pss = []
            for ti, (j, nt) in enumerate(tiles):
                pss.append(PJ[nxt("pj", 2)])
            for half in range(2):
                wt, wt_r = wts[half]
                for ti, (j, nt) in enumerate(tiles):
                    if gk in ("woa", "wob", "wo"):
                        lt, lt_r = trb[ti][3], R("trb%d_3" % ti)
                    else:
                        lt, lt_r = trb[ti][0], R("trb%d_0" % ti)
                    ps, psr = pss[ti]
                    for d4 in range(4):
                        dc = half * 4 + d4
                        mm(ps[0:nt, 0:ncol], lt[:, dc, 0:nt], wt[:, d4, 0:ncol], dc == 0, dc == 7, [lt_r, wt_r], [psr])
            for ti, (j, nt) in enumerate(tiles):
                hn, hn_r = big[ti][0], R("big%d_0" % ti)
                mx, mx_r = big[ti][1], R("big%d_1" % ti)
                ba, ba_r = tkb[ti][0], R("tkb%d_0" % ti)
                kab, kab_r = tkb[ti][1], R("tkb%d_1" % ti)
                xT, xT_r = trb[ti][0], R("trb%d_0" % ti)
                aT, aT_r = trb[ti][3], R("trb%d_3" % ti)
                pos0 = j * 128
                ps, psr = pss[ti]
                cs = slice(hf * 512, (hf + 1) * 512)
                p_ = ps[0:nt, 0:512]
                if gk == "kb":
                    acopy(hn[0:nt, cs], p_, [psr], [hn_r])
                    vcopy(ba[0:nt, cs], hn[0:nt, cs], [hn_r], [ba_r], eng="pool")
                    if hf == 1:
                        dma("sp", rows(kdst, j, nt), hn[0:nt, :], [hn_r], [OUT])
                        k = nxt("tp", 2)
                        tp, tpr = TP[k]
                        for c in range(8):
                            tr(tp[:, c, 0:nt], ba[0:nt, c * 128:(c + 1) * 128], idb[0:nt, 0:nt], [ba_r, R("idb")], [tpr])
                        acopy(kTs[:, :, pos0:pos0 + nt], tp[:, :, 0:nt], [tpr], [R("kTs")])
                elif gk == "vb":
                    acopy(mx[0:nt, cs], p_, [psr], [mx_r])
                    vcopy(vst[0:nt, j, hf * 4:(hf + 1) * 4, 0:128], mx[0:nt, cs].rearrange("p (h d) -> p h d", d=128),
                          [mx_r], [R("vst")], eng="pool")
                    if hf == 1:
                        dma("sp", rows(vdst, j, nt), mx[0:nt, :], [mx_r], [OUT])
                elif gk == "small":
                    acopy(smf[ti][0:nt, :], ps[0:nt, 0:80], [psr], [R("smf%d" % ti)])
                    dma("sp", rows(ikdst, j, nt), smf[ti][0:nt, 0:64], [R("smf%d" % ti)], [OUT])
                    acopy(kib[ti][0:nt, :], smf[ti][0:nt, 0:64], [R("smf%d" % ti)], [R("kib%d" % ti)])
                    tp, tpr = TP[nxt("tp", 2)]
                    tr(tp[0:64, 0, 0:nt], kib[ti][0:nt, 0:64], idb[0:nt, 0:nt], [R("kib%d" % ti), R("idb")], [tpr])
                    acopy(kiTs[0:64, pos0:pos0 + nt], tp[0:64, 0, 0:nt], [tpr], [R("kiTs")])
                    mlstm_gates(ti, nt)
                elif gk in ("qa", "qb"):
                    qi_ = nxt("qt", 2)
                    vcopy(qtmp[qi_][0:nt, :], p_, [psr], [R("qtmp%d" % qi_)])
                    dstT, dstT_r = trb[ti][1], R("trb%d_1" % ti)
                    tp, tpr = TP[nxt("tp", 2)]
                    for c in range(4):
                        tr(tp[:, c, 0:nt], qtmp[qi_][0:nt, c * 128:(c + 1) * 128], idb[0:nt, 0:nt],
                           [R("qtmp%d" % qi_), R("idb")], [tpr])
                    acopy(dstT[:, hf * 4:(hf + 1) * 4, 0:nt], tp[:, 0:4, 0:nt], [tpr], [dstT_r])
                    if gk == "qb" and hf == 1:
                        drain_tile(ti)
                        last = ti == len(tiles) - 1

                        def between():
                            pop_one()
                        dsa_attn(ti, j, nt, mask_eng="dve", between=None if last else between)
                elif gk == "ka":
                    act(kab[0:nt, cs], p_, AF.Copy, [psr], [kab_r], scale=1.0 / 16.0)
                    if hf == 1:
                        transposes_to(trb[ti][2], R("trb%d_2" % ti), kab, kab_r, nt)
                elif gk == "va":
                    acopy(vaug[ti][0:nt, hf * 2:(hf + 1) * 2, 0:256], p_.rearrange("p (h d) -> p h d", d=256),
                          [psr], [R("vaug%d" % ti)])
                    if hf == 1 and ti == len(tiles) - 1:
                        mlstm_all(tiles, hook=pop_fine)
                elif gk == "oa":
                    k = nxt("rb", 3)
                    act(tmpA[k][0:nt, :], p_, AF.Sigmoid, [psr], [R("tmpA%d" % k)])
                    tt(hn[0:nt, cs], hn[0:nt, cs], tmpA[k][0:nt, :], ALU.mult, [hn_r, R("tmpA%d" % k)], [hn_r], eng="pool")
                elif gk == "za":
                    k = nxt("rb", 3)
                    act(tmpA[k][0:nt, :], p_, AF.Silu, [psr], [R("tmpA%d" % k)])
                    tt(ba[0:nt, cs], hn[0:nt, cs], tmpA[k][0:nt, :], ALU.mult, [hn_r, R("tmpA%d" % k)], [ba_r],
                       eng="pool" if gk == "za" else "dve")
                    if hf == 1:
                        def ev(tp, tpr, nt=nt, aT=aT, aT_r=aT_r):
                            for c in range(8):
                                act(aT[:, c, 0:nt], tp[:, c, 0:nt], AF.Copy, [tpr, R("angT")], [aT_r],
                                    scale=angT[:, c:c + 1])
                        transposes_to(aT, aT_r, ba, ba_r, nt, evac=ev)
                elif gk == "woa":
                    vcopy(mx[0:nt, cs], p_, [psr], [mx_r])
                elif gk == "ga":
                    k = nxt("rb", 3)
                    act(tmpA[k][0:nt, :], p_, AF.Sigmoid, [psr], [R("tmpA%d" % k)])
                    tt(mx[0:nt, cs], mx[0:nt, cs], tmpA[k][0:nt, :], ALU.mult, [mx_r, R("tmpA%d" % k)], [mx_r], eng="pool")
                elif gk == "qi":
                    qi_ = nxt("qt", 2)
                    acopy(qtmp[qi_][0:nt, :], p_, [psr], [R("qtmp%d" % qi_)])
                    dstT, dstT_r = qiTb[ti], R("qiTb%d" % ti)
                    tp, tpr = TP[nxt("tp", 2)]
                    for c in range(8):
                        tr(tp[0:64, c, 0:nt], qtmp[qi_][0:nt, c * 64:(c + 1) * 64], idb[0:nt, 0:nt],
                           [R("qtmp%d" % qi_), R("idb")], [tpr])
                    acopy(dstT[0:64, :, 0:nt], tp[0:64, :, 0:nt], [tpr], [dstT_r])
                    bgs[ti].extend(dsa_pre_thunks(ti, j, nt, ksel))
                elif gk == "zb":
                    k = nxt("rb", 3)
                    act(tmpA[k][0:nt, :], p_, AF.Silu, [psr], [R("tmpA%d" % k)])
                    tt(ba[0:nt, cs], hn[0:nt, cs], tmpA[k][0:nt, :], ALU.mult, [hn_r, R("tmpA%d" % k)], [ba_r],
                       eng="pool" if gk == "za" else "dve")
                    if hf == 1:
                        transposes_to(aT, aT_r, ba, ba_r, nt)
                elif gk == "wob":
                    vcopy(hn[0:nt, cs], p_, [psr], [hn_r])
                elif gk == "gb":
                    k = nxt("rb", 3)
                    act(tmpA[k][0:nt, :], p_, AF.Sigmoid, [psr], [R("tmpA%d" % k)])
                    tt(tmpA[k][0:nt, :], tmpA[k][0:nt, :], hn[0:nt, cs], ALU.mult, [hn_r, R("tmpA%d" % k)],
                       [R("tmpA%d" % k)])
                    tt(mx[0:nt, cs], mx[0:nt, cs], tmpA[k][0:nt, :], ALU.add, [mx_r, R("tmpA%d" % k)], [mx_r])
                    if hf == 1:
                        acopy(ba[0:nt, :], mx[0:nt, :], [mx_r], [ba_r])
                        transposes_to(aT, aT_r, ba, ba_r, nt)
                        dma("sp", hn[0:nt, :], rows(xsrc, j, nt), [X], [hn_r])
                        if nxt_blk is not None and ti < len(nxt_blk[2]):
                            jn_, ntn_ = nxt_blk[2][ti]
                            dma("pool", xbn[ti][0:ntn_, :], xrows(nxt_blk[0], nxt_blk[1], jn_, ntn_), [X],
                                [R("xbn%d" % ti)])
                elif gk == "wo":
                    stt(mx[0:nt, cs], hn[0:nt, cs], ALPHA, p_, ALU.mult, ALU.add, [hn_r, psr], [mx_r])
                    if hf == 1:
                        for c in range(2):
                            S.op("dve", lambda e, c=c, mx=mx, nt=nt: e.bn_stats(out=lst[0:nt, c, :],
                                                                                in_=mx[0:nt, c * 512:(c + 1) * 512]),
                                 [mx_r], [R("lst")])
                        S.op("dve", lambda e, nt=nt: e.bn_aggr(out=lmv[0:nt, :], in_=lst[0:nt, :, :]), [R("lst")], [R("lmv")])
                        act(lrs[0:nt, :], lmv[0:nt, 1:2], AF.Ln, [R("lmv")], [R("lrs")], bias=LN_EPS, scale=1.0)
                        act(lrs[0:nt, :], lrs[0:nt, :], AF.Exp, [R("lrs")], [R("lrs")], scale=-0.5)
                        ts(mx[0:nt, :], mx[0:nt, :], lmv[0:nt, 0:1], ALU.subtract, [mx_r, R("lmv"), R("lrs")], [mx_r],
                           s2=lrs[0:nt, 0:1], op1=ALU.mult)
                        tt(mx[0:nt, :], mx[0:nt, :], lngb[0:nt, :], ALU.mult, [mx_r, R("lngb")], [mx_r])
                        tt(mx[0:nt, :], mx[0:nt, :], lnbb[0:nt, :], ALU.add, [mx_r, R("lnbb")], [mx_r])
                        dma("sp", rows(ydst, j, nt), mx[0:nt, :], [mx_r], [OUT])
                        if nxt_blk is not None and ti < len(nxt_blk[2]):
                            x_transpose(ti, nxt_blk[2][ti][1])
                if g < g_attn:
                    advance(g)

    def state_out(C_t, n_t, m_t, b):
        for h in range(4):
            for c in range(2):
                dst = bass.AP(C_t, ((b * 4 + h) * 256 + c * 128) * 256, [[256, 128], [1, 256]])
                dma("sp", dst, Cst[:, c, h, 0:256], Rc_all, [OUT])
        for c in range(2):
            dstn = bass.AP(n_t, b * 1024 + c * 128, [[1, 128], [256, 4]])
            dma("sp", dstn, Cst[:, c, :, 256], Rc_all, [OUT], nonc=True)
        dstm = bass.AP(m_t, b * 4, [[4, 1], [1, 4]])
        dma("sp", dstm, mrun[0:1, 0:4], [R("mrun")], [OUT])

    def main_body():
        plan = []
        nbp = T // (128 * NBT)
        for b in range(NPS):
            for bi in range(nbp):
                plan.append(("p", b, [(bi * NBT + i, 128) for i in range(NBT)], bi == 0, bi == nbp - 1))
        nb = PAST // 128
        for b in range(NSS):
            plan.append(("s", b, [(nb, DS)], True, True))
        for i, (kind, b, tiles, first, last) in enumerate(plan):
            if first and kind == "p":
                memset(Cst, 0.0, Rc_all)
                memset(mrun, 0.0, [R("mrun")])
            if first and kind == "s":
                for h in range(4):
                    for c in range(2):
                        src = bass.AP(sC_t, ((b * 4 + h) * 256 + c * 128) * 256, [[256, 128], [1, 256]])
                        dma("sp", Cst[:, c, h, 0:256], src, [X], Rc_all)
                for c in range(2):
                    srcn = bass.AP(sn_t, b * 1024 + c * 128, [[1, 128], [256, 4]])
                    dma("sp", Cst[:, c, :, 256], srcn, [X], Rc_all, nonc=True)
                dma("sp", mrun, bass.AP(sm_t, b * 4, [[0, 128], [1, 4]]), [X], [R("mrun")])
                kt, kt_r = tkb[0][0], R("tkb0_0")
                for blk in range(nb):
                    dma("pool", kt, ck[b, blk * 128:(blk + 1) * 128, :], [X], [kt_r])
                    tp, tpr = TP[nxt("tp", 2)]
                    for c in range(8):
                        tr(tp[:, c, :], kt[:, c * 128:(c + 1) * 128], idb, [kt_r, R("idb")], [tpr])
                    acopy(kTs[:, :, blk * 128:(blk + 1) * 128], tp[:, :, :], [tpr], [R("kTs")])
                    dma("pool", vst[:, blk, :, 0:128],
                        cv[b, blk * 128:(blk + 1) * 128, :].rearrange("p (h d) -> p h d", d=128), [X], [R("vst")])
                kit, kit_r = tkb[0][1], R("tkb0_1")
                kit3 = kit[:, 0:nb * 64].rearrange("p (n c) -> p n c", c=64)
                dma("pool", kit3, cik[b].rearrange("(n p) c -> p n c", p=128), [X], [kit_r])
                for k0 in range(0, nb, 8):
                    kn = min(8, nb - k0)
                    tp, tpr = TP[nxt("tp", 2)]
                    for q in range(kn):
                        tr(tp[0:64, q, :], kit3[:, k0 + q, :], idb, [kit_r, R("idb")], [tpr])
                    acopy(kiTs[0:64, k0 * 128:(k0 + kn) * 128].rearrange("p (n t) -> p n t", t=128), tp[0:64, 0:kn, :],
                          [tpr], [R("kiTs")])
            nb_ = plan[i + 1] if i + 1 < len(plan) else None
            block(kind, b, tiles, have_x=(i > 0), nxt_blk=nb_)
            if last:
                if kind == "p":
                    state_out(Cp_t, np_t, mp_t, b)
                else:
                    state_out(Cs_t, ns_t, ms_t, b)

    try:
        main_body()
    except _Stop:
        return finish()
    S.wait_all("sp", [OUT])
    S.emit()
    return nc, S


def _run(inputs, NPS, T, NSS, PAST, ncores, stop=None):
    nc, S = build(NPS, T, NSS, PAST, stop=stop)
    wgh = _host_weight_groups(np.asarray(inputs["w_in"], np.float32), np.asarray(inputs["w_out_a"], np.float32),
                              np.asarray(inputs["w_out_b"], np.float32), np.asarray(inputs["w_o"], np.float32))
    wgh = np.ascontiguousarray(wgh.reshape(NG, 1024, 512))
    oh = _bucket_onehot()
    in_maps = []
    f = lambda a: np.ascontiguousarray(np.asarray(a, np.float32))
    for c in range(ncores):
        ps = slice(c * NPS, (c + 1) * NPS)
        ss = slice(c * NSS, (c + 1) * NSS)
        in_maps.append({
            "xp": f(inputs["x_prompt"][ps]),
            "xs": f(inputs["x_sample"][ss]),
            "ck": f(inputs["cache_k"][ss]).reshape(NSS, PAST, D),
            "cv": f(inputs["cache_v"][ss]).reshape(NSS, PAST, D),
            "cik": f(inputs["cache_idx_k"][ss]),
            "sC": f(inputs["state_C"][ss]),
            "sn": f(inputs["state_n"][ss]),
            "sm": f(inputs["state_m"][ss]),
            "wg": wgh,
            "bg": f(inputs["b_gates"]),
            "ang": f(inputs["a_norm_g"]),
            "relb": f(inputs["rel_bias"]),
            "lng": f(inputs["ln_g"]),
            "lnb": f(inputs["ln_b"]),
            "oh": oh,
        })
    res = run_bass_kernel_spmd(nc, in_maps, core_ids=list(range(ncores)))
    rr = res.results
    cat = lambda k: np.concatenate([np.asarray(r[k], np.float32) for r in rr], axis=0)
    y_p = cat("yp")
    y_s = cat("ys")
    k_p = cat("kp").reshape(-1, T, 8, 128)
    v_p = cat("vp").reshape(-1, T, 8, 128)
    ik_p = cat("ikp")
    C_p, n_p, m_p = cat("Cp"), cat("np"), cat("mp")
    k_s = cat("ks").reshape(-1, DS, 8, 128)
    v_s = cat("vs").reshape(-1, DS, 8, 128)
    ik_s = cat("iks")
    C_s, n_s, m_s = cat("Cs"), cat("ns"), cat("ms")
    return (y_p, y_s, k_p, v_p, ik_p, C_p, n_p, m_p, k_s, v_s, ik_s, C_s, n_s, m_s)


def kernel(x_prompt, x_sample, cache_k, cache_v, cache_idx_k, state_C, state_n, state_m,
           w_in, b_gates, a_norm_g, w_out_a, w_out_b, w_o, rel_bias, ln_g, ln_b):
    inputs = dict(x_prompt=x_prompt, x_sample=x_sample, cache_k=cache_k, cache_v=cache_v, cache_idx_k=cache_idx_k,
                  state_C=state_C, state_n=state_n, state_m=state_m, w_in=w_in, b_gates=b_gates, a_norm_g=a_norm_g,
                  w_out_a=w_out_a, w_out_b=w_out_b, w_o=w_o, rel_bias=rel_bias, ln_g=ln_g, ln_b=ln_b)
    B, T = x_prompt.shape[0], x_prompt.shape[1]
    BS, PAST = x_sample.shape[0], cache_k.shape[1]
    return _run(inputs, B // NCORES, T, BS // NCORES, PAST, NCORES)
```

```python
import math
import numpy as np
import concourse.bass as bass
import concourse.mybir as mybir
from concourse.bass_utils import run_bass_kernel_spmd

F32 = mybir.dt.float32
BF16 = mybir.dt.bfloat16
AF = mybir.ActivationFunctionType
ALU = mybir.AluOpType
AX = mybir.AxisListType

D = 1024
NCORES = 8
DS = 32
NEG = -2.0e30
REPL = -1.0e30
LN_EPS = 1e-5
HN_EPS = 1e-6
ALPHA = 2.0 ** 0.25
NW = 2
NWH = 4
NBT = 2

_off = {}
_c = 0
for _n, _s in (("qa", 1024), ("ka", 1024), ("va", 1024), ("oa", 1024), ("za", 1024), ("ia", 4), ("fa", 4),
               ("qb", 1024), ("kb", 1024), ("vb", 1024), ("zb", 1024), ("qi", 512), ("ki", 64), ("wi", 8),
               ("ga", 1024), ("gb", 1024)):
    _off[_n] = _c
    _c += _s
IN_COLS = _c
GROUPS = []
for _k in ("kb", "vb"):
    GROUPS += [(_k, 0), (_k, 1)]
GROUPS += [("small", 0), ("qi", 0)]
for _k in ("qa", "ka", "va", "za", "oa", "woa", "ga", "qb", "zb", "wob", "gb", "wo"):
    GROUPS += [(_k, 0), (_k, 1)]
NG = len(GROUPS)


def _host_weight_groups(w_in, w_out_a, w_out_b, w_o):
    out = np.zeros((NG, 128, 8, 512), np.float32)
    for g, (kind, hf) in enumerate(GROUPS):
        if kind == "small":
            cols = np.zeros((1024, 512), np.float32)
            cols[:, 0:64] = w_in[:, _off["ki"]:_off["ki"] + 64]
            cols[:, 64:68] = w_in[:, _off["ia"]:_off["ia"] + 4]
            cols[:, 68:72] = w_in[:, _off["fa"]:_off["fa"] + 4]
            cols[:, 72:80] = w_in[:, _off["wi"]:_off["wi"] + 8]
        elif kind == "woa":
            cols = w_out_a[:, hf * 512:(hf + 1) * 512]
        elif kind == "wob":
            cols = w_out_b[:, hf * 512:(hf + 1) * 512]
        elif kind == "wo":
            cols = w_o[:, hf * 512:(hf + 1) * 512]
        elif kind == "qi":
            cols = w_in[:, _off["qi"]:_off["qi"] + 512]
        else:
            cols = w_in[:, _off[kind] + hf * 512:_off[kind] + (hf + 1) * 512]
        out[g] = cols.reshape(8, 128, 512).transpose(1, 0, 2)
    return out


def _bucket_onehot():
    import jax
    import jax.numpy as jnp
    n_buckets, max_distance = 32, 128
    cpu = jax.devices("cpu")[0]
    with jax.default_device(cpu):
        rel = jnp.arange(384, dtype=jnp.int32) - 256
        half = n_buckets // 2
        max_exact = half // 2
        ret = jnp.where(rel > 0, half, 0)
        n = jnp.abs(rel)
        nf = jnp.maximum(n, 1).astype(jnp.float32)
        large = max_exact + (jnp.log(nf / max_exact) / math.log(max_distance / max_exact)
                             * (half - max_exact)).astype(jnp.int32)
        large = jnp.minimum(large, half - 1)
        bk = np.asarray(ret + jnp.where(n < max_exact, n, large))
    oh = np.zeros((32, 384), np.float32)
    oh[bk, np.arange(384)] = 1.0
    return oh


class Res:
    __slots__ = ("name", "w", "r")

    def __init__(self, name):
        self.name = name
        self.w = None
        self.r = {}


EPOCH = 30000
NDMASEM = 8


class Sched:
    ENGS = ("pe", "act", "dve", "pool", "sp")

    def __init__(self, nc):
        self.nc = nc
        self.ops = {e: [] for e in self.ENGS}
        self.cnt = {e: 0 for e in self.ENGS}
        self.epoch = {e: 0 for e in self.ENGS}
        self.sems = {}
        self.seen = {e: {} for e in self.ENGS}
        self.dma_i = {e: 0 for e in self.ENGS}
        self.ninstr = 0
        self.final = {}

    def _sem(self, key):
        if key not in self.sems:
            self.sems[key] = self.nc.alloc_semaphore(name="s_%s_%s" % key)
        return self.sems[key]

    def _deps(self, eng, reads, writes):
        toks = []
        for r in reads:
            if r.w is not None:
                toks.append(r.w)
        for w in writes:
            if w.w is not None:
                toks.append(w.w)
            toks.extend(w.r.items())
        waits = {}
        for (key, val) in toks:
            if eng == "pe" and key[0] == "pe":
                continue
            if self.seen[eng].get(key, 0) >= val:
                continue
            if waits.get(key, 0) < val:
                waits[key] = val
        for key, val in waits.items():
            self.seen[eng][key] = val
        return [(self._sem(k), v) for k, v in waits.items()]

    def _commit(self, tok, reads, writes):
        for w in writes:
            w.w = tok
            w.r = {}
        for r in reads:
            if r not in writes:
                if r.r.get(tok[0], 0) < tok[1]:
                    r.r[tok[0]] = tok[1]

    def op(self, eng, fn, reads=(), writes=()):
        pbs = [r for r in reads if r.name.startswith("pb") and r not in writes]
        if pbs:
            writes = list(writes) + pbs
        waits = self._deps(eng, reads, writes)
        if self.cnt[eng] >= EPOCH:
            self.epoch[eng] += 1
            self.cnt[eng] = 0
        self.cnt[eng] += 1
        key = (eng, self.epoch[eng])
        tok = (key, self.cnt[eng])
        self.ops[eng].append((waits, fn, self._sem(key), 1))
        self.final[key] = tok[1]
        self._commit(tok, reads, writes)
        self.ninstr += 1 + len(waits)
        return tok

    def dma(self, eng, fn, reads=(), writes=()):
        i = self.dma_i[eng]
        self.dma_i[eng] += 1
        nsem = 3 if eng == "pool" else NDMASEM
        slot = i % nsem
        gen = i // nsem
        key = ("dma_" + eng, slot)
        waits = self._deps(eng, reads, writes)
        sem = self._sem(key)
        if gen > 0 and self.seen[eng].get(key, 0) < 16 * gen:
            waits.append((sem, 16 * gen))
            self.seen[eng][key] = 16 * gen
        tok = (key, 16 * (gen + 1))
        self.ops[eng].append((waits, fn, sem, 16))
        self.final[key] = tok[1]
        self._commit(tok, reads, writes)
        self.ninstr += 1 + len(waits)
        return tok

    def wait_all(self, eng, resources=()):
        waits = []
        for key, val in self.final.items():
            if self.seen[eng].get(key, 0) < val:
                waits.append((self._sem(key), val))
                self.seen[eng][key] = val
        self.ops[eng].append((waits, None, None, 0))

    def emit(self):
        nc = self.nc
        ops = self.ops
        with nc.Block() as block:
            def run(e, lst):
                for waits, fn, sem, inc in lst:
                    for s, v in waits:
                        e.wait_ge(s, v)
                    if fn is not None:
                        fn(e).then_inc(sem, inc)

            @block.tensor
            def _(e):
                run(e, ops["pe"])

            @block.scalar
            def _(e):
                run(e, ops["act"])

            @block.vector
            def _(e):
                run(e, ops["dve"])

            @block.gpsimd
            def _(e):
                run(e, ops["pool"])

            @block.sync
            def _(e):
                run(e, ops["sp"])


class _Stop(Exception):
    pass


def build(NPS, T, NSS, PAST, debug=False, stop=None):
    assert T % 256 == 0 and PAST % 128 == 0
    nc = bass.Bass("TRN2", target_bir_lowering=False)
    S = Sched(nc)
    KSEL_P = min(256, T // 4)
    KSEL_S = min(256, (PAST + DS) // 4)
    assert KSEL_P % 8 == 0 and KSEL_S % 8 == 0
    SK = max(T, PAST + 128)
    NKB = SK // 128

    def din(name, shape, dt=F32):
        return nc.dram_tensor(name, list(shape), dt, kind="ExternalInput")

    def dout(name, shape, dt=F32):
        return nc.dram_tensor(name, list(shape), dt, kind="ExternalOutput")

    xp = din("xp", [NPS, T, D]).ap()
    xs = din("xs", [NSS, DS, D]).ap()
    ck = din("ck", [NSS, PAST, D]).ap()
    cv = din("cv", [NSS, PAST, D]).ap()
    cik = din("cik", [NSS, PAST, 64]).ap()
    sC_t = din("sC", [NSS, 4, 256, 256])
    sn_t = din("sn", [NSS, 4, 256])
    sm_t = din("sm", [NSS, 4])
    wg = din("wg", [NG, 1024, 512]).ap()
    bg_t = din("bg", [8])
    ang_t = din("ang", [D])
    relb_t = din("relb", [32, 8])
    lng_t = din("lng", [D])
    lnb_t = din("lnb", [D])
    oh = din("oh", [32, 384]).ap()

    yp = dout("yp", [NPS, T, D]).ap()
    ysd = dout("ys", [NSS, DS, D]).ap()
    kp = dout("kp", [NPS, T, D]).ap()
    vp = dout("vp", [NPS, T, D]).ap()
    ikp = dout("ikp", [NPS, T, 64]).ap()
    Cp_t = dout("Cp", [NPS, 4, 256, 256])
    np_t = dout("np", [NPS, 4, 256])
    mp_t = dout("mp", [NPS, 4])
    ksd = dout("ks", [NSS, DS, D]).ap()
    vsd = dout("vs", [NSS, DS, D]).ap()
    iksd = dout("iks", [NSS, DS, 64]).ap()
    Cs_t = dout("Cs", [NSS, 4, 256, 256])
    ns_t = dout("ns", [NSS, 4, 256])
    ms_t = dout("ms", [NSS, 4])

    wbf = nc.dram_tensor("wbf", [NG, 1024, 512], BF16, kind="Internal").ap()
    tblD_t = nc.dram_tensor("tblD", [8, 384], F32, kind="Internal")

    OUT = Res("outputs")
    R_wbf = [Res("wbf%d" % g) for g in range(NG)]
    R_tbl = Res("tblD")

    _rs = {}

    def sb(name, shape, dt):
        t = nc.alloc_sbuf_tensor(name, list(shape), dt).ap()
        _rs[name] = Res(name)
        return t

    def R(name):
        return _rs[name]

    kTs = sb("kTs", [128, 8, SK], BF16)
    vst = sb("vst", [128, NKB, 8, 129], BF16)
    kiTs = sb("kiTs", [64, SK], BF16)
    wring = [sb("wr%d" % i, [128, 4, 512], BF16) for i in range(NWH)]
    big = [[sb("big%d_%d" % (ti, k), [128, D], F32) for k in range(2)] for ti in range(NBT)]
    tkb = [[sb("tkb%d_%d" % (ti, k), [128, D], BF16) for k in range(2)] for ti in range(NBT)]
    vaug = [sb("vaug%d" % ti, [128, 4, 257], BF16) for ti in range(NBT)]
    trb = [[sb("trb%d_%d" % (ti, k), [128, 8, 128], BF16) for k in range(4)] for ti in range(NBT)]
    qtmp = [sb("qtmp%d" % i, [128, 512], BF16) for i in range(2)]
    smf = [sb("smf%d" % ti, [128, 80], F32) for ti in range(NBT)]
    kib = [sb("kib%d" % ti, [128, 64], BF16) for ti in range(NBT)]
    sc = sb("sc", [128, SK], F32)
    sel = sb("sel", [128, SK], BF16)
    selTs = [sb("selT%d" % ti, [128, NKB, 128], BF16) for ti in range(NBT)]
    qiTb = [sb("qiTb%d" % ti, [64, 8, 128], BF16) for ti in range(NBT)]
    PT = [sb("PT%d" % i, [128, 4, 128], BF16) for i in range(3)]
    rb = [sb("rb%d" % i, [128, 512], F32) for i in range(3)]
    tmpA = rb
    for i in range(3):
        _rs["tmpA%d" % i] = _rs["rb%d" % i]
    Cst = sb("Cst", [128, 2, 4, 257], F32)
    Csb = sb("Csb", [128, 2, 4, 257], BF16)
    G = sb("G", [128, 8, 2, 128], BF16)
    lngb = sb("lngb", [128, D], F32)
    lnbb = sb("lnbb", [128, D], F32)
    idf = sb("idf", [128, 128], F32)
    idb = sb("idb", [128, 128], BF16)
    tri = sb("tri", [128, 128], F32)
    ones = sb("ones", [128, 128], F32)
    bgb = sb("bgb", [128, 8], F32)
    angT = sb("angT", [128, 8], F32)
    cfar = sb("cfar", [128, 8], F32)
    ncfar = sb("ncfar", [128, 8], F32)
    relb = sb("relb_s", [32, 8], F32)
    ohs = rb[0][0:32, 0:384]
    _rs["ohs"] = _rs["rb0"]
    tb8 = sc[0:8, 0:384]
    _rs["tb8"] = _rs["sc"]
    mrun = sb("mrun", [128, 4], F32)
    gt = sb("gt", [128, 8], F32)
    lfn = sb("lfn", [128, 4], F32)
    dd = sb("dd", [128, 4], F32)
    dmx = sb("dmx", [4, 1], F32)
    D4 = sb("D4", [4, 4], F32)
    Mb = sb("Mb", [128, 4], F32)
    ex12s = [sb("ex12_%d" % ti, [128, 12], F32) for ti in range(NBT)]
    sTm = [sb("sTm%d" % i, [128, 128], BF16) for i in range(2)]
    vu = [sb("vu%d" % i, [128, 257], BF16) for i in range(2)]
    dns = [sb("dn%d" % ti, [128, 4], F32) for ti in range(NBT)]
    for h_ in range(4):
        _rs["Cst%d" % h_] = Res("Cst%d" % h_)
        _rs["Csb%d" % h_] = Res("Csb%d" % h_)
    Rc_all = [_rs["Cst%d" % h_] for h_ in range(4)]
    st4 = sb("st4", [128, 4, 6], F32)
    mv4 = sb("mv4", [128, 4, 2], F32)
    rs4 = sb("rs4", [128, 4], F32)
    aw = sb("aw", [128, 8], F32)
    sg = sb("sg", [128, 8], F32)
    m8 = sb("m8", [128, 8], F32)
    NIT = 9
    bhi = sb("bhi", [128, 1], F32)
    blo = sb("blo", [128, 1], F32)
    bd0 = sb("bd0", [128, 1], F32)
    dks = sb("dks", [128, NIT], F32)
    ndks = sb("ndks", [128, NIT], F32)
    cvec = sb("cvec", [128, NIT], F32)
    ivec3 = sb("ivec3", [128, 3], F32)
    nm3 = sb("nm3", [128, 3], F32)
    ssum = sb("ssum", [128, 3], F32)
    p3 = sb("p3", [128, 3], F32)
    pc = sb("pc", [128, 1], F32)
    thr3 = sb("thr3", [128, 1], F32)
    c3 = sb("c3", [128, 1], F32)
    c23 = sb("c23", [128, 2], F32)
    thr2 = sb("thr2", [128, 1], F32)
    _rs["junkA"] = Res("junkA")
    _rs["junkD"] = Res("junkD")
    xbn = [tkb[ti][0] for ti in range(NBT)]
    for ti in range(NBT):
        _rs["xbn%d" % ti] = _rs["tkb%d_0" % ti]
    sc2 = sb("sc2", [128, T], F32)
    rsum = sb("rsum", [128, 8], F32)
    lst = sb("lst", [128, 2, 6], F32)
    lmv = sb("lmv", [128, 2], F32)
    lrs = sb("lrs", [128, 1], F32)
    lnb1 = sb("lnb1", [128, 1], F32)

    banks = []
    for i in range(8):
        t = nc.alloc_psum_tensor("pb%d" % i, [128, 512], F32).ap()
        _rs["pb%d" % i] = Res("pb%d" % i)
        banks.append(t)
    PJ = [(banks[0], R("pb0")), (banks[1], R("pb1"))]
    TPf = [(banks[2], R("pb2")), (banks[3], R("pb3"))]
    TP = [(banks[2].bitcast(BF16).rearrange("p (c t) -> p c t", t=128), R("pb2")),
          (banks[3].bitcast(BF16).rearrange("p (c t) -> p c t", t=128), R("pb3"))]
    LG = [(banks[4], R("pb4")), (banks[5], R("pb5"))]
    OV = [(banks[6], R("pb6")), (banks[7], R("pb7"))]
    cnt = {"pj": 0, "tp": 0, "lg": 0, "qt": 0, "ta": 0, "pt": 0, "rb": 0, "blk": -1, "stm": 0}

    def nxt(k, n):
        v = cnt[k] % n
        cnt[k] += 1
        return v

    def mm(out, lhsT, rhs, start, stop, reads, writes):
        S.op("pe", lambda e: e.matmul(out=out, lhsT=lhsT, rhs=rhs, start=start, stop=stop), reads, writes)

    def tr(out, in_, ident, reads, writes):
        S.op("pe", lambda e: e.transpose(out=out, in_=in_, identity=ident), reads, writes)

    def act(out, in_, func, reads, writes, bias=None, scale=None, accum_out=None):
        kw = {}
        if bias is not None:
            kw["bias"] = bias
        if scale is not None:
            kw["scale"] = scale
        if accum_out is not None:
            kw["accum_out"] = accum_out
        S.op("act", lambda e: e.activation(out=out, in_=in_, func=func, **kw), reads, writes)

    def acopy(out, in_, reads, writes):
        S.op("act", lambda e: e.copy(out=out, in_=in_), reads, writes)

    def vcopy(out, in_, reads, writes, eng="dve"):
        S.op(eng, lambda e: e.tensor_copy(out=out, in_=in_), reads, writes)

    def tt(out, in0, in1, op, reads, writes, eng="dve"):
        S.op(eng, lambda e: e.tensor_tensor(out=out, in0=in0, in1=in1, op=op), reads, writes)

    def ts(out, in0, s1, op0, reads, writes, s2=None, op1=None, eng="dve"):
        if op1 is None:
            S.op(eng, lambda e: e.tensor_scalar(out=out, in0=in0, scalar1=s1, scalar2=None, op0=op0), reads, writes)
        else:
            S.op(eng, lambda e: e.tensor_scalar(out=out, in0=in0, scalar1=s1, scalar2=s2, op0=op0, op1=op1),
                 reads, writes)

    def stt(out, in0, scalar, in1, op0, op1, reads, writes):
        S.op("dve", lambda e: e.scalar_tensor_tensor(out=out, in0=in0, scalar=scalar, in1=in1, op0=op0, op1=op1),
             reads, writes)

    def memset(ap, val, writes, eng="pool"):
        S.op(eng, lambda e: e.memset(ap, val), [], writes)

    def dma(eng, out, in_, reads, writes, nonc=False):
        if nonc:
            S.dma(eng, lambda e: e.dma_start(out=out, in_=in_, allow_slow_non_contiguous=True), reads, writes)
        else:
            S.dma(eng, lambda e: e.dma_start(out=out, in_=in_), reads, writes)

    X = Res("ext_in")

    import os
    SET = os.environ.get("KSET", "abcdef")
    for g in range(NG):
        if "a" in SET:
            dma("pool", wbf[g], wg[g], [X], [R_wbf[g]])
    memset(idf, 0.0, [R("idf")])
    S.op("pool", lambda e: e.affine_select(out=idf, in_=idf, pattern=[[-1, 128]], compare_op=ALU.not_equal,
                                           fill=1.0, base=0, channel_multiplier=1), [R("idf")], [R("idf")])
    vcopy(idb, idf, [R("idf")], [R("idb")])
    memset(tri, 1.0, [R("tri")])
    S.op("pool", lambda e: e.affine_select(out=tri, in_=tri, pattern=[[1, 128]], compare_op=ALU.is_ge,
                                           fill=0.0, base=0, channel_multiplier=-1), [R("tri")], [R("tri")])
    memset(ones, 1.0, [R("ones")])
    for k_ in range(NIT):
        memset(cvec[:, k_:k_ + 1], 0.25 ** (k_ + 1), [R("cvec")])
    for i_ in range(3):
        memset(ivec3[:, i_:i_ + 1], float(i_ + 1), [R("ivec3")])
    memset(vst, 1.0, [R("vst")])
    for ti in range(NBT):
        memset(vaug[ti], 1.0, [R("vaug%d" % ti)])
    if "b" in SET:
        dma("sp", bgb, bass.AP(bg_t, 0, [[0, 128], [1, 8]]), [X], [R("bgb")])
        dma("sp", lngb, bass.AP(lng_t, 0, [[0, 128], [1, D]]), [X], [R("lngb")])
        dma("sp", lnbb, bass.AP(lnb_t, 0, [[0, 128], [1, D]]), [X], [R("lnbb")])
    if "c" in SET:
        dma("sp", angT, bass.AP(ang_t, 0, [[1, 128], [128, 8]]), [X], [R("angT")], nonc=True)
    dma("sp", cfar, bass.AP(relb_t, 15 * 8, [[0, 128], [1, 8]]), [X], [R("cfar")])
    dma("sp", relb, bass.AP(relb_t, 0, [[8, 32], [1, 8]]), [X], [R("relb_s")])
    dma("sp", ohs, oh, [X], [R("ohs")])
    ts(ncfar, cfar, -1.0, ALU.mult, [R("cfar")], [R("ncfar")])
    pjt, pjr = PJ[0]
    mm(pjt[0:8, 0:384], relb, ohs, True, True, [R("relb_s"), R("ohs")], [pjr])
    vcopy(tb8, pjt[0:8, 0:384], [pjr], [R("tb8")])
    dma("sp", bass.AP(tblD_t, 0, [[384, 8], [1, 384]]), tb8, [R("tb8")], [R_tbl])
    antiI = rb[1][:, 0:128]
    _rs["antiI"] = _rs["rb1"]
    memset(antiI, 0.0, [R("antiI")])
    S.op("pool", lambda e: e.affine_select(out=antiI, in_=antiI, pattern=[[1, 128]], compare_op=ALU.not_equal,
                                           fill=1.0, base=-127, channel_multiplier=1), [R("antiI")], [R("antiI")])
    for h in range(8):
        for blk in range(2):
            k = (h * 2 + blk) % 2
            hk, hk_r = rb[2 * k], R("rb%d" % (2 * k))
            src = bass.AP(tblD_t, h * 384 + 129 - 128 * blk, [[1, 128], [1, 128]])
            dma("sp", hk[:, 0:128], src, [R_tbl], [hk_r])
            lg, lgr = LG[k]
            mm(lg[:, 0:128], hk[:, 0:128], antiI, True, True, [hk_r, R("antiI")], [lgr])
            act(G[:, h, blk, :], lg[:, 0:128], AF.Exp, [lgr, R("ncfar")], [R("G")], bias=ncfar[:, h:h + 1], scale=1.0)

    def finish():
        allr = list(_rs.values()) + [OUT, R_tbl] + R_wbf
        for e_ in ("sp",):
            S.wait_all(e_, allr)
        S.emit()
        return nc, S

    if stop == "setup":
        return finish()

    wstate = {"issued": 0, "used": 0}
    nblocks_total = NPS * (T // (128 * NBT)) + NSS
    total_uses = nblocks_total * NG

    def w_issue_upto(u):
        while wstate["issued"] <= u and wstate["issued"] < 2 * total_uses:
            i = wstate["issued"]
            g = (i // 2) % NG
            half = i % 2
            slot = i % NWH
            src = wbf[g].rearrange("(p a) c -> p (a c)", a=8)[:, half * 2048:(half + 1) * 2048]
            dma("sp", wring[slot].rearrange("p a c -> p (a c)"), src, [R_wbf[g]], [R("wr%d" % slot)])
            wstate["issued"] += 1

    def w_get():
        u = wstate["used"]
        w_issue_upto(2 * u + 1 + NWH - 2)
        wstate["used"] += 1
        out = []
        for half in range(2):
            slot = (2 * u + half) % NWH
            out.append((wring[slot], R("wr%d" % slot)))
        return out

    def transposes_to(dst, dst_r, src, src_r, nt, nchunk=8, width=128, evac=None):
        k = nxt("tp", 2)
        tp, tpr = TP[k]
        for c in range(nchunk):
            tr(tp[0:width, c, 0:nt], src[0:nt, c * width:(c + 1) * width], idb[0:nt, 0:nt],
               [src_r, R("idb")], [tpr])
        if evac is None:
            acopy(dst[0:width, 0:nchunk, 0:nt], tp[0:width, 0:nchunk, 0:nt], [tpr], [dst_r])
        else:
            evac(tp, tpr)

    def mlstm_gates(ti, nt):
        hn, hn_r = big[ti][0], R("big%d_0" % ti)
        kab, kab_r = tkb[ti][1], R("tkb%d_1" % ti)
        qaT, qaT_r = trb[ti][1], R("trb%d_1" % ti)
        kaT, kaT_r = trb[ti][2], R("trb%d_2" % ti)
        va, va_r = vaug[ti], R("vaug%d" % ti)
        sm, sm_r = smf[ti], R("smf%d" % ti)
        ex12, ex12_r = ex12s[ti], R("ex12_%d" % ti)
        tpf, tpr = TPf[nxt("tp", 2)]
        tt(gt[0:nt, 0:8], sm[0:nt, 64:72], bgb[0:nt, 0:8], ALU.add, [sm_r, R("bgb")], [R("gt")])
        act(lfn[0:nt, :], gt[0:nt, 4:8], AF.Exp, [R("gt")], [R("lfn")], scale=-1.0)
        act(lfn[0:nt, :], lfn[0:nt, :], AF.Ln, [R("lfn")], [R("lfn")], bias=1.0, scale=1.0)
        mm(tpf[0:nt, 0:4], tri[0:nt, 0:nt], lfn[0:nt, 0:4], True, True, [R("tri"), R("lfn")], [tpr])
        mm(tpf[0:128, 4:8], ones[0:nt, 0:128], lfn[0:nt, 0:4], True, True, [R("ones"), R("lfn")], [tpr])
        tt(dd[0:nt, :], gt[0:nt, 0:4], tpf[0:nt, 0:4], ALU.add, [R("gt"), tpr], [R("dd")])
        mm(tpf[0:4, 16:16 + nt], dd[0:nt, 0:4], idf[0:nt, 0:nt], True, True, [R("dd"), R("idf")], [tpr])
        S.op("dve", lambda e: e.reduce_max(out=dmx[0:4, 0:1], in_=tpf[0:4, 16:16 + nt], axis=AX.X), [tpr], [R("dmx")])
        ts(D4[0:4, 0:4], idf[0:4, 0:4], dmx[0:4, 0:1], ALU.mult, [R("idf"), R("dmx")], [R("D4")])
        mm(tpf[0:128, 160:164], ones[0:4, 0:128], D4[0:4, 0:4], True, True, [R("ones"), R("D4")], [tpr])
        tt(Mb[:, :], mrun[:, :], tpf[0:128, 160:164], ALU.max, [R("mrun"), tpr], [R("Mb")])
        tt(ex12[0:nt, 0:4], dd[0:nt, :], Mb[0:nt, :], ALU.subtract, [R("dd"), R("Mb")], [ex12_r])
        tt(ex12[:, 4:8], mrun[:, :], Mb[:, :], ALU.subtract, [R("mrun"), R("Mb")], [ex12_r])
        tt(ex12[0:nt, 8:12], tpf[0:nt, 0:4], Mb[0:nt, :], ALU.subtract, [tpr, R("Mb")], [ex12_r])
        act(ex12[:, :], ex12[:, :], AF.Exp, [ex12_r], [ex12_r])
        tt(mrun[:, :], Mb[:, :], tpf[0:128, 4:8], ALU.subtract, [R("Mb"), tpr], [R("mrun")])

    def mlstm_head(ti, nt, h, hook=None):
        def hk():
            if hook is not None:
                hook()

        hn, hn_r = big[ti][0], R("big%d_0" % ti)
        kab, kab_r = tkb[ti][1], R("tkb%d_1" % ti)
        qaT, qaT_r = trb[ti][1], R("trb%d_1" % ti)
        kaT, kaT_r = trb[ti][2], R("trb%d_2" % ti)
        va, va_r = vaug[ti], R("vaug%d" % ti)
        ex12, ex12_r = ex12s[ti], R("ex12_%d" % ti)
        dn, dn_r = dns[ti], R("dn%d" % ti)
        cst_r, csb_r = R("Cst%d" % h), R("Csb%d" % h)
        lg, lgr = PJ[nxt("pj", 2)]
        ts(Cst[:, :, h, :], Cst[:, :, h, :], ex12[:, 4 + h:5 + h], ALU.mult, [cst_r, ex12_r], [cst_r])
        hk()
        acopy(Csb[:, :, h, :], Cst[:, :, h, :], [cst_r], [csb_r])
        for c in range(2):
            mm(lg[0:nt, 0:nt], kaT[:, 2 * h + c, 0:nt], qaT[:, 2 * h + c, 0:nt], c == 0, c == 1,
               [kaT_r, qaT_r], [lgr])
        k = nxt("stm", 2)
        hk()
        stt(sTm[k][0:nt, 0:nt], lg[0:nt, 0:nt], ex12[0:nt, h:h + 1], tri[0:nt, 0:nt], ALU.mult, ALU.mult,
            [lgr, ex12_r, R("tri")], [R("sTm%d" % k)])
        nd = lg[0:nt, 128:385]
        mm(nd, sTm[k][0:nt, 0:nt], va[0:nt, h, :], True, False, [R("sTm%d" % k), va_r], [lgr])
        for c in range(2):
            mm(nd, qaT[:, 2 * h + c, 0:nt], Csb[:, c, h, :], False, c == 1, [qaT_r, csb_r], [lgr])
        hk()
        act(dn[0:nt, h:h + 1], lg[0:nt, 384:385], AF.Abs, [lgr], [dn_r])
        tt(dn[0:nt, h:h + 1], dn[0:nt, h:h + 1], ex12[0:nt, 8 + h:9 + h], ALU.max, [dn_r, ex12_r], [dn_r])
        S.op("dve", lambda e: e.reciprocal(out=dn[0:nt, h:h + 1], in_=dn[0:nt, h:h + 1]), [dn_r], [dn_r])
        hk()
        act(hn[0:nt, h * 256:(h + 1) * 256], lg[0:nt, 128:384], AF.Copy, [lgr, dn_r], [hn_r],
            scale=dn[0:nt, h:h + 1])
        act(vu[k][0:nt, :], va[0:nt, h, :], AF.Copy, [va_r, ex12_r], [R("vu%d" % k)], scale=ex12[0:nt, h:h + 1])
        hk()
        for c in range(2):
            ov, ovr = OV[c]
            mm(ov[:, 0:257], kab[0:nt, h * 256 + c * 128:h * 256 + (c + 1) * 128], vu[k][0:nt, :], True, True,
               [kab_r, R("vu%d" % k)], [ovr])
            tt(Cst[:, c, h, :], Cst[:, c, h, :], ov[:, 0:257], ALU.add, [cst_r, ovr], [cst_r])
        hk()

    def mlstm_finish(ti, nt):
        hn, hn_r = big[ti][0], R("big%d_0" % ti)
        for h in range(4):
            S.op("dve", lambda e, h=h: e.bn_stats(out=st4[0:nt, h, :], in_=hn[0:nt, h * 256:(h + 1) * 256]),
                 [hn_r], [R("st4")])
        for h in range(4):
            S.op("dve", lambda e, h=h: e.bn_aggr(out=mv4[0:nt, h, :], in_=st4[0:nt, h, :]), [R("st4")], [R("mv4")])
        act(rs4[0:nt, :], mv4[0:nt, :, 1], AF.Ln, [R("mv4")], [R("rs4")], bias=HN_EPS, scale=1.0)
        act(rs4[0:nt, :], rs4[0:nt, :], AF.Exp, [R("rs4")], [R("rs4")], scale=-0.5)
        for h in range(4):
            ts(hn[0:nt, h * 256:(h + 1) * 256], hn[0:nt, h * 256:(h + 1) * 256], mv4[0:nt, h, 0:1], ALU.subtract,
               [hn_r, R("mv4"), R("rs4")], [hn_r], s2=rs4[0:nt, h:h + 1], op1=ALU.mult)

    def mlstm_all(tiles, hook=None):
        n = len(tiles)
        seq = []
        for step in range(4 + n - 1):
            for ti in range(n):
                h = step - ti
                if 0 <= h < 4:
                    seq.append((ti, h))
        for (ti, h) in seq:
            mlstm_head(ti, tiles[ti][1], h, hook=hook)
            if h == 3:
                mlstm_finish(ti, tiles[ti][1])

    def dsa_pre_thunks(ti, j, nt, ksel):
        qiT, qiT_r = qiTb[ti], R("qiTb%d" % ti)
        sm, sm_r = smf[ti], R("smf%d" % ti)
        selT, selT_r = selTs[ti], R("selT%d" % ti)
        scb, scb_r = (sc, R("sc")) if ti == 0 else (sc2, R("sc2"))
        pos0 = j * 128
        Sk = pos0 + nt
        nkb = j + 1
        th = []

        def t_w():
            act(aw[0:nt, :], sm[0:nt, 72:80], AF.Abs, [sm_r], [R("aw")])
            ts(sg[0:nt, :], sm[0:nt, 72:80], 0.0, ALU.is_gt, [sm_r], [R("sg")], s2=2.0, op1=ALU.mult)
            ts(sg[0:nt, :], sg[0:nt, :], -1.0, ALU.add, [R("sg")], [R("sg")])
        th.append(("pre", t_w))
        for c0 in range(0, Sk, 512):
            w = min(512, Sk - c0)
            for h in range(8):
                def t_s(c0=c0, w=w, h=h):
                    lg, lgr = TPf[nxt("tp", 2)]
                    mm(lg[0:nt, 0:w], qiT[0:64, h, 0:nt], kiTs[0:64, c0:c0 + w], True, True, [qiT_r, R("kiTs")], [lgr])
                    k = nxt("rb", 3)
                    act(rb[k][0:nt, 0:w], lg[0:nt, 0:w], AF.Relu, [lgr, R("aw")], [R("rb%d" % k)],
                        scale=aw[0:nt, h:h + 1])
                    if h == 0:
                        ts(scb[0:nt, c0:c0 + w], rb[k][0:nt, 0:w], sg[0:nt, 0:1], ALU.mult, [R("rb%d" % k), R("sg")],
                           [scb_r])
                    else:
                        stt(scb[0:nt, c0:c0 + w], rb[k][0:nt, 0:w], sg[0:nt, h:h + 1], scb[0:nt, c0:c0 + w], ALU.mult,
                            ALU.add, [R("rb%d" % k), R("sg"), scb_r], [scb_r])
                th.append(("pre", t_s))
        if nt == 128:
            th.append(("pre", lambda: memset(scb[0:64, pos0 + 64:pos0 + 128], NEG, [scb_r], eng="dve")))
        if Sk > ksel:
            Sadm = pos0 + 64 if nt == 128 else Sk
            cthr = 2.0 * ksel - Sk - 0.5

            def t_init():
                S.op("dve", lambda e: e.tensor_reduce(out=bhi[0:nt, 0:1], in_=scb[0:nt, 0:Sk], axis=AX.X, op=ALU.max),
                     [scb_r], [R("bhi")])
                S.op("dve", lambda e: e.tensor_reduce(out=blo[0:nt, 0:1], in_=scb[0:nt, 0:Sadm], axis=AX.X, op=ALU.min),
                     [scb_r], [R("blo")])
                tt(bd0[0:nt, :], bhi[0:nt, :], blo[0:nt, :], ALU.subtract, [R("bhi"), R("blo")], [R("bd0")])
                ts(dks[0:nt, :], cvec[0:nt, :], bd0[0:nt, 0:1], ALU.mult, [R("cvec"), R("bd0")], [R("dks")])
                ts(ndks[0:nt, :], dks[0:nt, :], -1.0, ALU.mult, [R("dks")], [R("ndks")])
                ts(nm3[0:nt, :], ivec3[0:nt, :], ndks[0:nt, 0:1], ALU.mult, [R("ivec3"), R("ndks"), R("blo")], [R("nm3")],
                   s2=blo[0:nt, 0:1], op1=ALU.subtract)
            th.append(("fast", t_init))
            def t_thr3(k_):
                stt(thr3[0:nt, :], dks[0:nt, k_:k_ + 1], 3.0, blo[0:nt, :], ALU.mult, ALU.add,
                    [R("dks"), R("blo")], [R("thr3")])
            th.append(("fast", lambda: t_thr3(0)))
            def dve_count(thr_ap, thr_r, col):
                S.op("dve", lambda e: e.tensor_scalar(out=sel[0:nt, 0:Sk], in0=scb[0:nt, 0:Sk],
                                                      scalar1=thr_ap[0:nt, 0:1], scalar2=0.0, op0=ALU.is_ge, op1=ALU.add,
                                                      accum_out=c23[0:nt, col:col + 1]),
                     [scb_r, thr_r], [R("junkD"), R("c23")])

            def t_thr2(k_):
                stt(thr2[0:nt, :], dks[0:nt, k_:k_ + 1], 2.0, blo[0:nt, :], ALU.mult, ALU.add,
                    [R("dks"), R("blo")], [R("thr2")])

            for k_ in range(NIT):
                odd = (k_ % 2 == 1)

                def t_a():
                    act(sel[0:nt, 0:Sk], scb[0:nt, 0:Sk], AF.Sign, [scb_r, R("nm3")], [R("junkA"), R("ssum")],
                        bias=nm3[0:nt, 0:1], scale=1.0, accum_out=ssum[0:nt, 0:1])

                def t_b(odd=odd):
                    if odd:
                        dve_count(thr2, R("thr2"), 0)
                    else:
                        act(sel[0:nt, 0:Sk], scb[0:nt, 0:Sk], AF.Sign, [scb_r, R("nm3")], [R("junkA"), R("ssum")],
                            bias=nm3[0:nt, 1:2], scale=1.0, accum_out=ssum[0:nt, 1:2])
                    dve_count(thr3, R("thr3"), 1)

                def t_c(k_=k_, odd=odd):
                    if odd:
                        ts(ssum[0:nt, 1:3], c23[0:nt, 0:2], 2.0, ALU.mult, [R("c23")], [R("ssum")], s2=-float(Sk),
                           op1=ALU.add)
                    else:
                        ts(ssum[0:nt, 2:3], c23[0:nt, 1:2], 2.0, ALU.mult, [R("c23")], [R("ssum")], s2=-float(Sk),
                           op1=ALU.add)
                    S.op("dve", lambda e: e.tensor_scalar(out=p3[0:nt, :], in0=ssum[0:nt, 0:3], scalar1=cthr, scalar2=0.0,
                                                          op0=ALU.is_ge, op1=ALU.add, accum_out=pc[0:nt, 0:1]),
                         [R("ssum")], [R("p3"), R("pc")])
                    stt(blo[0:nt, :], pc[0:nt, :], dks[0:nt, k_:k_ + 1], blo[0:nt, :], ALU.mult, ALU.add,
                        [R("pc"), R("dks"), R("blo")], [R("blo")])
                    if k_ + 1 < NIT:
                        ts(nm3[0:nt, :], ivec3[0:nt, :], ndks[0:nt, k_ + 1:k_ + 2], ALU.mult,
                           [R("ivec3"), R("ndks"), R("blo")], [R("nm3")], s2=blo[0:nt, 0:1], op1=ALU.subtract)
                        t_thr3(k_ + 1)
                        if (k_ + 1) % 2 == 1:
                            t_thr2(k_ + 1)
                th.append(("chain", t_a))
                th.append(("chain", t_b))
                th.append(("chainend", t_c))
            th.append(("fast", lambda: ts(sel[0:nt, 0:Sk], scb[0:nt, 0:Sk], blo[0:nt, 0:1], ALU.is_ge,
                                          [scb_r, R("blo")], [R("sel"), R("junkA"), R("junkD")])))
        else:
            th.append(("fast", lambda: ts(sel[0:nt, 0:Sk], scb[0:nt, 0:Sk], -1.0e29, ALU.is_gt, [scb_r],
                                          [R("sel"), R("junkA"), R("junkD")])))
        for k0 in range(0, nkb, 8):
            def t_T(k0=k0):
                kn = min(8, nkb - k0)
                tp, tpr = TP[nxt("tp", 2)]
                full = 0
                for q in range(kn):
                    kb = k0 + q
                    wk = 128 if kb < nkb - 1 else nt
                    tr(tp[0:wk, q, 0:nt], sel[0:nt, kb * 128:kb * 128 + wk], idb[0:nt, 0:nt],
                       [R("sel"), R("junkA"), R("junkD"), R("idb")], [tpr])
                    if wk == 128:
                        full += 1
                if full:
                    acopy(selT[:, k0:k0 + full, 0:nt], tp[:, 0:full, 0:nt], [tpr], [selT_r])
                if full < kn:
                    acopy(selT[0:nt, k0 + full, 0:nt], tp[0:nt, full, 0:nt], [tpr], [selT_r])
            th.append(("fast", t_T))
        return th

    def dsa_attn(ti, j, nt, mask_eng="pool", between=None):
        ob, ob_r = big[ti][0], R("big%d_0" % ti)
        qbT, qbT_r = trb[ti][1], R("trb%d_1" % ti)
        selT, selT_r = selTs[ti], R("selT%d" % ti)
        nkb = j + 1
        groups = []
        nfull = nkb if nt == 128 else nkb - 1
        for k0 in range(0, nfull, 4):
            groups.append((k0, min(4, nfull - k0), 128))
        if nfull < nkb:
            groups.append((nfull, 1, nt))
        scale = 128.0 ** -0.5
        units = [(h, gi) for h in range(8) for gi in range(len(groups))]
        lgs = {}

        def qk(u):
            h, gi = units[u]
            k0, nq, wk = groups[gi]
            lg, lgr = LG[nxt("lg", 2)]
            lg3 = lg.rearrange("p (q t) -> p q t", t=128)
            for q in range(nq):
                kb = k0 + q
                mm(lg3[0:wk, q, 0:nt], kTs[:, h, kb * 128:kb * 128 + wk], qbT[:, h, 0:nt], True, True,
                   [R("kTs"), qbT_r], [lgr])
            lgs[u] = (lg3, lgr)

        qk(0)
        for u, (h, gi) in enumerate(units):
            k0, nq, wk = groups[gi]
            ov, ovr = OV[h % 2]
            if u + 1 < len(units):
                qk(u + 1)
            lg3, lgr = lgs.pop(u)
            pi = nxt("pt", 3)
            pt, ptr = PT[pi], R("PT%d" % pi)
            act(pt[0:wk, 0:nq, 0:nt], lg3[0:wk, 0:nq, 0:nt], AF.Exp, [lgr, R("cfar")], [ptr],
                bias=cfar[0:wk, h:h + 1], scale=scale)
            meng = mask_eng if mask_eng != "mix" else ("pool" if u % 2 == 0 else "dve")
            tt(pt[0:wk, 0:nq, 0:nt], pt[0:wk, 0:nq, 0:nt], selT[0:wk, k0:k0 + nq, 0:nt], ALU.mult,
               [ptr, selT_r], [ptr], eng=meng)
            for q in range(nq):
                kb = k0 + q
                blk = j - kb
                if blk in (0, 1):
                    tt(pt[0:wk, q, 0:nt], pt[0:wk, q, 0:nt], G[0:wk, h, blk, 0:nt], ALU.mult, [ptr, R("G")], [ptr],
                       eng=meng)
            for q in range(nq):
                kb = k0 + q
                mm(ov[0:nt, 0:129], pt[0:wk, q, 0:nt], vst[0:wk, kb, h, :], kb == 0, kb == nkb - 1,
                   [ptr, R("vst")], [ovr])
            if gi == len(groups) - 1:
                S.op("dve", lambda e, h=h, ov=ov: e.reciprocal(out=rsum[0:nt, h:h + 1], in_=ov[0:nt, 128:129]),
                     [ovr], [R("rsum")])
                act(ob[0:nt, h * 128:(h + 1) * 128], ov[0:nt, 0:128], AF.Copy, [ovr, R("rsum")], [ob_r],
                    scale=rsum[0:nt, h:h + 1])
            if between is not None:
                between()

    def store_kv_tile(kind, b, ti, j, nt):
        pass

    def xrows(kind, b, j, nt):
        if kind == "p":
            return xp[b, j * 128:j * 128 + nt, :]
        return xs[b, 0:nt, :]

    def x_load(kind, b, tiles):
        for ti, (j, nt) in enumerate(tiles):
            dma("pool", xbn[ti][0:nt, :], xrows(kind, b, j, nt), [X], [R("xbn%d" % ti)])

    def x_transpose(ti, nt):
        transposes_to(trb[ti][0], R("trb%d_0" % ti), xbn[ti], R("xbn%d" % ti), nt)

    def block(kind, b, tiles, have_x=False, nxt_blk=None):
        cnt["blk"] += 1
        if kind == "p":
            xsrc, ydst, kdst, vdst, ikdst, ksel = xp, yp, kp, vp, ikp, KSEL_P
        else:
            xsrc, ydst, kdst, vdst, ikdst, ksel = xs, ysd, ksd, vsd, iksd, KSEL_S

        def rows(ap, j, nt):
            if kind == "p":
                return ap[b, j * 128:j * 128 + nt, :]
            return ap[b, 0:nt, :]

        if not have_x:
            x_load(kind, b, tiles)
            for ti, (j, nt) in enumerate(tiles):
                x_transpose(ti, nt)
        bgs = [[] for _ in tiles]
        g_attn = GROUPS.index(("qb", 1))

        def advance(g):
            lists = [L for L in bgs if L]
            if not lists:
                return
            P = lists[0]
            n = 0
            chain = False
            while P and n < 9:
                kind_, f_ = P.pop(0)
                f_()
                n += 1
                if kind_ == "chainend":
                    chain = True
                    break
            if chain and len(lists) > 1:
                Q = lists[1]
                m = 0
                while Q and m < 4 and Q[0][0] == "pre":
                    Q.pop(0)[1]()
                    m += 1

        def drain_tile(ti):
            for t_ in range(ti + 1):
                while bgs[t_]:
                    bgs[t_].pop(0)[1]()

        fine = {"t": 0}

        def pop_fine():
            lists = [L for L in bgs if L]
            if not lists:
                return
            fine["t"] ^= 1
            if fine["t"] and len(lists) > 1 and lists[1][0][0] == "pre" and lists[0][0][0] in ("chain", "chainend"):
                lists[1].pop(0)[1]()
            else:
                lists[0].pop(0)[1]()

        def pop_one():
            for L in bgs:
                if L:
                    L.pop(0)[1]()
                    return

        for g, (gk, hf) in enumerate(GROUPS):
            if stop is not None and stop.startswith("g") and int(stop[1:]) == g:
                raise _Stop()
            if stop is not None and stop.startswith("b"):
                bi_, gi_ = stop[1:].split("g")
                if int(bi_) == cnt["blk"] and int(gi_) == g:
                    raise _Stop()
            wts = w_get()
            ncol = 80 if gk == "small" else 512
            pss = []
            for ti, (j, nt) in enumerate(tiles):
                pss.append(PJ[nxt("pj", 2)])
            for half in range(2):
                wt, wt_r = wts[half]
                for ti, (j, nt) in enumerate(tiles):
                    if gk in ("woa", "wob", "wo"):
                        lt, lt_r = trb[ti][3], R("trb%d_3" % ti)
                    else:
                        lt, lt_r = trb[ti][0], R("trb%d_0" % ti)
                    ps, psr = pss[ti]
                    for d4 in range(4):
                        dc = half * 4 + d4
                        mm(ps[0:nt, 0:ncol], lt[:, dc, 0:nt], wt[:, d4, 0:ncol], dc == 0, dc == 7, [lt_r, wt_r], [psr])
            for ti, (j, nt) in enumerate(tiles):
                hn, hn_r = big[ti][0], R("big%d_0" % ti)
                mx, mx_r = big[ti][1], R("big%d_1" % ti)
                ba, ba_r = tkb[ti][0], R("tkb%d_0" % ti)
                kab, kab_r = tkb[ti][1], R("tkb%d_1" % ti)
                xT, xT_r = trb[ti][0], R("trb%d_0" % ti)
                aT, aT_r = trb[ti][3], R("trb%d_3" % ti)
                pos0 = j * 128
                ps, psr = pss[ti]
                cs = slice(hf * 512, (hf + 1) * 512)
                p_ = ps[0:nt, 0:512]
                if gk == "kb":
                    acopy(hn[0:nt, cs], p_, [psr], [hn_r])
                    vcopy(ba[0:nt, cs], p_, [psr], [ba_r])
                    if hf == 1:
                        dma("sp", rows(kdst, j, nt), hn[0:nt, :], [hn_r], [OUT])
                        k = nxt("tp", 2)
                        tp, tpr = TP[k]
                        for c in range(8):
                            tr(tp[:, c, 0:nt], ba[0:nt, c * 128:(c + 1) * 128], idb[0:nt, 0:nt], [ba_r, R("idb")], [tpr])
                        acopy(kTs[:, :, pos0:pos0 + nt], tp[:, :, 0:nt], [tpr], [R("kTs")])
                elif gk == "vb":
                    acopy(mx[0:nt, cs], p_, [psr], [mx_r])
                    vcopy(vst[0:nt, j, hf * 4:(hf + 1) * 4, 0:128], p_.rearrange("p (h d) -> p h d", d=128),
                          [psr], [R("vst")])
                    if hf == 1:
                        dma("sp", rows(vdst, j, nt), mx[0:nt, :], [mx_r], [OUT])
                elif gk == "small":
                    acopy(smf[ti][0:nt, :], ps[0:nt, 0:80], [psr], [R("smf%d" % ti)])
                    dma("sp", rows(ikdst, j, nt), smf[ti][0:nt, 0:64], [R("smf%d" % ti)], [OUT])
                    acopy(kib[ti][0:nt, :], smf[ti][0:nt, 0:64], [R("smf%d" % ti)], [R("kib%d" % ti)])
                    tp, tpr = TP[nxt("tp", 2)]
                    tr(tp[0:64, 0, 0:nt], kib[ti][0:nt, 0:64], idb[0:nt, 0:nt], [R("kib%d" % ti), R("idb")], [tpr])
                    acopy(kiTs[0:64, pos0:pos0 + nt], tp[0:64, 0, 0:nt], [tpr], [R("kiTs")])
                    mlstm_gates(ti, nt)
                elif gk in ("qa", "qb"):
                    qi_ = nxt("qt", 2)
                    vcopy(qtmp[qi_][0:nt, :], p_, [psr], [R("qtmp%d" % qi_)])
                    dstT, dstT_r = trb[ti][1], R("trb%d_1" % ti)
                    tp, tpr = TP[nxt("tp", 2)]
                    for c in range(4):
                        tr(tp[:, c, 0:nt], qtmp[qi_][0:nt, c * 128:(c + 1) * 128], idb[0:nt, 0:nt],
                           [R("qtmp%d" % qi_), R("idb")], [tpr])
                    acopy(dstT[:, hf * 4:(hf + 1) * 4, 0:nt], tp[:, 0:4, 0:nt], [tpr], [dstT_r])
                    if gk == "qb" and hf == 1:
                        drain_tile(ti)
                        last = ti == len(tiles) - 1

                        def between():
                            pop_one()
                        dsa_attn(ti, j, nt, mask_eng="dve", between=None if last else between)
                elif gk == "ka":
                    act(kab[0:nt, cs], p_, AF.Copy, [psr], [kab_r], scale=1.0 / 16.0)
                    if hf == 1:
                        transposes_to(trb[ti][2], R("trb%d_2" % ti), kab, kab_r, nt)
                elif gk == "va":
                    acopy(vaug[ti][0:nt, hf * 2:(hf + 1) * 2, 0:256], p_.rearrange("p (h d) -> p h d", d=256),
                          [psr], [R("vaug%d" % ti)])
                    if hf == 1 and ti == len(tiles) - 1:
                        mlstm_all(tiles, hook=pop_fine)
                elif gk == "za":
                    k = nxt("rb", 3)
                    act(tmpA[k][0:nt, :], p_, AF.Silu, [psr], [R("tmpA%d" % k)])
                    tt(hn[0:nt, cs], hn[0:nt, cs], tmpA[k][0:nt, :], ALU.mult, [hn_r, R("tmpA%d" % k)], [hn_r], eng="pool")
                elif gk == "oa":
                    k = nxt("rb", 3)
                    act(tmpA[k][0:nt, :], p_, AF.Sigmoid, [psr], [R("tmpA%d" % k)])
                    tt(ba[0:nt, cs], hn[0:nt, cs], tmpA[k][0:nt, :], ALU.mult, [hn_r, R("tmpA%d" % k)], [ba_r],
                       eng="pool")
                    if hf == 1:
                        def ev(tp, tpr, nt=nt, aT=aT, aT_r=aT_r):
                            for c in range(8):
                                act(aT[:, c, 0:nt], tp[:, c, 0:nt], AF.Copy, [tpr, R("angT")], [aT_r],
                                    scale=angT[:, c:c + 1])
                        transposes_to(aT, aT_r, ba, ba_r, nt, evac=ev)
                elif gk == "woa":
                    vcopy(mx[0:nt, cs], p_, [psr], [mx_r])
                elif gk == "ga":
                    k = nxt("rb", 3)
                    act(tmpA[k][0:nt, :], p_, AF.Sigmoid, [psr], [R("tmpA%d" % k)])
                    tt(mx[0:nt, cs], mx[0:nt, cs], tmpA[k][0:nt, :], ALU.mult, [mx_r, R("tmpA%d" % k)], [mx_r], eng="pool")
                elif gk == "qi":
                    qi_ = nxt("qt", 2)
                    acopy(qtmp[qi_][0:nt, :], p_, [psr], [R("qtmp%d" % qi_)])
                    dstT, dstT_r = qiTb[ti], R("qiTb%d" % ti)
                    tp, tpr = TP[nxt("tp", 2)]
                    for c in range(8):
                        tr(tp[0:64, c, 0:nt], qtmp[qi_][0:nt, c * 64:(c + 1) * 64], idb[0:nt, 0:nt],
                           [R("qtmp%d" % qi_), R("idb")], [tpr])
                    acopy(dstT[0:64, :, 0:nt], tp[0:64, :, 0:nt], [tpr], [dstT_r])
                    bgs[ti].extend(dsa_pre_thunks(ti, j, nt, ksel))
                elif gk == "zb":
                    k = nxt("rb", 3)
                    act(tmpA[k][0:nt, :], p_, AF.Silu, [psr], [R("tmpA%d" % k)])
                    tt(ba[0:nt, cs], hn[0:nt, cs], tmpA[k][0:nt, :], ALU.mult, [hn_r, R("tmpA%d" % k)], [ba_r],
                       eng="pool" if gk == "za" else "dve")
                    if hf == 1:
                        transposes_to(aT, aT_r, ba, ba_r, nt)
                elif gk == "wob":
                    vcopy(hn[0:nt, cs], p_, [psr], [hn_r])
                elif gk == "gb":
                    k = nxt("rb", 3)
                    act(tmpA[k][0:nt, :], p_, AF.Sigmoid, [psr], [R("tmpA%d" % k)])
                    tt(tmpA[k][0:nt, :], tmpA[k][0:nt, :], hn[0:nt, cs], ALU.mult, [hn_r, R("tmpA%d" % k)],
                       [R("tmpA%d" % k)])
                    tt(mx[0:nt, cs], mx[0:nt, cs], tmpA[k][0:nt, :], ALU.add, [mx_r, R("tmpA%d" % k)], [mx_r])
                    if hf == 1:
                        acopy(ba[0:nt, :], mx[0:nt, :], [mx_r], [ba_r])
                        transposes_to(aT, aT_r, ba, ba_r, nt)
                        dma("sp", hn[0:nt, :], rows(xsrc, j, nt), [X], [hn_r])
                        if nxt_blk is not None and ti < len(nxt_blk[2]):
                            jn_, ntn_ = nxt_blk[2][ti]
                            dma("pool", xbn[ti][0:ntn_, :], xrows(nxt_blk[0], nxt_blk[1], jn_, ntn_), [X],
                                [R("xbn%d" % ti)])
                elif gk == "wo":
                    stt(mx[0:nt, cs], hn[0:nt, cs], ALPHA, p_, ALU.mult, ALU.add, [hn_r, psr], [mx_r])
                    if hf == 1:
                        for c in range(2):
                            S.op("dve", lambda e, c=c, mx=mx, nt=nt: e.bn_stats(out=lst[0:nt, c, :],
                                                                                in_=mx[0:nt, c * 512:(c + 1) * 512]),
                                 [mx_r], [R("lst")])
                        S.op("dve", lambda e, nt=nt: e.bn_aggr(out=lmv[0:nt, :], in_=lst[0:nt, :, :]), [R("lst")], [R("lmv")])
                        act(lrs[0:nt, :], lmv[0:nt, 1:2], AF.Ln, [R("lmv")], [R("lrs")], bias=LN_EPS, scale=1.0)
                        act(lrs[0:nt, :], lrs[0:nt, :], AF.Exp, [R("lrs")], [R("lrs")], scale=-0.5)
                        stt(lnb1[0:nt, :], lmv[0:nt, 0:1], -1.0, lrs[0:nt, :], ALU.mult, ALU.mult,
                            [R("lmv"), R("lrs")], [R("lnb1")])
                        act(mx[0:nt, :], mx[0:nt, :], AF.Identity, [mx_r, R("lrs"), R("lnb1")], [mx_r],
                            bias=lnb1[0:nt, 0:1], scale=lrs[0:nt, 0:1])
                        tt(mx[0:nt, :], mx[0:nt, :], lngb[0:nt, :], ALU.mult, [mx_r, R("lngb")], [mx_r])
                        tt(mx[0:nt, :], mx[0:nt, :], lnbb[0:nt, :], ALU.add, [mx_r, R("lnbb")], [mx_r])
                        dma("sp", rows(ydst, j, nt), mx[0:nt, :], [mx_r], [OUT])
                        if nxt_blk is not None and ti < len(nxt_blk[2]):
                            x_transpose(ti, nxt_blk[2][ti][1])
                if g < g_attn:
                    advance(g)

    def state_out(C_t, n_t, m_t, b):
        for h in range(4):
            for c in range(2):
                dst = bass.AP(C_t, ((b * 4 + h) * 256 + c * 128) * 256, [[256, 128], [1, 256]])
                dma("sp", dst, Cst[:, c, h, 0:256], Rc_all, [OUT])
        for c in range(2):
            dstn = bass.AP(n_t, b * 1024 + c * 128, [[1, 128], [256, 4]])
            dma("sp", dstn, Cst[:, c, :, 256], Rc_all, [OUT], nonc=True)
        dstm = bass.AP(m_t, b * 4, [[4, 1], [1, 4]])
        dma("sp", dstm, mrun[0:1, 0:4], [R("mrun")], [OUT])

    def main_body():
        plan = []
        nbp = T // (128 * NBT)
        for b in range(NPS):
            for bi in range(nbp):
                plan.append(("p", b, [(bi * NBT + i, 128) for i in range(NBT)], bi == 0, bi == nbp - 1))
        nb = PAST // 128
        for b in range(NSS):
            plan.append(("s", b, [(nb, DS)], True, True))
        for i, (kind, b, tiles, first, last) in enumerate(plan):
            if first and kind == "p":
                memset(Cst, 0.0, Rc_all)
                memset(mrun, 0.0, [R("mrun")])
            if first and kind == "s":
                for h in range(4):
                    for c in range(2):
                        src = bass.AP(sC_t, ((b * 4 + h) * 256 + c * 128) * 256, [[256, 128], [1, 256]])
                        dma("sp", Cst[:, c, h, 0:256], src, [X], Rc_all)
                for c in range(2):
                    srcn = bass.AP(sn_t, b * 1024 + c * 128, [[1, 128], [256, 4]])
                    dma("sp", Cst[:, c, :, 256], srcn, [X], Rc_all, nonc=True)
                dma("sp", mrun, bass.AP(sm_t, b * 4, [[0, 128], [1, 4]]), [X], [R("mrun")])
                kt, kt_r = tkb[0][0], R("tkb0_0")
                for blk in range(nb):
                    dma("pool", kt, ck[b, blk * 128:(blk + 1) * 128, :], [X], [kt_r])
                    tp, tpr = TP[nxt("tp", 2)]
                    for c in range(8):
                        tr(tp[:, c, :], kt[:, c * 128:(c + 1) * 128], idb, [kt_r, R("idb")], [tpr])
                    acopy(kTs[:, :, blk * 128:(blk + 1) * 128], tp[:, :, :], [tpr], [R("kTs")])
                    dma("pool", vst[:, blk, :, 0:128],
                        cv[b, blk * 128:(blk + 1) * 128, :].rearrange("p (h d) -> p h d", d=128), [X], [R("vst")])
                kit, kit_r = tkb[0][1], R("tkb0_1")
                kit3 = kit[:, 0:nb * 64].rearrange("p (n c) -> p n c", c=64)
                dma("pool", kit3, cik[b].rearrange("(n p) c -> p n c", p=128), [X], [kit_r])
                for k0 in range(0, nb, 8):
                    kn = min(8, nb - k0)
                    tp, tpr = TP[nxt("tp", 2)]
                    for q in range(kn):
                        tr(tp[0:64, q, :], kit3[:, k0 + q, :], idb, [kit_r, R("idb")], [tpr])
                    acopy(kiTs[0:64, k0 * 128:(k0 + kn) * 128].rearrange("p (n t) -> p n t", t=128), tp[0:64, 0:kn, :],
                          [tpr], [R("kiTs")])
            nb_ = plan[i + 1] if i + 1 < len(plan) else None
            block(kind, b, tiles, have_x=(i > 0), nxt_blk=nb_)
            if last:
                if kind == "p":
                    state_out(Cp_t, np_t, mp_t, b)
                else:
                    state_out(Cs_t, ns_t, ms_t, b)

    try:
        main_body()
    except _Stop:
        return finish()
    S.wait_all("sp", [OUT])
    S.emit()
    return nc, S


def _run(inputs, NPS, T, NSS, PAST, ncores, stop=None):
    nc, S = build(NPS, T, NSS, PAST, stop=stop)
    wgh = _host_weight_groups(np.asarray(inputs["w_in"], np.float32), np.asarray(inputs["w_out_a"], np.float32),
                              np.asarray(inputs["w_out_b"], np.float32), np.asarray(inputs["w_o"], np.float32))
    wgh = np.ascontiguousarray(wgh.reshape(NG, 1024, 512))
    oh = _bucket_onehot()
    in_maps = []
    f = lambda a: np.ascontiguousarray(np.asarray(a, np.float32))
    for c in range(ncores):
        ps = slice(c * NPS, (c + 1) * NPS)
        ss = slice(c * NSS, (c + 1) * NSS)
        in_maps.append({
            "xp": f(inputs["x_prompt"][ps]),
            "xs": f(inputs["x_sample"][ss]),
            "ck": f(inputs["cache_k"][ss]).reshape(NSS, PAST, D),
            "cv": f(inputs["cache_v"][ss]).reshape(NSS, PAST, D),
            "cik": f(inputs["cache_idx_k"][ss]),
            "sC": f(inputs["state_C"][ss]),
            "sn": f(inputs["state_n"][ss]),
            "sm": f(inputs["state_m"][ss]),
            "wg": wgh,
            "bg": f(inputs["b_gates"]),
            "ang": f(inputs["a_norm_g"]),
            "relb": f(inputs["rel_bias"]),
            "lng": f(inputs["ln_g"]),
            "lnb": f(inputs["ln_b"]),
            "oh": oh,
        })
    res = run_bass_kernel_spmd(nc, in_maps, core_ids=list(range(ncores)))
    rr = res.results
    cat = lambda k: np.concatenate([np.asarray(r[k], np.float32) for r in rr], axis=0)
    y_p = cat("yp")
    y_s = cat("ys")
    k_p = cat("kp").reshape(-1, T, 8, 128)
    v_p = cat("vp").reshape(-1, T, 8, 128)
    ik_p = cat("ikp")
    C_p, n_p, m_p = cat("Cp"), cat("np"), cat("mp")
    k_s = cat("ks").reshape(-1, DS, 8, 128)
    v_s = cat("vs").reshape(-1, DS, 8, 128)
    ik_s = cat("iks")
    C_s, n_s, m_s = cat("Cs"), cat("ns"), cat("ms")
    return (y_p, y_s, k_p, v_p, ik_p, C_p, n_p, m_p, k_s, v_s, ik_s, C_s, n_s, m_s)


def kernel(x_prompt, x_sample, cache_k, cache_v, cache_idx_k, state_C, state_n, state_m,
           w_in, b_gates, a_norm_g, w_out_a, w_out_b, w_o, rel_bias, ln_g, ln_b):
    inputs = dict(x_prompt=x_prompt, x_sample=x_sample, cache_k=cache_k, cache_v=cache_v, cache_idx_k=cache_idx_k,
                  state_C=state_C, state_n=state_n, state_m=state_m, w_in=w_in, b_gates=b_gates, a_norm_g=a_norm_g,
                  w_out_a=w_out_a, w_out_b=w_out_b, w_o=w_o, rel_bias=rel_bias, ln_g=ln_g, ln_b=ln_b)
    B, T = x_prompt.shape[0], x_prompt.shape[1]
    BS, PAST = x_sample.shape[0], cache_k.shape[1]
    return _run(inputs, B // NCORES, T, BS // NCORES, PAST, NCORES)
```

```python
import math
import numpy as np
import concourse.bass as bass
import concourse.mybir as mybir
from concourse.bass_utils import run_bass_kernel_spmd

F32 = mybir.dt.float32
BF16 = mybir.dt.bfloat16
AF = mybir.ActivationFunctionType
ALU = mybir.AluOpType
AX = mybir.AxisListType

D = 1024
NCORES = 8
DS = 32
NEG = -2.0e30
REPL = -1.0e30
LN_EPS = 1e-5
HN_EPS = 1e-6
ALPHA = 2.0 ** 0.25
NW = 2
NWH = 4
NBT = 2

_off = {}
_c = 0
for _n, _s in (("qa", 1024), ("ka", 1024), ("va", 1024), ("oa", 1024), ("za", 1024), ("ia", 4), ("fa", 4),
               ("qb", 1024), ("kb", 1024), ("vb", 1024), ("zb", 1024), ("qi", 512), ("ki", 64), ("wi", 8),
               ("ga", 1024), ("gb", 1024)):
    _off[_n] = _c
    _c += _s
IN_COLS = _c
GROUPS = []
for _k in ("kb", "vb"):
    GROUPS += [(_k, 0), (_k, 1)]
GROUPS += [("small", 0), ("qi", 0)]
for _k in ("qa", "ka", "va", "oa", "za", "woa", "ga", "qb", "zb", "wob", "gb", "wo"):
    GROUPS += [(_k, 0), (_k, 1)]
NG = len(GROUPS)


def _host_weight_groups(w_in, w_out_a, w_out_b, w_o):
    out = np.zeros((NG, 128, 8, 512), np.float32)
    for g, (kind, hf) in enumerate(GROUPS):
        if kind == "small":
            cols = np.zeros((1024, 512), np.float32)
            cols[:, 0:64] = w_in[:, _off["ki"]:_off["ki"] + 64]
            cols[:, 64:68] = w_in[:, _off["ia"]:_off["ia"] + 4]
            cols[:, 68:72] = w_in[:, _off["fa"]:_off["fa"] + 4]
            cols[:, 72:80] = w_in[:, _off["wi"]:_off["wi"] + 8]
        elif kind == "woa":
            cols = w_out_a[:, hf * 512:(hf + 1) * 512]
        elif kind == "wob":
            cols = w_out_b[:, hf * 512:(hf + 1) * 512]
        elif kind == "wo":
            cols = w_o[:, hf * 512:(hf + 1) * 512]
        elif kind == "qi":
            cols = w_in[:, _off["qi"]:_off["qi"] + 512]
        else:
            cols = w_in[:, _off[kind] + hf * 512:_off[kind] + (hf + 1) * 512]
        out[g] = cols.reshape(8, 128, 512).transpose(1, 0, 2)
    return out


def _bucket_onehot():
    import jax
    import jax.numpy as jnp
    n_buckets, max_distance = 32, 128
    cpu = jax.devices("cpu")[0]
    with jax.default_device(cpu):
        rel = jnp.arange(384, dtype=jnp.int32) - 256
        half = n_buckets // 2
        max_exact = half // 2
        ret = jnp.where(rel > 0, half, 0)
        n = jnp.abs(rel)
        nf = jnp.maximum(n, 1).astype(jnp.float32)
        large = max_exact + (jnp.log(nf / max_exact) / math.log(max_distance / max_exact)
                             * (half - max_exact)).astype(jnp.int32)
        large = jnp.minimum(large, half - 1)
        bk = np.asarray(ret + jnp.where(n < max_exact, n, large))
    oh = np.zeros((32, 384), np.float32)
    oh[bk, np.arange(384)] = 1.0
    return oh


class Res:
    __slots__ = ("name", "w", "r")

    def __init__(self, name):
        self.name = name
        self.w = None
        self.r = {}


EPOCH = 30000
NDMASEM = 8


class Sched:
    ENGS = ("pe", "act", "dve", "pool", "sp")

    def __init__(self, nc):
        self.nc = nc
        self.ops = {e: [] for e in self.ENGS}
        self.cnt = {e: 0 for e in self.ENGS}
        self.epoch = {e: 0 for e in self.ENGS}
        self.sems = {}
        self.seen = {e: {} for e in self.ENGS}
        self.dma_i = {e: 0 for e in self.ENGS}
        self.ninstr = 0
        self.final = {}

    def _sem(self, key):
        if key not in self.sems:
            self.sems[key] = self.nc.alloc_semaphore(name="s_%s_%s" % key)
        return self.sems[key]

    def _deps(self, eng, reads, writes):
        toks = []
        for r in reads:
            if r.w is not None:
                toks.append(r.w)
        for w in writes:
            if w.w is not None:
                toks.append(w.w)
            toks.extend(w.r.items())
        waits = {}
        for (key, val) in toks:
            if eng == "pe" and key[0] == "pe":
                continue
            if self.seen[eng].get(key, 0) >= val:
                continue
            if waits.get(key, 0) < val:
                waits[key] = val
        for key, val in waits.items():
            self.seen[eng][key] = val
        return [(self._sem(k), v) for k, v in waits.items()]

    def _commit(self, tok, reads, writes):
        for w in writes:
            w.w = tok
            w.r = {}
        for r in reads:
            if r not in writes:
                if r.r.get(tok[0], 0) < tok[1]:
                    r.r[tok[0]] = tok[1]

    def op(self, eng, fn, reads=(), writes=()):
        pbs = [r for r in reads if r.name.startswith("pb") and r not in writes]
        if pbs:
            writes = list(writes) + pbs
        waits = self._deps(eng, reads, writes)
        if self.cnt[eng] >= EPOCH:
            self.epoch[eng] += 1
            self.cnt[eng] = 0
        self.cnt[eng] += 1
        key = (eng, self.epoch[eng])
        tok = (key, self.cnt[eng])
        self.ops[eng].append((waits, fn, self._sem(key), 1))
        self.final[key] = tok[1]
        self._commit(tok, reads, writes)
        self.ninstr += 1 + len(waits)
        return tok

    def dma(self, eng, fn, reads=(), writes=()):
        i = self.dma_i[eng]
        self.dma_i[eng] += 1
        nsem = 3 if eng == "pool" else NDMASEM
        slot = i % nsem
        gen = i // nsem
        key = ("dma_" + eng, slot)
        waits = self._deps(eng, reads, writes)
        sem = self._sem(key)
        if gen > 0 and self.seen[eng].get(key, 0) < 16 * gen:
            waits.append((sem, 16 * gen))
            self.seen[eng][key] = 16 * gen
        tok = (key, 16 * (gen + 1))
        self.ops[eng].append((waits, fn, sem, 16))
        self.final[key] = tok[1]
        self._commit(tok, reads, writes)
        self.ninstr += 1 + len(waits)
        return tok

    def wait_all(self, eng, resources=()):
        waits = []
        for key, val in self.final.items():
            if self.seen[eng].get(key, 0) < val:
                waits.append((self._sem(key), val))
                self.seen[eng][key] = val
        self.ops[eng].append((waits, None, None, 0))

    def emit(self):
        nc = self.nc
        ops = self.ops
        with nc.Block() as block:
            def run(e, lst):
                for waits, fn, sem, inc in lst:
                    for s, v in waits:
                        e.wait_ge(s, v)
                    if fn is not None:
                        fn(e).then_inc(sem, inc)

            @block.tensor
            def _(e):
                run(e, ops["pe"])

            @block.scalar
            def _(e):
                run(e, ops["act"])

            @block.vector
            def _(e):
                run(e, ops["dve"])

            @block.gpsimd
            def _(e):
                run(e, ops["pool"])

            @block.sync
            def _(e):
                run(e, ops["sp"])


class _Stop(Exception):
    pass


def build(NPS, T, NSS, PAST, debug=False, stop=None):
    assert T % 256 == 0 and PAST % 128 == 0
    nc = bass.Bass("TRN2", target_bir_lowering=False)
    S = Sched(nc)
    KSEL_P = min(256, T // 4)
    KSEL_S = min(256, (PAST + DS) // 4)
    assert KSEL_P % 8 == 0 and KSEL_S % 8 == 0
    SK = max(T, PAST + 128)
    NKB = SK // 128

    def din(name, shape, dt=F32):
        return nc.dram_tensor(name, list(shape), dt, kind="ExternalInput")

    def dout(name, shape, dt=F32):
        return nc.dram_tensor(name, list(shape), dt, kind="ExternalOutput")

    xp = din("xp", [NPS, T, D]).ap()
    xs = din("xs", [NSS, DS, D]).ap()
    ck = din("ck", [NSS, PAST, D]).ap()
    cv = din("cv", [NSS, PAST, D]).ap()
    cik = din("cik", [NSS, PAST, 64]).ap()
    sC_t = din("sC", [NSS, 4, 256, 256])
    sn_t = din("sn", [NSS, 4, 256])
    sm_t = din("sm", [NSS, 4])
    wg = din("wg", [NG, 1024, 512]).ap()
    bg_t = din("bg", [8])
    ang_t = din("ang", [D])
    relb_t = din("relb", [32, 8])
    lng_t = din("lng", [D])
    lnb_t = din("lnb", [D])
    oh = din("oh", [32, 384]).ap()

    yp = dout("yp", [NPS, T, D]).ap()
    ysd = dout("ys", [NSS, DS, D]).ap()
    kp = dout("kp", [NPS, T, D]).ap()
    vp = dout("vp", [NPS, T, D]).ap()
    ikp = dout("ikp", [NPS, T, 64]).ap()
    Cp_t = dout("Cp", [NPS, 4, 256, 256])
    np_t = dout("np", [NPS, 4, 256])
    mp_t = dout("mp", [NPS, 4])
    ksd = dout("ks", [NSS, DS, D]).ap()
    vsd = dout("vs", [NSS, DS, D]).ap()
    iksd = dout("iks", [NSS, DS, 64]).ap()
    Cs_t = dout("Cs", [NSS, 4, 256, 256])
    ns_t = dout("ns", [NSS, 4, 256])
    ms_t = dout("ms", [NSS, 4])

    wbf = nc.dram_tensor("wbf", [NG, 1024, 512], BF16, kind="Internal").ap()
    tblD_t = nc.dram_tensor("tblD", [8, 384], F32, kind="Internal")

    OUT = Res("outputs")
    R_wbf = [Res("wbf%d" % g) for g in range(NG)]
    R_tbl = Res("tblD")

    _rs = {}

    def sb(name, shape, dt):
        t = nc.alloc_sbuf_tensor(name, list(shape), dt).ap()
        _rs[name] = Res(name)
        return t

    def R(name):
        return _rs[name]

    kTs = sb("kTs", [128, 8, SK], BF16)
    vst = sb("vst", [128, NKB, 8, 129], BF16)
    kiTs = sb("kiTs", [64, SK], BF16)
    wring = [sb("wr%d" % i, [128, 4, 512], BF16) for i in range(NWH)]
    big = [[sb("big%d_%d" % (ti, k), [128, D], F32) for k in range(2)] for ti in range(NBT)]
    tkb = [[sb("tkb%d_%d" % (ti, k), [128, D], BF16) for k in range(2)] for ti in range(NBT)]
    vaug = [sb("vaug%d" % ti, [128, 4, 257], BF16) for ti in range(NBT)]
    trb = [[sb("trb%d_%d" % (ti, k), [128, 8, 128], BF16) for k in range(4)] for ti in range(NBT)]
    qtmp = [sb("qtmp%d" % i, [128, 512], BF16) for i in range(2)]
    smf = [sb("smf%d" % ti, [128, 80], F32) for ti in range(NBT)]
    kib = [sb("kib%d" % ti, [128, 64], BF16) for ti in range(NBT)]
    sc = sb("sc", [128, SK], F32)
    sel = sb("sel", [128, SK], BF16)
    selTs = [sb("selT%d" % ti, [128, NKB, 128], BF16) for ti in range(NBT)]
    qiTb = [sb("qiTb%d" % ti, [64, 8, 128], BF16) for ti in range(NBT)]
    PT = [sb("PT%d" % i, [128, 4, 128], BF16) for i in range(3)]
    rb = [sb("rb%d" % i, [128, 512], F32) for i in range(3)]
    tmpA = rb
    for i in range(3):
        _rs["tmpA%d" % i] = _rs["rb%d" % i]
    Cst = sb("Cst", [128, 2, 4, 257], F32)
    Csb = sb("Csb", [128, 2, 4, 257], BF16)
    G = sb("G", [128, 8, 2, 128], BF16)
    lngb = sb("lngb", [128, D], F32)
    lnbb = sb("lnbb", [128, D], F32)
    idf = sb("idf", [128, 128], F32)
    idb = sb("idb", [128, 128], BF16)
    tri = sb("tri", [128, 128], F32)
    ones = sb("ones", [128, 128], F32)
    bgb = sb("bgb", [128, 8], F32)
    angT = sb("angT", [128, 8], F32)
    cfar = sb("cfar", [128, 8], F32)
    ncfar = sb("ncfar", [128, 8], F32)
    relb = sb("relb_s", [32, 8], F32)
    ohs = rb[0][0:32, 0:384]
    _rs["ohs"] = _rs["rb0"]
    tb8 = sc[0:8, 0:384]
    _rs["tb8"] = _rs["sc"]
    mrun = sb("mrun", [128, 4], F32)
    gt = sb("gt", [128, 8], F32)
    lfn = sb("lfn", [128, 4], F32)
    dd = sb("dd", [128, 4], F32)
    dmx = sb("dmx", [4, 1], F32)
    D4 = sb("D4", [4, 4], F32)
    Mb = sb("Mb", [128, 4], F32)
    ex12s = [sb("ex12_%d" % ti, [128, 12], F32) for ti in range(NBT)]
    sTm = [sb("sTm%d" % i, [128, 128], BF16) for i in range(2)]
    vu = [sb("vu%d" % i, [128, 257], BF16) for i in range(2)]
    dns = [sb("dn%d" % ti, [128, 4], F32) for ti in range(NBT)]
    for h_ in range(4):
        _rs["Cst%d" % h_] = Res("Cst%d" % h_)
        _rs["Csb%d" % h_] = Res("Csb%d" % h_)
    Rc_all = [_rs["Cst%d" % h_] for h_ in range(4)]
    st4 = sb("st4", [128, 4, 6], F32)
    mv4 = sb("mv4", [128, 4, 2], F32)
    rs4 = sb("rs4", [128, 4], F32)
    aw = sb("aw", [128, 8], F32)
    sg = sb("sg", [128, 8], F32)
    m8 = sb("m8", [128, 8], F32)
    NIT = 9
    bhi = sb("bhi", [128, 1], F32)
    blo = sb("blo", [128, 1], F32)
    bd0 = sb("bd0", [128, 1], F32)
    dks = sb("dks", [128, NIT], F32)
    ndks = sb("ndks", [128, NIT], F32)
    cvec = sb("cvec", [128, NIT], F32)
    ivec3 = sb("ivec3", [128, 3], F32)
    nm3 = sb("nm3", [128, 3], F32)
    ssum = sb("ssum", [128, 3], F32)
    p3 = sb("p3", [128, 3], F32)
    pc = sb("pc", [128, 1], F32)
    thr3 = sb("thr3", [128, 1], F32)
    c3 = sb("c3", [128, 1], F32)
    c23 = sb("c23", [128, 2], F32)
    thr2 = sb("thr2", [128, 1], F32)
    _rs["junkA"] = Res("junkA")
    _rs["junkD"] = Res("junkD")
    xbn = [tkb[ti][0] for ti in range(NBT)]
    for ti in range(NBT):
        _rs["xbn%d" % ti] = _rs["tkb%d_0" % ti]
    sc2 = sb("sc2", [128, T], F32)
    rsum = sb("rsum", [128, 8], F32)
    lst = sb("lst", [128, 2, 6], F32)
    lmv = sb("lmv", [128, 2], F32)
    lrs = sb("lrs", [128, 1], F32)

    banks = []
    for i in range(8):
        t = nc.alloc_psum_tensor("pb%d" % i, [128, 512], F32).ap()
        _rs["pb%d" % i] = Res("pb%d" % i)
        banks.append(t)
    PJ = [(banks[0], R("pb0")), (banks[1], R("pb1"))]
    TPf = [(banks[2], R("pb2")), (banks[3], R("pb3"))]
    TP = [(banks[2].bitcast(BF16).rearrange("p (c t) -> p c t", t=128), R("pb2")),
          (banks[3].bitcast(BF16).rearrange("p (c t) -> p c t", t=128), R("pb3"))]
    LG = [(banks[4], R("pb4")), (banks[5], R("pb5"))]
    OV = [(banks[6], R("pb6")), (banks[7], R("pb7"))]
    cnt = {"pj": 0, "tp": 0, "lg": 0, "qt": 0, "ta": 0, "pt": 0, "rb": 0, "blk": -1, "stm": 0}

    def nxt(k, n):
        v = cnt[k] % n
        cnt[k] += 1
        return v

    def mm(out, lhsT, rhs, start, stop, reads, writes):
        S.op("pe", lambda e: e.matmul(out=out, lhsT=lhsT, rhs=rhs, start=start, stop=stop), reads, writes)

    def tr(out, in_, ident, reads, writes):
        S.op("pe", lambda e: e.transpose(out=out, in_=in_, identity=ident), reads, writes)

    def act(out, in_, func, reads, writes, bias=None, scale=None, accum_out=None):
        kw = {}
        if bias is not None:
            kw["bias"] = bias
        if scale is not None:
            kw["scale"] = scale
        if accum_out is not None:
            kw["accum_out"] = accum_out
        S.op("act", lambda e: e.activation(out=out, in_=in_, func=func, **kw), reads, writes)

    def acopy(out, in_, reads, writes):
        S.op("act", lambda e: e.copy(out=out, in_=in_), reads, writes)

    def vcopy(out, in_, reads, writes, eng="dve"):
        S.op(eng, lambda e: e.tensor_copy(out=out, in_=in_), reads, writes)

    def tt(out, in0, in1, op, reads, writes, eng="dve"):
        S.op(eng, lambda e: e.tensor_tensor(out=out, in0=in0, in1=in1, op=op), reads, writes)

    def ts(out, in0, s1, op0, reads, writes, s2=None, op1=None, eng="dve"):
        if op1 is None:
            S.op(eng, lambda e: e.tensor_scalar(out=out, in0=in0, scalar1=s1, scalar2=None, op0=op0), reads, writes)
        else:
            S.op(eng, lambda e: e.tensor_scalar(out=out, in0=in0, scalar1=s1, scalar2=s2, op0=op0, op1=op1),
                 reads, writes)

    def stt(out, in0, scalar, in1, op0, op1, reads, writes):
        S.op("dve", lambda e: e.scalar_tensor_tensor(out=out, in0=in0, scalar=scalar, in1=in1, op0=op0, op1=op1),
             reads, writes)

    def memset(ap, val, writes, eng="pool"):
        S.op(eng, lambda e: e.memset(ap, val), [], writes)

    def dma(eng, out, in_, reads, writes, nonc=False):
        if nonc:
            S.dma(eng, lambda e: e.dma_start(out=out, in_=in_, allow_slow_non_contiguous=True), reads, writes)
        else:
            S.dma(eng, lambda e: e.dma_start(out=out, in_=in_), reads, writes)

    X = Res("ext_in")

    import os
    SET = os.environ.get("KSET", "abcdef")
    for g in range(NG):
        if "a" in SET:
            dma("pool", wbf[g], wg[g], [X], [R_wbf[g]])
    memset(idf, 0.0, [R("idf")])
    S.op("pool", lambda e: e.affine_select(out=idf, in_=idf, pattern=[[-1, 128]], compare_op=ALU.not_equal,
                                           fill=1.0, base=0, channel_multiplier=1), [R("idf")], [R("idf")])
    vcopy(idb, idf, [R("idf")], [R("idb")])
    memset(tri, 1.0, [R("tri")])
    S.op("pool", lambda e: e.affine_select(out=tri, in_=tri, pattern=[[1, 128]], compare_op=ALU.is_ge,
                                           fill=0.0, base=0, channel_multiplier=-1), [R("tri")], [R("tri")])
    memset(ones, 1.0, [R("ones")])
    for k_ in range(NIT):
        memset(cvec[:, k_:k_ + 1], 0.25 ** (k_ + 1), [R("cvec")])
    for i_ in range(3):
        memset(ivec3[:, i_:i_ + 1], float(i_ + 1), [R("ivec3")])
    memset(vst, 1.0, [R("vst")])
    for ti in range(NBT):
        memset(vaug[ti], 1.0, [R("vaug%d" % ti)])
    if "b" in SET:
        dma("sp", bgb, bass.AP(bg_t, 0, [[0, 128], [1, 8]]), [X], [R("bgb")])
        dma("sp", lngb, bass.AP(lng_t, 0, [[0, 128], [1, D]]), [X], [R("lngb")])
        dma("sp", lnbb, bass.AP(lnb_t, 0, [[0, 128], [1, D]]), [X], [R("lnbb")])
    if "c" in SET:
        dma("sp", angT, bass.AP(ang_t, 0, [[1, 128], [128, 8]]), [X], [R("angT")], nonc=True)
    dma("sp", cfar, bass.AP(relb_t, 15 * 8, [[0, 128], [1, 8]]), [X], [R("cfar")])
    dma("sp", relb, bass.AP(relb_t, 0, [[8, 32], [1, 8]]), [X], [R("relb_s")])
    dma("sp", ohs, oh, [X], [R("ohs")])
    ts(ncfar, cfar, -1.0, ALU.mult, [R("cfar")], [R("ncfar")])
    pjt, pjr = PJ[0]
    mm(pjt[0:8, 0:384], relb, ohs, True, True, [R("relb_s"), R("ohs")], [pjr])
    vcopy(tb8, pjt[0:8, 0:384], [pjr], [R("tb8")])
    dma("sp", bass.AP(tblD_t, 0, [[384, 8], [1, 384]]), tb8, [R("tb8")], [R_tbl])
    antiI = rb[1][:, 0:128]
    _rs["antiI"] = _rs["rb1"]
    memset(antiI, 0.0, [R("antiI")])
    S.op("pool", lambda e: e.affine_select(out=antiI, in_=antiI, pattern=[[1, 128]], compare_op=ALU.not_equal,
                                           fill=1.0, base=-127, channel_multiplier=1), [R("antiI")], [R("antiI")])
    for h in range(8):
        for blk in range(2):
            k = (h * 2 + blk) % 2
            hk, hk_r = rb[2 * k], R("rb%d" % (2 * k))
            src = bass.AP(tblD_t, h * 384 + 129 - 128 * blk, [[1, 128], [1, 128]])
            dma("sp", hk[:, 0:128], src, [R_tbl], [hk_r])
            lg, lgr = LG[k]
            mm(lg[:, 0:128], hk[:, 0:128], antiI, True, True, [hk_r, R("antiI")], [lgr])
            act(G[:, h, blk, :], lg[:, 0:128], AF.Exp, [lgr, R("ncfar")], [R("G")], bias=ncfar[:, h:h + 1], scale=1.0)

    def finish():
        allr = list(_rs.values()) + [OUT, R_tbl] + R_wbf
        for e_ in ("sp",):
            S.wait_all(e_, allr)
        S.emit()
        return nc, S

    if stop == "setup":
        return finish()

    wstate = {"issued": 0, "used": 0}
    nblocks_total = NPS * (T // (128 * NBT)) + NSS
    total_uses = nblocks_total * NG

    def w_issue_upto(u):
        while wstate["issued"] <= u and wstate["issued"] < 2 * total_uses:
            i = wstate["issued"]
            g = (i // 2) % NG
            half = i % 2
            slot = i % NWH
            src = wbf[g].rearrange("(p a) c -> p (a c)", a=8)[:, half * 2048:(half + 1) * 2048]
            dma("sp", wring[slot].rearrange("p a c -> p (a c)"), src, [R_wbf[g]], [R("wr%d" % slot)])
            wstate["issued"] += 1

    def w_get():
        u = wstate["used"]
        w_issue_upto(2 * u + 1 + NWH - 2)
        wstate["used"] += 1
        out = []
        for half in range(2):
            slot = (2 * u + half) % NWH
            out.append((wring[slot], R("wr%d" % slot)))
        return out

    def transposes_to(dst, dst_r, src, src_r, nt, nchunk=8, width=128, evac=None):
        k = nxt("tp", 2)
        tp, tpr = TP[k]
        for c in range(nchunk):
            tr(tp[0:width, c, 0:nt], src[0:nt, c * width:(c + 1) * width], idb[0:nt, 0:nt],
               [src_r, R("idb")], [tpr])
        if evac is None:
            acopy(dst[0:width, 0:nchunk, 0:nt], tp[0:width, 0:nchunk, 0:nt], [tpr], [dst_r])
        else:
            evac(tp, tpr)

    def mlstm_gates(ti, nt):
        hn, hn_r = big[ti][0], R("big%d_0" % ti)
        kab, kab_r = tkb[ti][1], R("tkb%d_1" % ti)
        qaT, qaT_r = trb[ti][1], R("trb%d_1" % ti)
        kaT, kaT_r = trb[ti][2], R("trb%d_2" % ti)
        va, va_r = vaug[ti], R("vaug%d" % ti)
        sm, sm_r = smf[ti], R("smf%d" % ti)
        ex12, ex12_r = ex12s[ti], R("ex12_%d" % ti)
        tpf, tpr = TPf[nxt("tp", 2)]
        tt(gt[0:nt, 0:8], sm[0:nt, 64:72], bgb[0:nt, 0:8], ALU.add, [sm_r, R("bgb")], [R("gt")])
        act(lfn[0:nt, :], gt[0:nt, 4:8], AF.Exp, [R("gt")], [R("lfn")], scale=-1.0)
        act(lfn[0:nt, :], lfn[0:nt, :], AF.Ln, [R("lfn")], [R("lfn")], bias=1.0, scale=1.0)
        mm(tpf[0:nt, 0:4], tri[0:nt, 0:nt], lfn[0:nt, 0:4], True, True, [R("tri"), R("lfn")], [tpr])
        mm(tpf[0:128, 4:8], ones[0:nt, 0:128], lfn[0:nt, 0:4], True, True, [R("ones"), R("lfn")], [tpr])
        tt(dd[0:nt, :], gt[0:nt, 0:4], tpf[0:nt, 0:4], ALU.add, [R("gt"), tpr], [R("dd")])
        mm(tpf[0:4, 16:16 + nt], dd[0:nt, 0:4], idf[0:nt, 0:nt], True, True, [R("dd"), R("idf")], [tpr])
        S.op("dve", lambda e: e.reduce_max(out=dmx[0:4, 0:1], in_=tpf[0:4, 16:16 + nt], axis=AX.X), [tpr], [R("dmx")])
        ts(D4[0:4, 0:4], idf[0:4, 0:4], dmx[0:4, 0:1], ALU.mult, [R("idf"), R("dmx")], [R("D4")])
        mm(tpf[0:128, 160:164], ones[0:4, 0:128], D4[0:4, 0:4], True, True, [R("ones"), R("D4")], [tpr])
        tt(Mb[:, :], mrun[:, :], tpf[0:128, 160:164], ALU.max, [R("mrun"), tpr], [R("Mb")])
        tt(ex12[0:nt, 0:4], dd[0:nt, :], Mb[0:nt, :], ALU.subtract, [R("dd"), R("Mb")], [ex12_r])
        tt(ex12[:, 4:8], mrun[:, :], Mb[:, :], ALU.subtract, [R("mrun"), R("Mb")], [ex12_r])
        tt(ex12[0:nt, 8:12], tpf[0:nt, 0:4], Mb[0:nt, :], ALU.subtract, [tpr, R("Mb")], [ex12_r])
        act(ex12[:, :], ex12[:, :], AF.Exp, [ex12_r], [ex12_r])
        tt(mrun[:, :], Mb[:, :], tpf[0:128, 4:8], ALU.subtract, [R("Mb"), tpr], [R("mrun")])

    def mlstm_head(ti, nt, h, hook=None):
        def hk():
            if hook is not None:
                hook()

        hn, hn_r = big[ti][0], R("big%d_0" % ti)
        kab, kab_r = tkb[ti][1], R("tkb%d_1" % ti)
        qaT, qaT_r = trb[ti][1], R("trb%d_1" % ti)
        kaT, kaT_r = trb[ti][2], R("trb%d_2" % ti)
        va, va_r = vaug[ti], R("vaug%d" % ti)
        ex12, ex12_r = ex12s[ti], R("ex12_%d" % ti)
        dn, dn_r = dns[ti], R("dn%d" % ti)
        cst_r, csb_r = R("Cst%d" % h), R("Csb%d" % h)
        lg, lgr = PJ[nxt("pj", 2)]
        ts(Cst[:, :, h, :], Cst[:, :, h, :], ex12[:, 4 + h:5 + h], ALU.mult, [cst_r, ex12_r], [cst_r])
        hk()
        acopy(Csb[:, :, h, :], Cst[:, :, h, :], [cst_r], [csb_r])
        for c in range(2):
            mm(lg[0:nt, 0:nt], kaT[:, 2 * h + c, 0:nt], qaT[:, 2 * h + c, 0:nt], c == 0, c == 1,
               [kaT_r, qaT_r], [lgr])
        k = nxt("stm", 2)
        hk()
        stt(sTm[k][0:nt, 0:nt], lg[0:nt, 0:nt], ex12[0:nt, h:h + 1], tri[0:nt, 0:nt], ALU.mult, ALU.mult,
            [lgr, ex12_r, R("tri")], [R("sTm%d" % k)])
        nd = lg[0:nt, 128:385]
        mm(nd, sTm[k][0:nt, 0:nt], va[0:nt, h, :], True, False, [R("sTm%d" % k), va_r], [lgr])
        for c in range(2):
            mm(nd, qaT[:, 2 * h + c, 0:nt], Csb[:, c, h, :], False, c == 1, [qaT_r, csb_r], [lgr])
        hk()
        act(dn[0:nt, h:h + 1], lg[0:nt, 384:385], AF.Abs, [lgr], [dn_r])
        tt(dn[0:nt, h:h + 1], dn[0:nt, h:h + 1], ex12[0:nt, 8 + h:9 + h], ALU.max, [dn_r, ex12_r], [dn_r])
        S.op("dve", lambda e: e.reciprocal(out=dn[0:nt, h:h + 1], in_=dn[0:nt, h:h + 1]), [dn_r], [dn_r])
        hk()
        act(hn[0:nt, h * 256:(h + 1) * 256], lg[0:nt, 128:384], AF.Copy, [lgr, dn_r], [hn_r],
            scale=dn[0:nt, h:h + 1])
        act(vu[k][0:nt, :], va[0:nt, h, :], AF.Copy, [va_r, ex12_r], [R("vu%d" % k)], scale=ex12[0:nt, h:h + 1])
        hk()
        for c in range(2):
            ov, ovr = OV[c]
            mm(ov[:, 0:257], kab[0:nt, h * 256 + c * 128:h * 256 + (c + 1) * 128], vu[k][0:nt, :], True, True,
               [kab_r, R("vu%d" % k)], [ovr])
            tt(Cst[:, c, h, :], Cst[:, c, h, :], ov[:, 0:257], ALU.add, [cst_r, ovr], [cst_r])
        hk()

    def mlstm_finish(ti, nt):
        hn, hn_r = big[ti][0], R("big%d_0" % ti)
        for h in range(4):
            S.op("dve", lambda e, h=h: e.bn_stats(out=st4[0:nt, h, :], in_=hn[0:nt, h * 256:(h + 1) * 256]),
                 [hn_r], [R("st4")])
        for h in range(4):
            S.op("dve", lambda e, h=h: e.bn_aggr(out=mv4[0:nt, h, :], in_=st4[0:nt, h, :]), [R("st4")], [R("mv4")])
        act(rs4[0:nt, :], mv4[0:nt, :, 1], AF.Ln, [R("mv4")], [R("rs4")], bias=HN_EPS, scale=1.0)
        act(rs4[0:nt, :], rs4[0:nt, :], AF.Exp, [R("rs4")], [R("rs4")], scale=-0.5)
        for h in range(4):
            ts(hn[0:nt, h * 256:(h + 1) * 256], hn[0:nt, h * 256:(h + 1) * 256], mv4[0:nt, h, 0:1], ALU.subtract,
               [hn_r, R("mv4"), R("rs4")], [hn_r], s2=rs4[0:nt, h:h + 1], op1=ALU.mult)

    def mlstm_all(tiles, hook=None):
        n = len(tiles)
        seq = []
        for step in range(4 + n - 1):
            for ti in range(n):
                h = step - ti
                if 0 <= h < 4:
                    seq.append((ti, h))
        for (ti, h) in seq:
            mlstm_head(ti, tiles[ti][1], h, hook=hook)
            if h == 3:
                mlstm_finish(ti, tiles[ti][1])

    def dsa_pre_thunks(ti, j, nt, ksel):
        qiT, qiT_r = qiTb[ti], R("qiTb%d" % ti)
        sm, sm_r = smf[ti], R("smf%d" % ti)
        selT, selT_r = selTs[ti], R("selT%d" % ti)
        scb, scb_r = (sc, R("sc")) if ti == 0 else (sc2, R("sc2"))
        pos0 = j * 128
        Sk = pos0 + nt
        nkb = j + 1
        th = []

        def t_w():
            act(aw[0:nt, :], sm[0:nt, 72:80], AF.Abs, [sm_r], [R("aw")])
            ts(sg[0:nt, :], sm[0:nt, 72:80], 0.0, ALU.is_gt, [sm_r], [R("sg")], s2=2.0, op1=ALU.mult)
            ts(sg[0:nt, :], sg[0:nt, :], -1.0, ALU.add, [R("sg")], [R("sg")])
        th.append(("pre", t_w))
        for c0 in range(0, Sk, 512):
            w = min(512, Sk - c0)
            for h in range(8):
                def t_s(c0=c0, w=w, h=h):
                    lg, lgr = TPf[nxt("tp", 2)]
                    mm(lg[0:nt, 0:w], qiT[0:64, h, 0:nt], kiTs[0:64, c0:c0 + w], True, True, [qiT_r, R("kiTs")], [lgr])
                    k = nxt("rb", 3)
                    act(rb[k][0:nt, 0:w], lg[0:nt, 0:w], AF.Relu, [lgr, R("aw")], [R("rb%d" % k)],
                        scale=aw[0:nt, h:h + 1])
                    if h == 0:
                        ts(scb[0:nt, c0:c0 + w], rb[k][0:nt, 0:w], sg[0:nt, 0:1], ALU.mult, [R("rb%d" % k), R("sg")],
                           [scb_r])
                    else:
                        stt(scb[0:nt, c0:c0 + w], rb[k][0:nt, 0:w], sg[0:nt, h:h + 1], scb[0:nt, c0:c0 + w], ALU.mult,
                            ALU.add, [R("rb%d" % k), R("sg"), scb_r], [scb_r])
                th.append(("pre", t_s))
        if nt == 128:
            th.append(("pre", lambda: memset(scb[0:64, pos0 + 64:pos0 + 128], NEG, [scb_r], eng="dve")))
        if Sk > ksel:
            Sadm = pos0 + 64 if nt == 128 else Sk
            cthr = 2.0 * ksel - Sk - 0.5

            def t_init():
                S.op("dve", lambda e: e.tensor_reduce(out=bhi[0:nt, 0:1], in_=scb[0:nt, 0:Sk], axis=AX.X, op=ALU.max),
                     [scb_r], [R("bhi")])
                S.op("dve", lambda e: e.tensor_reduce(out=blo[0:nt, 0:1], in_=scb[0:nt, 0:Sadm], axis=AX.X, op=ALU.min),
                     [scb_r], [R("blo")])
                tt(bd0[0:nt, :], bhi[0:nt, :], blo[0:nt, :], ALU.subtract, [R("bhi"), R("blo")], [R("bd0")])
                ts(dks[0:nt, :], cvec[0:nt, :], bd0[0:nt, 0:1], ALU.mult, [R("cvec"), R("bd0")], [R("dks")])
                ts(ndks[0:nt, :], dks[0:nt, :], -1.0, ALU.mult, [R("dks")], [R("ndks")])
                ts(nm3[0:nt, :], ivec3[0:nt, :], ndks[0:nt, 0:1], ALU.mult, [R("ivec3"), R("ndks"), R("blo")], [R("nm3")],
                   s2=blo[0:nt, 0:1], op1=ALU.subtract)
            th.append(("fast", t_init))
            def t_thr3(k_):
                stt(thr3[0:nt, :], dks[0:nt, k_:k_ + 1], 3.0, blo[0:nt, :], ALU.mult, ALU.add,
                    [R("dks"), R("blo")], [R("thr3")])
            th.append(("fast", lambda: t_thr3(0)))
            def dve_count(thr_ap, thr_r, col):
                S.op("dve", lambda e: e.tensor_scalar(out=sel[0:nt, 0:Sk], in0=scb[0:nt, 0:Sk],
                                                      scalar1=thr_ap[0:nt, 0:1], scalar2=0.0, op0=ALU.is_ge, op1=ALU.add,
                                                      accum_out=c23[0:nt, col:col + 1]),
                     [scb_r, thr_r], [R("junkD"), R("c23")])

            def t_thr2(k_):
                stt(thr2[0:nt, :], dks[0:nt, k_:k_ + 1], 2.0, blo[0:nt, :], ALU.mult, ALU.add,
                    [R("dks"), R("blo")], [R("thr2")])

            for k_ in range(NIT):
                odd = (k_ % 2 == 1)

                def t_a():
                    act(sel[0:nt, 0:Sk], scb[0:nt, 0:Sk], AF.Sign, [scb_r, R("nm3")], [R("junkA"), R("ssum")],
                        bias=nm3[0:nt, 0:1], scale=1.0, accum_out=ssum[0:nt, 0:1])

                def t_b(odd=odd):
                    if odd:
                        dve_count(thr2, R("thr2"), 0)
                    else:
                        act(sel[0:nt, 0:Sk], scb[0:nt, 0:Sk], AF.Sign, [scb_r, R("nm3")], [R("junkA"), R("ssum")],
                            bias=nm3[0:nt, 1:2], scale=1.0, accum_out=ssum[0:nt, 1:2])
                    dve_count(thr3, R("thr3"), 1)

                def t_c(k_=k_, odd=odd):
                    if odd:
                        ts(ssum[0:nt, 1:3], c23[0:nt, 0:2], 2.0, ALU.mult, [R("c23")], [R("ssum")], s2=-float(Sk),
                           op1=ALU.add)
                    else:
                        ts(ssum[0:nt, 2:3], c23[0:nt, 1:2], 2.0, ALU.mult, [R("c23")], [R("ssum")], s2=-float(Sk),
                           op1=ALU.add)
                    S.op("dve", lambda e: e.tensor_scalar(out=p3[0:nt, :], in0=ssum[0:nt, 0:3], scalar1=cthr, scalar2=0.0,
                                                          op0=ALU.is_ge, op1=ALU.add, accum_out=pc[0:nt, 0:1]),
                         [R("ssum")], [R("p3"), R("pc")])
                    stt(blo[0:nt, :], pc[0:nt, :], dks[0:nt, k_:k_ + 1], blo[0:nt, :], ALU.mult, ALU.add,
                        [R("pc"), R("dks"), R("blo")], [R("blo")])
                    if k_ + 1 < NIT:
                        ts(nm3[0:nt, :], ivec3[0:nt, :], ndks[0:nt, k_ + 1:k_ + 2], ALU.mult,
                           [R("ivec3"), R("ndks"), R("blo")], [R("nm3")], s2=blo[0:nt, 0:1], op1=ALU.subtract)
                        t_thr3(k_ + 1)
                        if (k_ + 1) % 2 == 1:
                            t_thr2(k_ + 1)
                th.append(("chain", t_a))
                th.append(("chain", t_b))
                th.append(("chainend", t_c))
            th.append(("fast", lambda: ts(sel[0:nt, 0:Sk], scb[0:nt, 0:Sk], blo[0:nt, 0:1], ALU.is_ge,
                                          [scb_r, R("blo")], [R("sel"), R("junkA"), R("junkD")])))
        else:
            th.append(("fast", lambda: ts(sel[0:nt, 0:Sk], scb[0:nt, 0:Sk], -1.0e29, ALU.is_gt, [scb_r],
                                          [R("sel"), R("junkA"), R("junkD")])))
        for k0 in range(0, nkb, 8):
            def t_T(k0=k0):
                kn = min(8, nkb - k0)
                tp, tpr = TP[nxt("tp", 2)]
                full = 0
                for q in range(kn):
                    kb = k0 + q
                    wk = 128 if kb < nkb - 1 else nt
                    tr(tp[0:wk, q, 0:nt], sel[0:nt, kb * 128:kb * 128 + wk], idb[0:nt, 0:nt],
                       [R("sel"), R("junkA"), R("junkD"), R("idb")], [tpr])
                    if wk == 128:
                        full += 1
                if full:
                    acopy(selT[:, k0:k0 + full, 0:nt], tp[:, 0:full, 0:nt], [tpr], [selT_r])
                if full < kn:
                    acopy(selT[0:nt, k0 + full, 0:nt], tp[0:nt, full, 0:nt], [tpr], [selT_r])
            th.append(("fast", t_T))
        return th

    def dsa_attn(ti, j, nt, mask_eng="pool", between=None):
        ob, ob_r = big[ti][0], R("big%d_0" % ti)
        qbT, qbT_r = trb[ti][1], R("trb%d_1" % ti)
        selT, selT_r = selTs[ti], R("selT%d" % ti)
        nkb = j + 1
        groups = []
        nfull = nkb if nt == 128 else nkb - 1
        for k0 in range(0, nfull, 4):
            groups.append((k0, min(4, nfull - k0), 128))
        if nfull < nkb:
            groups.append((nfull, 1, nt))
        scale = 128.0 ** -0.5
        units = [(h, gi) for h in range(8) for gi in range(len(groups))]
        lgs = {}

        def qk(u):
            h, gi = units[u]
            k0, nq, wk = groups[gi]
            lg, lgr = LG[nxt("lg", 2)]
            lg3 = lg.rearrange("p (q t) -> p q t", t=128)
            for q in range(nq):
                kb = k0 + q
                mm(lg3[0:wk, q, 0:nt], kTs[:, h, kb * 128:kb * 128 + wk], qbT[:, h, 0:nt], True, True,
                   [R("kTs"), qbT_r], [lgr])
            lgs[u] = (lg3, lgr)

        qk(0)
        for u, (h, gi) in enumerate(units):
            k0, nq, wk = groups[gi]
            ov, ovr = OV[h % 2]
            if u + 1 < len(units):
                qk(u + 1)
            lg3, lgr = lgs.pop(u)
            pi = nxt("pt", 3)
            pt, ptr = PT[pi], R("PT%d" % pi)
            act(pt[0:wk, 0:nq, 0:nt], lg3[0:wk, 0:nq, 0:nt], AF.Exp, [lgr, R("cfar")], [ptr],
                bias=cfar[0:wk, h:h + 1], scale=scale)
            meng = mask_eng if mask_eng != "mix" else ("pool" if u % 2 == 0 else "dve")
            tt(pt[0:wk, 0:nq, 0:nt], pt[0:wk, 0:nq, 0:nt], selT[0:wk, k0:k0 + nq, 0:nt], ALU.mult,
               [ptr, selT_r], [ptr], eng=meng)
            for q in range(nq):
                kb = k0 + q
                blk = j - kb
                if blk in (0, 1):
                    tt(pt[0:wk, q, 0:nt], pt[0:wk, q, 0:nt], G[0:wk, h, blk, 0:nt], ALU.mult, [ptr, R("G")], [ptr],
                       eng=meng)
            for q in range(nq):
                kb = k0 + q
                mm(ov[0:nt, 0:129], pt[0:wk, q, 0:nt], vst[0:wk, kb, h, :], kb == 0, kb == nkb - 1,
                   [ptr, R("vst")], [ovr])
            if gi == len(groups) - 1:
                S.op("dve", lambda e, h=h, ov=ov: e.reciprocal(out=rsum[0:nt, h:h + 1], in_=ov[0:nt, 128:129]),
                     [ovr], [R("rsum")])
                act(ob[0:nt, h * 128:(h + 1) * 128], ov[0:nt, 0:128], AF.Copy, [ovr, R("rsum")], [ob_r],
                    scale=rsum[0:nt, h:h + 1])
            if between is not None:
                between()

    def store_kv_tile(kind, b, ti, j, nt):
        pass

    def xrows(kind, b, j, nt):
        if kind == "p":
            return xp[b, j * 128:j * 128 + nt, :]
        return xs[b, 0:nt, :]

    def x_load(kind, b, tiles):
        for ti, (j, nt) in enumerate(tiles):
            dma("pool", xbn[ti][0:nt, :], xrows(kind, b, j, nt), [X], [R("xbn%d" % ti)])

    def x_transpose(ti, nt):
        transposes_to(trb[ti][0], R("trb%d_0" % ti), xbn[ti], R("xbn%d" % ti), nt)

    def block(kind, b, tiles, have_x=False, nxt_blk=None):
        cnt["blk"] += 1
        if kind == "p":
            xsrc, ydst, kdst, vdst, ikdst, ksel = xp, yp, kp, vp, ikp, KSEL_P
        else:
            xsrc, ydst, kdst, vdst, ikdst, ksel = xs, ysd, ksd, vsd, iksd, KSEL_S

        def rows(ap, j, nt):
            if kind == "p":
                return ap[b, j * 128:j * 128 + nt, :]
            return ap[b, 0:nt, :]

        if not have_x:
            x_load(kind, b, tiles)
            for ti, (j, nt) in enumerate(tiles):
                x_transpose(ti, nt)
        bgs = [[] for _ in tiles]
        g_attn = GROUPS.index(("qb", 1))

        def advance(g):
            lists = [L for L in bgs if L]
            if not lists:
                return
            P = lists[0]
            n = 0
            chain = False
            while P and n < 9:
                kind_, f_ = P.pop(0)
                f_()
                n += 1
                if kind_ == "chainend":
                    chain = True
                    break
            if chain and len(lists) > 1:
                Q = lists[1]
                m = 0
                while Q and m < 4 and Q[0][0] == "pre":
                    Q.pop(0)[1]()
                    m += 1

        def drain_tile(ti):
            for t_ in range(ti + 1):
                while bgs[t_]:
                    bgs[t_].pop(0)[1]()

        fine = {"t": 0}

        def pop_fine():
            lists = [L for L in bgs if L]
            if not lists:
                return
            fine["t"] ^= 1
            if fine["t"] and len(lists) > 1 and lists[1][0][0] == "pre" and lists[0][0][0] in ("chain", "chainend"):
                lists[1].pop(0)[1]()
            else:
                lists[0].pop(0)[1]()

        def pop_one():
            for L in bgs:
                if L:
                    L.pop(0)[1]()
                    return

        for g, (gk, hf) in enumerate(GROUPS):
            if stop is not None and stop.startswith("g") and int(stop[1:]) == g:
                raise _Stop()
            if stop is not None and stop.startswith("b"):
                bi_, gi_ = stop[1:].split("g")
                if int(bi_) == cnt["blk"] and int(gi_) == g:
                    raise _Stop()
            wts = w_get()
            ncol = 80 if gk == "small" else 512
            pss = []
            for ti, (j, nt) in enumerate(tiles):
                pss.append(PJ[nxt("pj", 2)])
            for half in range(2):
                wt, wt_r = wts[half]
                for ti, (j, nt) in enumerate(tiles):
                    if gk in ("woa", "wob", "wo"):
                        lt, lt_r = trb[ti][3], R("trb%d_3" % ti)
                    else:
                        lt, lt_r = trb[ti][0], R("trb%d_0" % ti)
                    ps, psr = pss[ti]
                    for d4 in range(4):
                        dc = half * 4 + d4
                        mm(ps[0:nt, 0:ncol], lt[:, dc, 0:nt], wt[:, d4, 0:ncol], dc == 0, dc == 7, [lt_r, wt_r], [psr])
            for ti, (j, nt) in enumerate(tiles):
                hn, hn_r = big[ti][0], R("big%d_0" % ti)
                mx, mx_r = big[ti][1], R("big%d_1" % ti)
                ba, ba_r = tkb[ti][0], R("tkb%d_0" % ti)
                kab, kab_r = tkb[ti][1], R("tkb%d_1" % ti)
                xT, xT_r = trb[ti][0], R("trb%d_0" % ti)
                aT, aT_r = trb[ti][3], R("trb%d_3" % ti)
                pos0 = j * 128
                ps, psr = pss[ti]
                cs = slice(hf * 512, (hf + 1) * 512)
                p_ = ps[0:nt, 0:512]
                if gk == "kb":
                    acopy(hn[0:nt, cs], p_, [psr], [hn_r])
                    vcopy(ba[0:nt, cs], p_, [psr], [ba_r])
                    if hf == 1:
                        dma("sp", rows(kdst, j, nt), hn[0:nt, :], [hn_r], [OUT])
                        k = nxt("tp", 2)
                        tp, tpr = TP[k]
                        for c in range(8):
                            tr(tp[:, c, 0:nt], ba[0:nt, c * 128:(c + 1) * 128], idb[0:nt, 0:nt], [ba_r, R("idb")], [tpr])
                        acopy(kTs[:, :, pos0:pos0 + nt], tp[:, :, 0:nt], [tpr], [R("kTs")])
                elif gk == "vb":
                    acopy(mx[0:nt, cs], p_, [psr], [mx_r])
                    vcopy(vst[0:nt, j, hf * 4:(hf + 1) * 4, 0:128], p_.rearrange("p (h d) -> p h d", d=128),
                          [psr], [R("vst")])
                    if hf == 1:
                        dma("sp", rows(vdst, j, nt), mx[0:nt, :], [mx_r], [OUT])
                elif gk == "small":
                    acopy(smf[ti][0:nt, :], ps[0:nt, 0:80], [psr], [R("smf%d" % ti)])
                    dma("sp", rows(ikdst, j, nt), smf[ti][0:nt, 0:64], [R("smf%d" % ti)], [OUT])
                    acopy(kib[ti][0:nt, :], smf[ti][0:nt, 0:64], [R("smf%d" % ti)], [R("kib%d" % ti)])
                    tp, tpr = TP[nxt("tp", 2)]
                    tr(tp[0:64, 0, 0:nt], kib[ti][0:nt, 0:64], idb[0:nt, 0:nt], [R("kib%d" % ti), R("idb")], [tpr])
                    acopy(kiTs[0:64, pos0:pos0 + nt], tp[0:64, 0, 0:nt], [tpr], [R("kiTs")])
                    mlstm_gates(ti, nt)
                elif gk in ("qa", "qb"):
                    qi_ = nxt("qt", 2)
                    vcopy(qtmp[qi_][0:nt, :], p_, [psr], [R("qtmp%d" % qi_)])
                    dstT, dstT_r = trb[ti][1], R("trb%d_1" % ti)
                    tp, tpr = TP[nxt("tp", 2)]
                    for c in range(4):
                        tr(tp[:, c, 0:nt], qtmp[qi_][0:nt, c * 128:(c + 1) * 128], idb[0:nt, 0:nt],
                           [R("qtmp%d" % qi_), R("idb")], [tpr])
                    acopy(dstT[:, hf * 4:(hf + 1) * 4, 0:nt], tp[:, 0:4, 0:nt], [tpr], [dstT_r])
                    if gk == "qb" and hf == 1:
                        drain_tile(ti)
                        last = ti == len(tiles) - 1

                        def between():
                            pop_one()
                        dsa_attn(ti, j, nt, mask_eng="dve", between=None if last else between)
                elif gk == "ka":
                    act(kab[0:nt, cs], p_, AF.Copy, [psr], [kab_r], scale=1.0 / 16.0)
                    if hf == 1:
                        transposes_to(trb[ti][2], R("trb%d_2" % ti), kab, kab_r, nt)
                elif gk == "va":
                    acopy(vaug[ti][0:nt, hf * 2:(hf + 1) * 2, 0:256], p_.rearrange("p (h d) -> p h d", d=256),
                          [psr], [R("vaug%d" % ti)])
                    if hf == 1 and ti == len(tiles) - 1:
                        mlstm_all(tiles, hook=pop_fine)
                elif gk == "oa":
                    k = nxt("rb", 3)
                    act(tmpA[k][0:nt, :], p_, AF.Sigmoid, [psr], [R("tmpA%d" % k)])
                    tt(hn[0:nt, cs], hn[0:nt, cs], tmpA[k][0:nt, :], ALU.mult, [hn_r, R("tmpA%d" % k)], [hn_r], eng="pool")
                elif gk == "za":
                    k = nxt("rb", 3)
                    act(tmpA[k][0:nt, :], p_, AF.Silu, [psr], [R("tmpA%d" % k)])
                    tt(ba[0:nt, cs], hn[0:nt, cs], tmpA[k][0:nt, :], ALU.mult, [hn_r, R("tmpA%d" % k)], [ba_r],
                       eng="pool" if gk == "za" else "dve")
                    if hf == 1:
                        def ev(tp, tpr, nt=nt, aT=aT, aT_r=aT_r):
                            for c in range(8):
                                act(aT[:, c, 0:nt], tp[:, c, 0:nt], AF.Copy, [tpr, R("angT")], [aT_r],
                                    scale=angT[:, c:c + 1])
                        transposes_to(aT, aT_r, ba, ba_r, nt, evac=ev)
                elif gk == "woa":
                    vcopy(mx[0:nt, cs], p_, [psr], [mx_r])
                elif gk == "ga":
                    k = nxt("rb", 3)
                    act(tmpA[k][0:nt, :], p_, AF.Sigmoid, [psr], [R("tmpA%d" % k)])
                    tt(mx[0:nt, cs], mx[0:nt, cs], tmpA[k][0:nt, :], ALU.mult, [mx_r, R("tmpA%d" % k)], [mx_r], eng="pool")
                elif gk == "qi":
                    qi_ = nxt("qt", 2)
                    acopy(qtmp[qi_][0:nt, :], p_, [psr], [R("qtmp%d" % qi_)])
                    dstT, dstT_r = qiTb[ti], R("qiTb%d" % ti)
                    tp, tpr = TP[nxt("tp", 2)]
                    for c in range(8):
                        tr(tp[0:64, c, 0:nt], qtmp[qi_][0:nt, c * 64:(c + 1) * 64], idb[0:nt, 0:nt],
                           [R("qtmp%d" % qi_), R("idb")], [tpr])
                    acopy(dstT[0:64, :, 0:nt], tp[0:64, :, 0:nt], [tpr], [dstT_r])
                    bgs[ti].extend(dsa_pre_thunks(ti, j, nt, ksel))
                elif gk == "zb":
                    k = nxt("rb", 3)
                    act(tmpA[k][0:nt, :], p_, AF.Silu, [psr], [R("tmpA%d" % k)])
                    tt(ba[0:nt, cs], hn[0:nt, cs], tmpA[k][0:nt, :], ALU.mult, [hn_r, R("tmpA%d" % k)], [ba_r],
                       eng="pool" if gk == "za" else "dve")
                    if hf == 1:
                        transposes_to(aT, aT_r, ba, ba_r, nt)
                elif gk == "wob":
                    vcopy(hn[0:nt, cs], p_, [psr], [hn_r])
                elif gk == "gb":
                    k = nxt("rb", 3)
                    act(tmpA[k][0:nt, :], p_, AF.Sigmoid, [psr], [R("tmpA%d" % k)])
                    tt(tmpA[k][0:nt, :], tmpA[k][0:nt, :], hn[0:nt, cs], ALU.mult, [hn_r, R("tmpA%d" % k)],
                       [R("tmpA%d" % k)])
                    tt(mx[0:nt, cs], mx[0:nt, cs], tmpA[k][0:nt, :], ALU.add, [mx_r, R("tmpA%d" % k)], [mx_r])
                    if hf == 1:
                        acopy(ba[0:nt, :], mx[0:nt, :], [mx_r], [ba_r])
                        transposes_to(aT, aT_r, ba, ba_r, nt)
                        dma("sp", hn[0:nt, :], rows(xsrc, j, nt), [X], [hn_r])
                        if nxt_blk is not None and ti < len(nxt_blk[2]):
                            jn_, ntn_ = nxt_blk[2][ti]
                            dma("pool", xbn[ti][0:ntn_, :], xrows(nxt_blk[0], nxt_blk[1], jn_, ntn_), [X],
                                [R("xbn%d" % ti)])
                elif gk == "wo":
                    stt(mx[0:nt, cs], hn[0:nt, cs], ALPHA, p_, ALU.mult, ALU.add, [hn_r, psr], [mx_r])
                    if hf == 1:
                        for c in range(2):
                            S.op("dve", lambda e, c=c, mx=mx, nt=nt: e.bn_stats(out=lst[0:nt, c, :],
                                                                                in_=mx[0:nt, c * 512:(c + 1) * 512]),
                                 [mx_r], [R("lst")])
                        S.op("dve", lambda e, nt=nt: e.bn_aggr(out=lmv[0:nt, :], in_=lst[0:nt, :, :]), [R("lst")], [R("lmv")])
                        act(lrs[0:nt, :], lmv[0:nt, 1:2], AF.Ln, [R("lmv")], [R("lrs")], bias=LN_EPS, scale=1.0)
                        act(lrs[0:nt, :], lrs[0:nt, :], AF.Exp, [R("lrs")], [R("lrs")], scale=-0.5)
                        ts(mx[0:nt, :], mx[0:nt, :], lmv[0:nt, 0:1], ALU.subtract, [mx_r, R("lmv"), R("lrs")], [mx_r],
                           s2=lrs[0:nt, 0:1], op1=ALU.mult)
                        tt(mx[0:nt, :], mx[0:nt, :], lngb[0:nt, :], ALU.mult, [mx_r, R("lngb")], [mx_r])
                        tt(mx[0:nt, :], mx[0:nt, :], lnbb[0:nt, :], ALU.add, [mx_r, R("lnbb")], [mx_r])
                        dma("pool", rows(ydst, j, nt), mx[0:nt, :], [mx_r], [OUT])
                        if nxt_blk is not None and ti < len(nxt_blk[2]):
                            x_transpose(ti, nxt_blk[2][ti][1])
                if g < g_attn:
                    advance(g)

    def state_out(C_t, n_t, m_t, b):
        for h in range(4):
            for c in range(2):
                dst = bass.AP(C_t, ((b * 4 + h) * 256 + c * 128) * 256, [[256, 128], [1, 256]])
                dma("sp", dst, Cst[:, c, h, 0:256], Rc_all, [OUT])
        for c in range(2):
            dstn = bass.AP(n_t, b * 1024 + c * 128, [[1, 128], [256, 4]])
            dma("sp", dstn, Cst[:, c, :, 256], Rc_all, [OUT], nonc=True)
        dstm = bass.AP(m_t, b * 4, [[4, 1], [1, 4]])
        dma("sp", dstm, mrun[0:1, 0:4], [R("mrun")], [OUT])

    def main_body():
        plan = []
        nbp = T // (128 * NBT)
        for b in range(NPS):
            for bi in range(nbp):
                plan.append(("p", b, [(bi * NBT + i, 128) for i in range(NBT)], bi == 0, bi == nbp - 1))
        nb = PAST // 128
        for b in range(NSS):
            plan.append(("s", b, [(nb, DS)], True, True))
        for i, (kind, b, tiles, first, last) in enumerate(plan):
            if first and kind == "p":
                memset(Cst, 0.0, Rc_all)
                memset(mrun, 0.0, [R("mrun")])
            if first and kind == "s":
                for h in range(4):
                    for c in range(2):
                        src = bass.AP(sC_t, ((b * 4 + h) * 256 + c * 128) * 256, [[256, 128], [1, 256]])
                        dma("sp", Cst[:, c, h, 0:256], src, [X], Rc_all)
                for c in range(2):
                    srcn = bass.AP(sn_t, b * 1024 + c * 128, [[1, 128], [256, 4]])
                    dma("sp", Cst[:, c, :, 256], srcn, [X], Rc_all, nonc=True)
                dma("sp", mrun, bass.AP(sm_t, b * 4, [[0, 128], [1, 4]]), [X], [R("mrun")])
                kt, kt_r = tkb[0][0], R("tkb0_0")
                for blk in range(nb):
                    dma("pool", kt, ck[b, blk * 128:(blk + 1) * 128, :], [X], [kt_r])
                    tp, tpr = TP[nxt("tp", 2)]
                    for c in range(8):
                        tr(tp[:, c, :], kt[:, c * 128:(c + 1) * 128], idb, [kt_r, R("idb")], [tpr])
                    acopy(kTs[:, :, blk * 128:(blk + 1) * 128], tp[:, :, :], [tpr], [R("kTs")])
                    dma("pool", vst[:, blk, :, 0:128],
                        cv[b, blk * 128:(blk + 1) * 128, :].rearrange("p (h d) -> p h d", d=128), [X], [R("vst")])
                kit, kit_r = tkb[0][1], R("tkb0_1")
                kit3 = kit[:, 0:nb * 64].rearrange("p (n c) -> p n c", c=64)
                dma("pool", kit3, cik[b].rearrange("(n p) c -> p n c", p=128), [X], [kit_r])
                for k0 in range(0, nb, 8):
                    kn = min(8, nb - k0)
                    tp, tpr = TP[nxt("tp", 2)]
                    for q in range(kn):
                        tr(tp[0:64, q, :], kit3[:, k0 + q, :], idb, [kit_r, R("idb")], [tpr])
                    acopy(kiTs[0:64, k0 * 128:(k0 + kn) * 128].rearrange("p (n t) -> p n t", t=128), tp[0:64, 0:kn, :],
                          [tpr], [R("kiTs")])
            nb_ = plan[i + 1] if i + 1 < len(plan) else None
            block(kind, b, tiles, have_x=(i > 0), nxt_blk=nb_)
            if last:
                if kind == "p":
                    state_out(Cp_t, np_t, mp_t, b)
                else:
                    state_out(Cs_t, ns_t, ms_t, b)

    try:
        main_body()
    except _Stop:
        return finish()
    S.wait_all("sp", [OUT])
    S.emit()
    return nc, S


def _run(inputs, NPS, T, NSS, PAST, ncores, stop=None):
    nc, S = build(NPS, T, NSS, PAST, stop=stop)
    wgh = _host_weight_groups(np.asarray(inputs["w_in"], np.float32), np.asarray(inputs["w_out_a"], np.float32),
                              np.asarray(inputs["w_out_b"], np.float32), np.asarray(inputs["w_o"], np.float32))
    wgh = np.ascontiguousarray(wgh.reshape(NG, 1024, 512))
    oh = _bucket_onehot()
    in_maps = []
    f = lambda a: np.ascontiguousarray(np.asarray(a, np.float32))
    for c in range(ncores):
        ps = slice(c * NPS, (c + 1) * NPS)
        ss = slice(c * NSS, (c + 1) * NSS)
        in_maps.append({
            "xp": f(inputs["x_prompt"][ps]),
            "xs": f(inputs["x_sample"][ss]),
            "ck": f(inputs["cache_k"][ss]).reshape(NSS, PAST, D),
            "cv": f(inputs["cache_v"][ss]).reshape(NSS, PAST, D),
            "cik": f(inputs["cache_idx_k"][ss]),
            "sC": f(inputs["state_C"][ss]),
            "sn": f(inputs["state_n"][ss]),
            "sm": f(inputs["state_m"][ss]),
            "wg": wgh,
            "bg": f(inputs["b_gates"]),
            "ang": f(inputs["a_norm_g"]),
            "relb": f(inputs["rel_bias"]),
            "lng": f(inputs["ln_g"]),
            "lnb": f(inputs["ln_b"]),
            "oh": oh,
        })
    res = run_bass_kernel_spmd(nc, in_maps, core_ids=list(range(ncores)))
    rr = res.results
    cat = lambda k: np.concatenate([np.asarray(r[k], np.float32) for r in rr], axis=0)
    y_p = cat("yp")
    y_s = cat("ys")
    k_p = cat("kp").reshape(-1, T, 8, 128)
    v_p = cat("vp").reshape(-1, T, 8, 128)
    ik_p = cat("ikp")
    C_p, n_p, m_p = cat("Cp"), cat("np"), cat("mp")
    k_s = cat("ks").reshape(-1, DS, 8, 128)
    v_s = cat("vs").reshape(-1, DS, 8, 128)
    ik_s = cat("iks")
    C_s, n_s, m_s = cat("Cs"), cat("ns"), cat("ms")
    return (y_p, y_s, k_p, v_p, ik_p, C_p, n_p, m_p, k_s, v_s, ik_s, C_s, n_s, m_s)


def kernel(x_prompt, x_sample, cache_k, cache_v, cache_idx_k, state_C, state_n, state_m,
           w_in, b_gates, a_norm_g, w_out_a, w_out_b, w_o, rel_bias, ln_g, ln_b):
    inputs = dict(x_prompt=x_prompt, x_sample=x_sample, cache_k=cache_k, cache_v=cache_v, cache_idx_k=cache_idx_k,
                  state_C=state_C, state_n=state_n, state_m=state_m, w_in=w_in, b_gates=b_gates, a_norm_g=a_norm_g,
                  w_out_a=w_out_a, w_out_b=w_out_b, w_o=w_o, rel_bias=rel_bias, ln_g=ln_g, ln_b=ln_b)
    B, T = x_prompt.shape[0], x_prompt.shape[1]
    BS, PAST = x_sample.shape[0], cache_k.shape[1]
    return _run(inputs, B // NCORES, T, BS // NCORES, PAST, NCORES)
```
